# Optimizing a Trainium2 kernel written in Bass

```python
import jax, jax.numpy as jnp
from jax import lax
import numpy as np

D_MODEL = 1024
BATCH = 2
SEQ = 16384
DEPTH = 1
DEC_BATCH = 8
DEC_SEQ = 2048
PAST_LEN = 128

MIX_WIDTH = D_MODEL
FNET_WIDTH = D_MODEL // 2
FNET_GROUPS = 4
FNET_GROUP_DIM = FNET_WIDTH // FNET_GROUPS
RET_WIDTH = MIX_WIDTH - FNET_WIDTH
RET_HEADS = 4
RET_HEAD_DIM = RET_WIDTH // RET_HEADS
D_FF = 4 * D_MODEL
CHUNK = 128
ROPE_BASE = 10000.0
EPS = 1e-6
N_MOD = 6
IN_WIDTH = FNET_WIDTH + 4 * RET_WIDTH
SPLITS = (FNET_WIDTH, FNET_WIDTH + RET_WIDTH, FNET_WIDTH + 2 * RET_WIDTH, FNET_WIDTH + 3 * RET_WIDTH)
DECAY_OFFSET_FWD = 0.0
DECAY_OFFSET_BWD = 0.5

kernel_name = "hymba_fnet_retnet_adaln_encoder"


def rmsnorm(x, g):
    xf = x.astype(jnp.float32)
    y = xf * lax.rsqrt(jnp.mean(xf * xf, axis=-1, keepdims=True) + EPS)
    return (y * g.astype(jnp.float32)).astype(x.dtype)


def modulate(h, shift, scale):
    return h * (1.0 + scale[:, None, :]) + shift[:, None, :]


def rotary(x):
    S = x.shape[1]
    half = x.shape[-1] // 2
    inv = ROPE_BASE ** (-jnp.arange(half, dtype=jnp.float32) / half)
    ang = jnp.arange(S, dtype=jnp.float32)[:, None] * inv[None, :]
    cos = jnp.cos(ang)[None, :, None, :]
    sin = jnp.sin(ang)[None, :, None, :]
    xf = x.astype(jnp.float32)
    x1, x2 = xf[..., :half], xf[..., half:]
    return jnp.concatenate([x1 * cos - x2 * sin, x1 * sin + x2 * cos], axis=-1)


def log_gammas(offset):
    return jnp.log1p(-jnp.exp2(-5.0 - offset - jnp.arange(RET_HEADS, dtype=jnp.float32)))


def retention_chunkwise(q, k, v, log_g, strict):
    B, S, H, Dh = q.shape
    N = S // CHUNK
    qc = q.reshape(B, N, CHUNK, H, Dh)
    kc = k.reshape(B, N, CHUNK, H, Dh)
    vc = v.reshape(B, N, CHUNK, H, Dh)
    pos = jnp.arange(CHUNK, dtype=jnp.float32)
    diff = pos[:, None] - pos[None, :]
    mask = (diff > 0) if strict else (diff >= 0)
    decay = jnp.where(mask[None], jnp.exp(log_g[:, None, None] * jnp.maximum(diff, 0.0)[None]), 0.0)
    scores = jnp.einsum('bnihd,bnjhd->bnhij', qc, kc) * decay[None, None]
    inner = jnp.einsum('bnhij,bnjhe->bnihe', scores, vc)
    zeta = jnp.exp(log_g[:, None] * (CHUNK - 1.0 - pos)[None])
    kv = jnp.einsum('bnjhd,hj,bnjhe->bnhde', kc, zeta, vc)
    chunk_decay = jnp.exp(log_g * CHUNK)[None, :, None, None]

    def step(state, kv_n):
        return state * chunk_decay + kv_n, state

    _, prev = lax.scan(step, jnp.zeros((B, H, Dh, Dh), q.dtype), jnp.moveaxis(kv, 1, 0))
    prev = jnp.moveaxis(prev, 0, 1)
    xi = jnp.exp(log_g[:, None] * (pos + 1.0)[None])
    cross = jnp.einsum('bnihd,hi,bnhde->bnihe', qc, xi, prev)
    return (inner + cross).reshape(B, S, H, Dh)


def hybrid_mixer(h, w_in, w_fnet, w_out):
    B, S, _ = h.shape
    proj = h @ w_in
    u, q, k, v, g = jnp.split(proj, SPLITS, axis=-1)
    ug = u.reshape(B, S, FNET_GROUPS, FNET_GROUP_DIM).astype(jnp.float32)
    f = jnp.fft.fft2(ug, axes=(1, 3), norm='ortho').real
    f = jnp.einsum('bsgc,gcd->bsgd', f.astype(h.dtype), w_fnet).reshape(B, S, FNET_WIDTH)
    q = rotary(q.reshape(B, S, RET_HEADS, RET_HEAD_DIM))
    k = rotary(k.reshape(B, S, RET_HEADS, RET_HEAD_DIM)) * (RET_HEAD_DIM ** -0.5)
    v = v.reshape(B, S, RET_HEADS, RET_HEAD_DIM).astype(jnp.float32)
    fwd = retention_chunkwise(q, k, v, log_gammas(DECAY_OFFSET_FWD), False)
    bwd = jnp.flip(retention_chunkwise(jnp.flip(q, 1), jnp.flip(k, 1), jnp.flip(v, 1),
                                       log_gammas(DECAY_OFFSET_BWD), True), axis=1)
    r = fwd + bwd
    r = r * lax.rsqrt(jnp.mean(r * r, axis=-1, keepdims=True) + EPS)
    r = r.reshape(B, S, RET_WIDTH).astype(h.dtype) * jax.nn.silu(g)
    return jnp.concatenate([f, r], axis=-1) @ w_out


def encoder_layer(x, c, ada_w, ada_b, norm_mix, w_in, w_fnet, w_out, norm_mlp, w_mlp_in, w_mlp_out):
    mod = jax.nn.silu(c) @ ada_w + ada_b
    sh1, sc1, gt1, sh2, sc2, gt2 = jnp.split(mod, N_MOD, axis=-1)
    h = modulate(rmsnorm(x, norm_mix), sh1, sc1)
    x = x + gt1[:, None, :] * hybrid_mixer(h, w_in, w_fnet, w_out)
    h = modulate(rmsnorm(x, norm_mlp), sh2, sc2)
    x = x + gt2[:, None, :] * (jnp.square(jax.nn.relu(h @ w_mlp_in)) @ w_mlp_out)
    return x


def run_trunk(x, c, ada_w, ada_b, norm_mix, w_in, w_fnet, w_out, norm_mlp, w_mlp_in, w_mlp_out, norm_final):
    for l in range(DEPTH):
        x = encoder_layer(x, c, ada_w[l], ada_b[l], norm_mix[l], w_in[l], w_fnet[l], w_out[l],
                          norm_mlp[l], w_mlp_in[l], w_mlp_out[l])
    return rmsnorm(x, norm_final)


def setup_inputs(seed: int = 0) -> dict:
    key = jax.random.key(seed)
    ks = jax.random.split(key, 16)
    f32 = jnp.float32
    nrm = lambda k, shape, s: jax.random.normal(k, shape, f32) * s
    return {
        "x_prompt": nrm(ks[0], (BATCH, SEQ, D_MODEL), 1.0),
        "x_sample": nrm(ks[1], (DEC_BATCH, DEC_SEQ, D_MODEL), 1.0),
        "c_prompt": nrm(ks[2], (BATCH, D_MODEL), 1.0),
        "c_sample": nrm(ks[3], (DEC_BATCH, D_MODEL), 1.0),
        "ada_w": nrm(ks[4], (DEPTH, D_MODEL, N_MOD * D_MODEL), 0.5 * D_MODEL ** -0.5),
        "ada_b": nrm(ks[5], (DEPTH, N_MOD * D_MODEL), 0.02),
        "norm_mix": 1.0 + nrm(ks[6], (DEPTH, D_MODEL), 0.02),
        "w_in": nrm(ks[7], (DEPTH, D_MODEL, IN_WIDTH), D_MODEL ** -0.5),
        "w_fnet": nrm(ks[8], (DEPTH, FNET_GROUPS, FNET_GROUP_DIM, FNET_GROUP_DIM), FNET_GROUP_DIM ** -0.5),
        "w_out": nrm(ks[9], (DEPTH, MIX_WIDTH, D_MODEL), MIX_WIDTH ** -0.5),
        "norm_mlp": 1.0 + nrm(ks[10], (DEPTH, D_MODEL), 0.02),
        "w_mlp_in": nrm(ks[11], (DEPTH, D_MODEL, D_FF), D_MODEL ** -0.5),
        "w_mlp_out": nrm(ks[12], (DEPTH, D_FF, D_MODEL), D_FF ** -0.5),
        "norm_final": 1.0 + nrm(ks[13], (D_MODEL,), 0.02),
    }


def reference(x_prompt, x_sample, c_prompt, c_sample, ada_w, ada_b, norm_mix, w_in, w_fnet, w_out,
              norm_mlp, w_mlp_in, w_mlp_out, norm_final):
    y_prompt = run_trunk(x_prompt, c_prompt, ada_w, ada_b, norm_mix, w_in, w_fnet, w_out,
                         norm_mlp, w_mlp_in, w_mlp_out, norm_final)
    y_sample = run_trunk(x_sample, c_sample, ada_w, ada_b, norm_mix, w_in, w_fnet, w_out,
                         norm_mlp, w_mlp_in, w_mlp_out, norm_final)
    return (y_prompt, y_sample)
```

```python
import contextlib
import os as _os
import re as _re
import numpy as np
import concourse.bass as bass
import concourse.mybir as mybir
from concourse.bass_utils import run_bass_kernel_spmd

F32 = mybir.dt.float32
BF16 = mybir.dt.bfloat16
AF = mybir.ActivationFunctionType
ALU = mybir.AluOpType

D = 1024
NCH = 8
DFF = 4096
EPS = 1e-6
_COARSE = bool(_os.environ.get("K_COARSE"))
PE, ACT, DVE, POOL, SP = "pe", "act", "dve", "pool", "sp"
COMPUTE = (PE, ACT, DVE, POOL)


class Op:
    __slots__ = ("eng", "fn", "deps", "is_dma", "slot", "needs_inc", "tok")


class Ctx:
    def __init__(self, nc, stack, n_dma_sems=80):
        self.nc = nc
        self.eng_sem = {e: stack.enter_context(nc.semaphore("sem_" + e)) for e in COMPUTE}
        self.eng_cnt = {e: 0 for e in COMPUTE}
        self.dma_sems = [stack.enter_context(nc.semaphore("semd%d" % i)) for i in range(n_dma_sems)]
        self.dma_cnt = [0] * n_dma_sems
        self.slot_map = {}
        self.last_writer = {}
        self.readers = {}
        self.waited = {e: {} for e in (PE, ACT, DVE, POOL, SP)}

    def slot_id(self, key):
        if key not in self.slot_map:
            assert len(self.slot_map) < len(self.dma_sems), "out of dma sems"
            self.slot_map[key] = len(self.slot_map)
        return self.slot_map[key]


class Phase:
    def __init__(self, ctx, name):
        self.ctx = ctx
        self.name = name
        self.ops = []

    def _add(self, eng, fn, reads, writes, is_dma=False, slot=None):
        c = self.ctx
        if _COARSE:
            pat = _os.environ.get("K_COARSE")
            reads = [_re.sub(pat, "", k) for k in reads]
            writes = [_re.sub(pat, "", k) for k in writes]
        op = Op()
        op.eng, op.fn, op.is_dma, op.slot = eng, fn, is_dma, slot
        op.needs_inc = is_dma
        op.tok = None
        deps = []
        for b in reads:
            w = c.last_writer.get(b)
            if w is not None:
                deps.append(w)
        for b in writes:
            w = c.last_writer.get(b)
            if w is not None:
                deps.append(w)
            last = {}
            for rd in c.readers.get(b, ()):
                if rd.is_dma:
                    deps.append(rd)
                else:
                    last[rd.eng] = rd
            deps.extend(last.values())
        op.deps = deps
        for b in reads:
            c.readers.setdefault(b, []).append(op)
        for b in writes:
            c.last_writer[b] = op
            c.readers[b] = []
        self.ops.append(op)
        return op

    def op(self, eng, fn, reads=(), writes=()):
        return self._add(eng, fn, reads, writes)

    def dma(self, queue, fn, reads=(), writes=(), slot=None):
        return self._add(queue, fn, reads, writes, is_dma=True, slot=self.ctx.slot_id(slot))

    def emit(self):
        c = self.ctx
        nc = c.nc
        for op in self.ops:
            for d in op.deps:
                if d.is_dma:
                    continue
                if d.eng == PE and op.eng == PE and not op.is_dma:
                    continue
                d.needs_inc = True
        for op in self.ops:
            if op.is_dma:
                c.dma_cnt[op.slot] += 16
                op.tok = (c.dma_sems[op.slot], c.dma_cnt[op.slot], ("d", op.slot))
            elif op.needs_inc:
                c.eng_cnt[op.eng] += 1
                op.tok = (c.eng_sem[op.eng], c.eng_cnt[op.eng], ("e", op.eng))
        per_eng = {e: [] for e in (PE, ACT, DVE, POOL, SP)}
        last_dma_tok = {}
        for op in self.ops:
            waits = {}
            for d in op.deps:
                if d.tok is None:
                    continue
                if (not d.is_dma) and d.eng == PE and op.eng == PE and not op.is_dma:
                    continue
                sem, val, key = d.tok
                if c.waited[op.eng].get(key, 0) >= val:
                    continue
                if key not in waits or waits[key][1] < val:
                    waits[key] = (sem, val)
            for key, (sem, val) in waits.items():
                c.waited[op.eng][key] = val
            per_eng[op.eng].append((op, list(waits.values())))
            if op.is_dma:
                last_dma_tok[op.tok[2]] = op.tok
        final_waits = []
        for key, (sem, val, _) in last_dma_tok.items():
            if c.waited[SP].get(key, 0) < val:
                c.waited[SP][key] = val
                final_waits.append((sem, val))
        with nc.Block() as block:
            def run(eng_name):
                def body(e):
                    for op, waits in per_eng[eng_name]:
                        for sem, val in waits:
                            e.wait_ge(sem, val)
                        ins = op.fn(e)
                        if op.is_dma:
                            ins.then_inc(op.tok[0], 16)
                        elif op.needs_inc:
                            ins.then_inc(op.tok[0], 1)
                    if eng_name == SP:
                        for sem, val in final_waits:
                            e.wait_ge(sem, val)
                return body
            block.sync(run(SP))
            if per_eng[ACT]:
                block.scalar(run(ACT))
            if per_eng[DVE]:
                block.vector(run(DVE))
            if per_eng[POOL]:
                block.gpsimd(run(POOL))
            if per_eng[PE]:
                block.tensor(run(PE))
        self.ops = []
        return {e: len(v) for e, v in per_eng.items()}


def _gammas():
    h = np.arange(4, dtype=np.float64)
    lf = np.log1p(-np.exp2(-5.0 - 0.0 - h))
    lb = np.log1p(-np.exp2(-5.0 - 0.5 - h))
    return lf, lb


def _tile_tables(pos0_list, valid_list):
    lf, lb = _gammas()
    inv = 10000.0 ** (-(np.arange(64, dtype=np.float64) / 64.0))
    p = np.arange(128, dtype=np.float64)
    out = np.zeros((len(pos0_list), 128, 276), np.float32)
    ks = 128.0 ** -0.5
    for i, (pos0, valid) in enumerate(zip(pos0_list, valid_list)):
        pos = (pos0 + np.arange(128)).astype(np.float64)
        ang = pos[:, None] * inv[None, :]
        out[i, :, 0:64] = np.cos(ang)
        out[i, :, 64:128] = np.cos(ang)
        out[i, :, 128:192] = -np.sin(ang)
        out[i, :, 192:256] = np.sin(ang)
        out[i, :, 256:260] = np.exp(lf[None, :] * (p[:, None] + 1.0))
        out[i, :, 260:264] = np.exp(lb[None, :] * (128.0 - p[:, None]))
        out[i, :, 264:268] = ks * np.exp(lf[None, :] * (127.0 - p[:, None])) * valid
        out[i, :, 268:272] = ks * np.exp(lb[None, :] * p[:, None]) * valid
        out[i, :, 272] = valid
    return out.reshape(len(pos0_list) * 128, 276)


def _mask_tables():
    lf, lb = _gammas()
    j = np.arange(128, dtype=np.float64)[:, None]
    i = np.arange(128, dtype=np.float64)[None, :]
    m = np.zeros((128, 4, 128), np.float32)
    for h in range(4):
        f = np.exp(lf[h] * np.maximum(i - j, 0.0))
        b = np.exp(lb[h] * np.maximum(j - i, 0.0))
        m[:, h, :] = (128.0 ** -0.5) * np.where(j <= i, f, b)
    g8 = np.zeros((128, 8), np.float32)
    g8[:, 0:4] = np.exp(lf * 128.0)[None, :]
    g8[:, 4:8] = np.exp(lb * 128.0)[None, :]
    return m, g8


def _fft_tables(T, vmap, k1_list):
    N = 128 * T
    k2 = np.arange(T, dtype=np.float64)
    f1 = np.zeros((T, 4 * T), np.float32)
    for v_, t in enumerate(vmap):
        if t < 0:
            continue
        v = v_ % T
        a = 2 * np.pi * ((t * k2) % T) / T
        f1[v, 0:T] = np.cos(a)
        f1[v, T:2 * T] = -np.sin(a)
        f1[v, 2 * T:3 * T] = -np.sin(a)
        f1[v, 3 * T:4 * T] = -np.cos(a)
    r = np.arange(128, dtype=np.float64)[:, None, None]
    k1 = np.asarray(k1_list, dtype=np.float64)[None, None, :]
    kk = k2[None, :, None]
    ph = (r * (k1 * T + kk)) % N
    al = 2 * np.pi * ph / N
    sc = 1.0 / np.sqrt(N)
    c2k = np.stack([np.cos(al) * sc, np.sin(al) * sc], axis=2).astype(np.float32)
    return f1, np.ascontiguousarray(c2k)


def build(Tp, Ts):
    NOWN = Tp // 4
    Tvp = Tp + 3
    own_v = [3 + 4 * m for m in range(NOWN)]
    seqs = {
        "p": dict(T=Tp, Tv=Tvp, own=own_v, NO=NOWN * 128 // Tp, row=0),
        "s": dict(T=Ts, Tv=Ts, own=list(range(Ts)), NO=128, row=1),
    }
    nc = bass.Bass("TRN2", target_bir_lowering=False)
    din = lambda n, s, d=F32: nc.dram_tensor(n, list(s), d, kind="ExternalInput").ap()
    dout = lambda n, s, d=F32: nc.dram_tensor(n, list(s), d, kind="ExternalOutput").ap()
    dscr = lambda n, s, d: nc.dram_tensor(n, list(s), d, kind="Internal").ap()

    X = {"p": din("xv", [Tvp * 128, D]), "s": din("xs", [Ts * 128, D])}
    TAB = {"p": din("tab_p", [Tvp * 128, 276]), "s": din("tab_s", [Ts * 128, 276])}
    F1 = {"p": din("f1_p", [Tp, 4 * Tp]), "s": din("f1_s", [Ts, 4 * Ts])}
    C2 = {"p": din("c2_p", [128, Tp, 2, NOWN * 128 // Tp]), "s": din("c2_s", [128, Ts, 2, 128])}
    c2T = din("c2T", [128, NCH, 2])
    ada_w = din("ada_w", [D, 6 * D])
    ada_b2 = din("ada_b2", [2, 6 * D])
    nm_fm = din("nm_fm", [128, 2, NCH])
    nf_bc = din("nf_bc", [128, D])
    w_in = din("w_in", [D, 2560])
    w_fnet = din("w_fnet", [4, 128, 128])
    w_out = din("w_out", [D, D])
    w_mlp_in = din("w_mlp_in", [D, DFF])
    w_mlp_out = din("w_mlp_out", [DFF, D])
    ident_d = din("ident", [128, 128])
    cs_d = din("cs128", [128, 256])
    mask_d = din("maskT", [128, 4, 128])
    g8_d = din("g8", [128, 8])
    Y = {"p": dout("yp", [NOWN * 128, D]), "s": dout("ys", [Ts * 128, D])}

    PQ = {"p": dscr("pq_p", [Tvp * 128, 1024], BF16), "s": dscr("pq_s", [Ts * 128, 1024], BF16)}
    MIX = {"p": dscr("mix_p", [512, NOWN * 128], BF16), "s": dscr("mix_s", [512, Ts * 128], BF16)}
    X1 = {"p": dscr("x1_p", [NOWN * 128, D], F32), "s": dscr("x1_s", [Ts * 128, D], F32)}
    wab_d = dscr("wab", [D, 1024], BF16)
    mod_d = dscr("mod", [2, 6 * D], F32)

    with contextlib.ExitStack() as gs:
        ctx = Ctx(nc, gs)
        uid = [0]

        def SB(st, n, s, d):
            uid[0] += 1
            return st.enter_context(nc.sbuf_tensor("%s_u%d" % (n, uid[0]), list(s), d))

        def PS(st, n):
            uid[0] += 1
            return st.enter_context(nc.psum_tensor("%s_u%d" % (n, uid[0]), [128, 512], F32))
        ident = SB(gs, "ident", [128, 128], BF16)
        mh = SB(gs, "mh", [128, 1], F32)
        epsb = SB(gs, "epsb", [128, 1], F32)
        modfm = SB(gs, "modfm", [128, 2, 4, NCH], F32)
        nmt = SB(gs, "nmt", [128, 2, NCH], F32)

        with contextlib.ExitStack() as st:
            P = Phase(ctx, "setup")
            c2 = SB(st, "c2", [128, NCH, 2], F32)
            scb = SB(st, "scb", [128, NCH, 2], BF16)
            adab = SB(st, "adab", [2, 6 * D], F32)
            modsb = SB(st, "modsb", [2, 6 * D], F32)
            aw = [SB(st, "aw%d" % i, [128, NCH, 512], BF16) for i in range(2)]
            wfb = SB(st, "wfb", [128, 4, 128], BF16)
            csb = SB(st, "csb", [128, 256], BF16)
            wcws = SB(st, "wcws", [128, 4, 256], BF16)
            wub = SB(st, "wub", [128, NCH, 512], BF16)
            wuT = SB(st, "wuT", [128, 4, NCH, 128], BF16)
            wabs = SB(st, "wabs", [128, NCH, 1024], BF16)
            pm = [PS(st, "pm%d" % i) for i in range(2)]
            pw = PS(st, "pw")
            pt = [PS(st, "ptb%d" % i) for i in range(2)]
            pab = [PS(st, "pab%d" % i) for i in range(2)]

            P.dma(POOL, lambda e: e.dma_start(out=ident[:], in_=ident_d[:, :]), writes=["ident"], slot="ident")
            P.op(POOL, lambda e: e.memset(mh[:], -0.5), writes=["mh"])
            P.op(POOL, lambda e: e.memset(epsb[:], EPS), writes=["epsb"])
            P.dma(SP, lambda e: e.dma_start(out=c2[:], in_=c2T[:, :, :]), writes=["c2"], slot="c2")
            P.dma(SP, lambda e: e.dma_start(out=adab[:], in_=ada_b2[:, :]), writes=["adab"], slot="adab")
            P.op(ACT, lambda e: e.activation(out=scb[:], in_=c2[:], func=AF.Silu), reads=["c2"], writes=["scb"])
            for nb in range(12):
                a = aw[nb % 2]
                ak = "aw%d" % (nb % 2)
                P.dma(POOL, lambda e, a=a, nb=nb: e.dma_start(
                    out=a[:], in_=ada_w[:, nb * 512:(nb + 1) * 512].rearrange("(k p) n -> p k n", p=128)),
                    writes=[ak], slot=ak)
                pmb = pm[nb % 2]
                pk = "pm%d" % (nb % 2)
                for k in range(NCH):
                    P.op(PE, lambda e, a=a, k=k, pmb=pmb: e.matmul(pmb[0:2, :], lhsT=scb[:, k, :], rhs=a[:, k, :],
                                                                 start=(k == 0), stop=(k == NCH - 1)),
                         reads=["scb", ak], writes=[pk])
                P.op(DVE, lambda e, nb=nb, pmb=pmb: e.tensor_tensor(out=modsb[:, nb * 512:(nb + 1) * 512], in0=pmb[0:2, :],
                                                                   in1=adab[:, nb * 512:(nb + 1) * 512], op=ALU.add),
                     reads=[pk, "adab"], writes=["modsb"])
            P.dma(SP, lambda e: e.dma_start(out=mod_d[:, :], in_=modsb[:]), reads=["modsb"], writes=["mod_d"], slot="mod_d")
            id2 = SB(st, "id2", [2, 2], F32)
            pmf = PS(st, "pmf")
            P.dma(SP, lambda e: e.dma_start(out=id2[:], in_=ident_d[0:2, 0:2]), writes=["id2"], slot="id2")
            P.dma(SP, lambda e: e.dma_start(out=nmt[:], in_=nm_fm[:, :, :]), writes=["nmt"], slot="nmt")
            cols_ = [1, 0, 4, 3]
            for qi, q in enumerate(cols_):
                for k in range(NCH):
                    j = qi * NCH + k
                    P.op(PE, lambda e, q=q, k=k, j=j: e.transpose(out=pmf[:, j * 2:(j + 1) * 2],
                                                                  in_=modsb[0:2, q * D + k * 128:q * D + (k + 1) * 128], identity=id2[:]),
                         reads=["modsb", "id2"], writes=["pmf"])
            P.op(DVE, lambda e: e.tensor_copy(out=modfm[:], in_=pmf[:, 0:64].rearrange("p (q k r) -> p r q k", q=4, k=NCH, r=2)),
                 reads=["pmf"], writes=["modfm"])
            for r in range(2):
                for (qi, ni) in ((0, 0), (2, 1)):
                    P.op(DVE, lambda e, r=r, qi=qi, ni=ni: e.scalar_tensor_tensor(
                        out=modfm[:, r, qi, :], in0=modfm[:, r, qi, :], scalar=1.0, in1=nmt[:, ni, :],
                        op0=ALU.add, op1=ALU.mult), reads=["modfm", "nmt"], writes=["modfm"])
            P.dma(POOL, lambda e: e.dma_start(out=wfb[:], in_=w_fnet.rearrange("g c d -> c g d")), writes=["wfb"], slot="wfb")
            P.dma(POOL, lambda e: e.dma_start(out=csb[:], in_=cs_d[:, :]), writes=["csb"], slot="csb")
            P.dma(POOL, lambda e: e.dma_start(out=wub[:], in_=w_in[:, 0:512].rearrange("(k p) n -> p k n", p=128)),
                  writes=["wub"], slot="wub")
            for g in range(4):
                for half in range(2):
                    P.op(PE, lambda e, g=g, half=half: e.matmul(pw[:, 0:128], lhsT=csb[:, half * 128:(half + 1) * 128],
                                                              rhs=wfb[:, g, :], start=True, stop=True),
                         reads=["csb", "wfb"], writes=["pw"])
                    P.op(DVE, lambda e, g=g, half=half: e.tensor_copy(out=wcws[:, g, half * 128:(half + 1) * 128], in_=pw[:, 0:128]),
                         reads=["pw"], writes=["wcws"])
            for k in range(NCH):
                ptb = pt[k % 2]
                pk = "ptb%d" % (k % 2)
                ptv = ptb[:, :].bitcast(BF16)
                for g in range(4):
                    P.op(PE, lambda e, k=k, g=g, ptv=ptv: e.transpose(out=ptv[:, g * 128:(g + 1) * 128],
                                                                    in_=wub[:, k, g * 128:(g + 1) * 128], identity=ident[:]),
                         reads=["wub", "ident"], writes=[pk])
                P.op(ACT, lambda e, k=k, ptv=ptv: e.activation(out=wuT[:, :, k, :],
                                                              in_=ptv[:, 0:512].rearrange("p (g d) -> p g d", g=4), func=AF.Copy),
                     reads=[pk], writes=["wuT"])
            for k in range(NCH):
                for gp in range(2):
                    pb = pab[(k * 2 + gp) % 2]
                    pk = "pab%d" % ((k * 2 + gp) % 2)
                    for gi in range(2):
                        g = gp * 2 + gi
                        P.op(PE, lambda e, k=k, g=g, gi=gi, pb=pb: e.matmul(pb[:, gi * 256:(gi + 1) * 256], lhsT=wuT[:, g, k, :],
                                                                           rhs=wcws[:, g, :], start=True, stop=True),
                             reads=["wuT", "wcws"], writes=[pk])
                    P.op(DVE if gp == 0 else ACT,
                         (lambda e, k=k, gp=gp, pb=pb: e.tensor_copy(out=wabs[:, k, gp * 512:(gp + 1) * 512], in_=pb[:, :])) if gp == 0 else
                         (lambda e, k=k, gp=gp, pb=pb: e.activation(out=wabs[:, k, gp * 512:(gp + 1) * 512], in_=pb[:, :], func=AF.Copy)),
                         reads=[pk], writes=["wabs"])
            P.dma(SP, lambda e: e.dma_start(out=wab_d.rearrange("(k p) n -> p k n", p=128), in_=wabs[:]),
                  reads=["wabs"], writes=["wab_d"], slot="wab_d")
            print("setup", P.emit())

        def ring(st, name, n, shape, dt):
            return [(SB(st, "%s%d" % (name, i), shape, dt), "%s%d" % (name, i)) for i in range(n)]

        def run_pipeline(stage_lists, nstages, order=None):
            n = len(stage_lists)
            order = order or list(range(nstages - 1, -1, -1))
            for t in range(n + nstages - 1):
                for k in order:
                    i = t - k
                    if 0 <= i < n and k < len(stage_lists[i]) and stage_lists[i][k] is not None:
                        stage_lists[i][k]()

        SNAP = {s: dscr("snapd_" + s, [len(seqs[s]["own"]) * 128, 512], BF16) for s in seqs}
        KV = {s: dscr("kvd_" + s, [seqs[s]["Tv"] * 128, 1024], BF16) for s in seqs}
        _lf, _lb = _gammas()
        G128 = [float(np.exp(_lf[h] * 128.0)) for h in range(4)] + [float(np.exp(_lb[h] * 128.0)) for h in range(4)]

        def state_update(P, S8, half, Bsrc, Bk, skey):
            for h in range(4):
                hh = half * 4 + h
                P.op(DVE, lambda e, h=h, hh=hh: e.scalar_tensor_tensor(out=S8[:, hh, :], in0=S8[:, hh, :], scalar=G128[hh],
                                                                      in1=Bsrc[:, h * 128:(h + 1) * 128], op0=ALU.mult, op1=ALU.add),
                     reads=[skey, Bk], writes=[skey])

        with contextlib.ExitStack() as ms:
            S8 = SB(ms, "S8", [128, 8, 128], F32)
            g8 = SB(ms, "g8", [128, 8], F32)
            maskT = SB(ms, "maskT", [128, 4, 128], F32)
            stat = SB(ms, "stat", [128, 3, 8], F32)
            junk = SB(ms, "junk", [128, D], BF16)

            def load_mod_tables(P, modfm=modfm, nmt=nmt):
                P.dma(SP, lambda e: e.dma_start(out=nmt[:], in_=nm_fm[:, :, :]), writes=["nmt"], slot="nmt")
                cols = [1, 0, 4, 3]
                for r in range(2):
                    for qi, q in enumerate(cols):
                        P.dma(SP, lambda e, r=r, qi=qi, q=q: e.dma_start(
                            out=modfm[:, r, qi, :], in_=mod_d[r, q * D:(q + 1) * D].rearrange("(k p) -> p k", p=128),
                            allow_slow_non_contiguous=True),
                            reads=["mod_d"], writes=["modfm"], slot="modfm")
                for r in range(2):
                    for (qi, ni) in ((0, 0), (2, 1)):
                        P.op(DVE, lambda e, r=r, qi=qi, ni=ni: e.scalar_tensor_tensor(
                            out=modfm[:, r, qi, :], in0=modfm[:, r, qi, :], scalar=1.0, in1=nmt[:, ni, :],
                            op0=ALU.add, op1=ALU.mult), reads=["modfm", "nmt"], writes=["modfm"])

            def norm_a(P, src_ap, src_key, xn, xnk, col, stat=stat, junk=junk, out_dt_scale=None):
                ssc, msc, rsc = stat[:, 0, col:col + 1], stat[:, 1, col:col + 1], stat[:, 2, col:col + 1]
                P.op(ACT, lambda e: e.activation(out=junk[:], in_=src_ap, func=AF.Square, accum_out=ssc),
                     reads=[src_key], writes=["ss%d" % col])
                P.op(DVE, lambda e: e.tensor_scalar(out=msc, in0=ssc, scalar1=1.0 / D, scalar2=EPS, op0=ALU.mult, op1=ALU.add),
                     reads=["ss%d" % col], writes=["ms%d" % col])
                P.op(POOL, lambda e: e.tensor_tensor(out=rsc, in0=msc, in1=mh[:, 0:1], op=ALU.pow),
                     reads=["ms%d" % col, "mh"], writes=["rs%d" % col])
                P.op(ACT, lambda e: e.activation(out=xn, in_=src_ap, func=AF.Identity, scale=rsc),
                     reads=[src_key, "rs%d" % col], writes=[xnk])

            def trans_b(P, xn, xnk, banks, r, qs, dst_fn, dstk, modfm=modfm):
                per = NCH // len(banks)
                for c in range(NCH):
                    Bt, Btk, _ = banks[c // per]
                    bv = Bt[:, :].bitcast(BF16)
                    j = c % per
                    P.op(PE, lambda e, c=c, j=j, bv=bv: e.transpose(out=bv[:, j * 128:(j + 1) * 128], in_=xn[:, c * 128:(c + 1) * 128],
                                                                  identity=ident[:]), reads=[xnk, "ident"], writes=[Btk])
                for c in range(NCH):
                    Bt, Btk, eng = banks[c // per]
                    bv = Bt[:, :].bitcast(BF16)
                    j = c % per
                    if eng == ACT:
                        P.op(ACT, lambda e, c=c, j=j, bv=bv: e.activation(out=dst_fn(c), in_=bv[:, j * 128:(j + 1) * 128], func=AF.Identity,
                                                                        scale=modfm[:, r, qs, c:c + 1], bias=modfm[:, r, qs + 1, c:c + 1]),
                             reads=[Btk, "modfm"], writes=[dstk(c)])
                    else:
                        P.op(DVE, lambda e, c=c, j=j, bv=bv: e.tensor_scalar(out=dst_fn(c), in0=bv[:, j * 128:(j + 1) * 128],
                                                                           scalar1=modfm[:, r, qs, c:c + 1], scalar2=modfm[:, r, qs + 1, c:c + 1],
                                                                           op0=ALU.mult, op1=ALU.add),
                             reads=[Btk, "modfm"], writes=[dstk(c)])

            def rotary(P, Bk, Bkk, tabt, tabk, dst, dstk, tmp):
                (ta, tak), (tb, tbk_) = tmp[0], tmp[1]
                v = Bk[:, :].rearrange("p (h t d) -> p h t d", h=4, t=2)
                cc = tabt[:, 0:128].rearrange("p (t d) -> p t d", t=2).unsqueeze(1).broadcast_to([128, 4, 2, 64])
                sv = tabt[:, 128:256].rearrange("p (t d) -> p t d", t=2).unsqueeze(1).broadcast_to([128, 4, 2, 64])
                P.op(DVE, lambda e: e.tensor_tensor(out=ta[:].rearrange("p h (t d) -> p h t d", t=2), in0=v, in1=cc, op=ALU.mult),
                     reads=[Bkk, tabk], writes=[tak])
                P.op(DVE, lambda e: e.tensor_tensor(out=tb[:].rearrange("p h (t d) -> p h t d", t=2), in0=v[:, :, ::-1, :], in1=sv, op=ALU.mult),
                     reads=[Bkk, tabk], writes=[tbk_])
                P.op(POOL, lambda e: e.tensor_tensor(out=dst[:], in0=ta[:], in1=tb[:], op=ALU.add), reads=[tak, tbk_], writes=[dstk])

            with contextlib.ExitStack() as st:
                P = Phase(ctx, "A1")
                W = SB(st, "W_a1", [128, NCH, 2048], BF16)
                xt = ring(st, "xt", 3, [128, D], F32)
                tabt = ring(st, "tabt", 6, [128, 276], F32)
                xn = ring(st, "xn", 2, [128, D], BF16)
                hT = ring(st, "hT", 2, [128, NCH, 128], BF16)
                pqs = ring(st, "pqs", 2, [128, 1024], BF16)
                kvs = ring(st, "kvs", 2, [128, 1024], BF16)
                tmpr = [ring(st, "tmp%d_" % i, 2, [128, 4, 128], F32) for i in range(2)]
                kr = ring(st, "kr", 2, [128, 4, 128], F32)
                kh = ring(st, "kh", 2, [128, 4, 128], BF16)
                snb = ring(st, "snb", 2, [128, 4, 128], BF16)
                B = [PS(st, "B%d" % i) for i in range(8)]

                P.dma(POOL, lambda e: e.dma_start(out=W[:, :, 0:1024], in_=wab_d.rearrange("(k p) n -> p k n", p=128)),
                      reads=["wab_d"], writes=["W"], slot="W0")
                P.dma(POOL, lambda e: e.dma_start(out=W[:, :, 1024:2048], in_=w_in[:, 1024:2048].rearrange("(k p) n -> p k n", p=128)),
                      writes=["W"], slot="W1")
                P.dma(SP, lambda e: e.dma_start(out=maskT[:], in_=mask_d[:, :, :]), writes=["maskT"], slot="maskT")
                stages = []
                cnt = 0
                for s in ("p", "s"):
                    sq = seqs[s]
                    r = sq["row"]
                    own_pos = {v: m for m, v in enumerate(sq["own"])}
                    for v in range(sq["Tv"] - 1, -1, -1):
                        i = cnt
                        cnt += 1
                        first = (v == sq["Tv"] - 1)

                        def sl(i=i, s=s, v=v):
                            x_t, xk = xt[i % 3]
                            tb_t, tbk = tabt[i % 6]
                            P.dma(SP, lambda e: e.dma_start(out=x_t[:], in_=X[s][v * 128:(v + 1) * 128, :]), writes=[xk], slot=xk)
                            P.dma(SP, lambda e: e.dma_start(out=tb_t[:], in_=TAB[s][v * 128:(v + 1) * 128, :]), writes=[tbk], slot=tbk)

                        def s0(i=i, s=s, v=v):
                            x_t, xk = xt[i % 3]
                            norm_a(P, x_t[:], xk, xn[i % 2][0][:], xn[i % 2][1], i % 8)

                        def s1(i=i, r=r):
                            h_t, hk = hT[i % 2]
                            trans_b(P, xn[i % 2][0], xn[i % 2][1], [(B[0], "B0", ACT), (B[1], "B1", DVE)], r, 0, lambda c: h_t[:, c, :],
                                    lambda c: "%s_%d" % (hk, c))

                        def s2(i=i, s=s, v=v):
                            h_t, hk = hT[i % 2]
                            tb_t, tbk = tabt[i % 6]
                            pq_t, pqk = pqs[i % 2]
                            for half in range(2):
                                for k in range(NCH):
                                    P.op(PE, lambda e, half=half, k=k: e.matmul(B[2 + half][:, :], lhsT=h_t[:, k, :],
                                                                               rhs=W[:, k, half * 512:(half + 1) * 512],
                                                                               start=(k == 0), stop=(k == NCH - 1)),
                                         reads=["%s_%d" % (hk, c_) for c_ in range(NCH)] + ["W"], writes=["B%d" % (2 + half)])
                            P.op(ACT, lambda e: e.activation(out=pq_t[:, 0:512], in_=B[2][:, :], func=AF.Identity, scale=tb_t[:, 272:273]),
                                 reads=["B2", tbk], writes=[pqk + "_a"])
                            P.op(DVE, lambda e: e.tensor_scalar(out=pq_t[:, 512:1024], in0=B[3][:, :], scalar1=tb_t[:, 272:273],
                                                                scalar2=None, op0=ALU.mult), reads=["B3", tbk], writes=[pqk + "_b"])
                            P.dma(SP, lambda e: e.dma_start(out=PQ[s][v * 128:(v + 1) * 128, :], in_=pq_t[:]),
                                  reads=[pqk + "_a", pqk + "_b"], writes=["PQ_" + s], slot="st_" + pqk)
                            bk = 4 if i % 2 == 0 else 7
                            for (bi, c0) in ((bk, 1024), (5, 1536)):
                                for k in range(NCH):
                                    P.op(PE, lambda e, bi=bi, c0=c0, k=k: e.matmul(B[bi][:, :], lhsT=h_t[:, k, :], rhs=W[:, k, c0:c0 + 512],
                                                                                  start=(k == 0), stop=(k == NCH - 1)),
                                         reads=["%s_%d" % (hk, c_) for c_ in range(NCH)] + ["W"], writes=["B%d" % bi])
                            v_t, vk = kvs[i % 2]
                            P.op(ACT, lambda e: e.activation(out=v_t[:, 512:1024], in_=B[5][:, :], func=AF.Copy), reads=["B5"], writes=[vk + "_v"])
                            kr_t, krk = kr[i % 2]
                            rotary(P, B[bk], "B%d" % bk, tb_t, tbk, kr_t, krk, tmpr[i % 2])
                            kh_t, khk = kh[i % 2]
                            P.op(POOL, lambda e: e.tensor_tensor(out=kh_t[:], in0=kr_t[:],
                                                                 in1=tb_t[:, 268:272].unsqueeze(2).broadcast_to([128, 4, 128]), op=ALU.mult),
                                 reads=[krk, tbk], writes=[khk])
                            P.op(POOL, lambda e: e.tensor_tensor(out=v_t[:, 0:512].rearrange("p (h d) -> p h d", h=4), in0=kr_t[:],
                                                                 in1=tb_t[:, 264:268].unsqueeze(2).broadcast_to([128, 4, 128]), op=ALU.mult),
                                 reads=[krk, tbk], writes=[vk + "_k"])
                            P.dma(SP, lambda e: e.dma_start(out=KV[s][v * 128:(v + 1) * 128, :], in_=v_t[:]),
                                  reads=[vk + "_k", vk + "_v"], writes=["KV_" + s], slot="st_" + vk)

                        def s3(i=i, s=s, v=v, first=first, own_pos=own_pos):
                            kh_t, khk = kh[i % 2]
                            v_t, vk = kvs[i % 2]
                            if first:
                                P.op(POOL, lambda e: e.memset(S8[:, 4:8, :], 0.0), writes=["S8b"])
                            for h in range(4):
                                P.op(PE, lambda e, h=h: e.matmul(B[6][:, h * 128:(h + 1) * 128], lhsT=kh_t[:, h, :],
                                                                 rhs=v_t[:, 512 + h * 128:512 + (h + 1) * 128], start=True, stop=True),
                                     reads=[khk, vk + "_v"], writes=["B6"])
                            if v in own_pos:
                                m = own_pos[v]
                                sn_t, snk = snb[m % 2]
                                P.op(ACT, lambda e: e.activation(out=sn_t[:], in_=S8[:, 4:8, :], func=AF.Copy), reads=["S8b"], writes=[snk])
                                P.dma(SP, lambda e: e.dma_start(out=SNAP[s][m * 128:(m + 1) * 128, :], in_=sn_t[:].rearrange("p h d -> p (h d)")),
                                      reads=[snk], writes=["SNAP_" + s], slot="st_" + snk)
                            state_update(P, S8, 1, B[6], "B6", "S8b")

                        stages.append([sl, s0, s1, s2, s3])
                run_pipeline(stages, 5, order=[0, 1, 2, 3, 4])
                print("A1", P.emit())

            for s in ("p", "s"):
                sq = seqs[s]
                T, Tv, NO = sq["T"], sq["Tv"], sq["NO"]
                Ka = T
                Kb = Tv - T
                NB = min(64, 1024 // (2 * T))
                KB2 = max(1, min(T, 512 // NO))
                with contextlib.ExitStack() as st:
                    P = Phase(ctx, "F" + s)
                    Za_r = ring(st, "Za", 2, [Ka, 128, 256], BF16)
                    zbp = ring(st, "zbp", 2, [max(Kb, 1), 4, 256], BF16)
                    f1a = SB(st, "f1a", [Ka, 4 * T], BF16)
                    c2s = SB(st, "c2s", [128, T, 2, NO], BF16)
                    Yp = SB(st, "Yp", [128, 2, 64, T], BF16)
                    f2t = SB(st, "f2t", [64, NO, T], BF16)
                    YB = [st.enter_context(nc.psum_tensor("YB%d_%s" % (i, s), [128, 1024], F32)) for i in range(2)]
                    OB = [PS(st, "OB%d" % i) for i in range(4)]
                    P.dma(POOL, lambda e: e.dma_start(out=f1a[:], in_=F1[s][0:Ka, :]), writes=["f1a"], slot="f1a")
                    cstep = max(1, 1024 // (2 * NO))
                    for t0_ in range(0, T, cstep):
                        t1_ = min(T, t0_ + cstep)
                        P.dma(POOL, lambda e, t0_=t0_, t1_=t1_: e.dma_start(out=c2s[:, t0_:t1_, :, :], in_=C2[s][:, t0_:t1_, :, :]),
                              writes=["c2s"], slot="c2s")

                    def load_group(g):
                        Za, Zak = Za_r[g % 2]
                        for z0 in range(0, Ka, 16):
                            z1 = min(Ka, z0 + 16)
                            P.dma(SP, lambda e, z0=z0, z1=z1: e.dma_start(
                                out=Za[z0:z1, :, :], in_=PQ[s][z0 * 128:z1 * 128, g * 256:(g + 1) * 256].rearrange("(t r) c -> t r c", r=128)),
                                reads=["PQ_" + s], writes=["%s_%d" % (Zak, z0 // 16)], slot=Zak)
                    def fold_group(g):
                        Za, Zak = Za_r[g % 2]
                        if Kb:
                            for pc in range(32):
                                zt, zk = zbp[pc % 2]
                                P.dma(SP, lambda e, pc=pc, zt=zt: e.dma_start(
                                    out=zt[:], in_=PQ[s][Ka * 128:Tv * 128, g * 256:(g + 1) * 256].rearrange(
                                        "(t r) c -> t r c", r=128)[:, pc * 4:(pc + 1) * 4, :]),
                                    reads=["PQ_" + s], writes=[zk], slot=zk)
                                P.op(DVE, lambda e, pc=pc, zt=zt: e.tensor_tensor(out=Za[0:Kb, pc * 4:(pc + 1) * 4, :],
                                                                                 in0=Za[0:Kb, pc * 4:(pc + 1) * 4, :], in1=zt[:], op=ALU.add),
                                     reads=["%s_%d" % (Zak, z_) for z_ in range((Ka + 15) // 16)] + [zk], writes=[Zak + "_0"])

                    ybc = 0
                    obc = 0
                    load_group(0)
                    fold_group(0)
                    for g in range(4):
                        if g + 1 < 4:
                            load_group(g + 1)
                        Za, Zak = Za_r[g % 2]
                        for dh in range(2):
                            for b0 in range(0, 64, NB):
                                yb = YB[ybc % 2]
                                ybk = "YB%d" % (ybc % 2)
                                for i in range(NB):
                                    dp = dh * 64 + b0 + i
                                    for pl in range(2):
                                        P.op(PE, lambda e, pl=pl, dp=dp, i=i, yb=yb, Za=Za: e.matmul(
                                            yb[:, i * 2 * T:(i + 1) * 2 * T], lhsT=Za[:, :, pl * 128 + dp],
                                            rhs=f1a[:, pl * 2 * T:(pl + 1) * 2 * T], start=(pl == 0), stop=(pl == 1)),
                                             reads=["%s_%d" % (Zak, z_) for z_ in range((Ka + 15) // 16)] + ["f1a"], writes=[ybk])
                                yv = yb[:, 0:NB * 2 * T].rearrange("p (n c t) -> p n c t", n=NB, c=2)
                                for c in range(2):
                                    dst = Yp[:, c, b0:b0 + NB, :]
                                    src = yv[:, :, c, :]
                                    if ybc % 2 == 0:
                                        P.op(ACT, lambda e, src=src, dst=dst: e.activation(out=dst, in_=src, func=AF.Copy),
                                             reads=[ybk], writes=["Yp_%d_%d" % (b0 // NB, c)])
                                    else:
                                        P.op(DVE, lambda e, src=src, dst=dst: e.tensor_copy(out=dst, in_=src),
                                             reads=[ybk], writes=["Yp_%d_%d" % (b0 // NB, c)])
                                ybc += 1
                            ypk = ["Yp_%d_%d" % (b, c) for b in range(64 // NB) for c in range(2)]
                            for k0 in range(0, T, KB2):
                                ob = OB[obc % 4]
                                obk = "OB%d" % (obc % 4)
                                obc += 1
                                for kk in range(KB2):
                                    k2 = k0 + kk
                                    for c in range(2):
                                        P.op(PE, lambda e, c=c, k2=k2, kk=kk, ob=ob: e.matmul(
                                            ob[0:64, kk * NO:(kk + 1) * NO], lhsT=Yp[:, c, :, k2], rhs=c2s[:, k2, c, :],
                                            start=(c == 0), stop=(c == 1)), reads=ypk + ["c2s"], writes=[obk])
                                src = ob[0:64, 0:KB2 * NO].rearrange("p (k m) -> p m k", k=KB2)
                                if obc % 2:
                                    P.op(ACT, lambda e, k0=k0, src=src: e.activation(out=f2t[:, :, k0:k0 + KB2], in_=src, func=AF.Copy),
                                         reads=[obk], writes=["f2_%d" % (k0 // KB2)])
                                else:
                                    P.op(DVE, lambda e, k0=k0, src=src: e.tensor_copy(out=f2t[:, :, k0:k0 + KB2], in_=src),
                                         reads=[obk], writes=["f2_%d" % (k0 // KB2)])
                            P.dma(SP, lambda e, g=g, dh=dh: e.dma_start(
                                out=MIX[s][g * 128 + dh * 64:g * 128 + dh * 64 + 64, :], in_=f2t[:].rearrange("p m k -> p (m k)")),
                                reads=["f2_%d" % b for b in range((T + KB2 - 1) // KB2)], writes=["MIX_" + s], slot="st_f2")
                        if g + 1 < 4:
                            fold_group(g + 1)
                    print("F" + s, P.emit())

            with contextlib.ExitStack() as st:
                P = Phase(ctx, "A2")
                W = SB(st, "W_a2", [128, NCH, 1536], BF16)
                Wo = SB(st, "Wo", [128, NCH, D], BF16)
                gt1 = SB(st, "gt1", [128, 2, D], F32)
                kvi = ring(st, "kvi", 8, [128, 1024], BF16)
                xt = ring(st, "xt", 3, [128, D], F32)
                tabt = ring(st, "tabt", 6, [128, 276], F32)
                xn = ring(st, "xn", 2, [128, D], BF16)
                hT = ring(st, "hT", 2, [128, NCH, 128], BF16)
                tmpr = [ring(st, "tmp%d_" % i, 2, [128, 4, 128], F32) for i in range(2)]
                tmpq = [ring(st, "tmq%d_" % i, 2, [128, 4, 128], F32) for i in range(2)]
                kr = ring(st, "kr", 2, [128, 4, 128], F32)
                qr = ring(st, "qr", 2, [128, 4, 128], F32)
                tm = ring(st, "tm", 2, [128, 16, 128], BF16)
                TQ = ring(st, "TQ", 3, [128, 16, 128], BF16)
                sT = ring(st, "sT", 2, [128, 4, 128], BF16)
                Sfb = ring(st, "Sfb", 3, [128, 4, 128], BF16)
                snp = ring(st, "snp", 3, [128, 4, 128], BF16)
                sg = ring(st, "sg", 4, [128, 512], F32)
                t1 = SB(st, "t1", [128, 512], F32)
                rb = ring(st, "rb", 2, [128, 512], BF16)
                rT = ring(st, "rT", 2, [128, 4, 128], BF16)
                fT = ring(st, "fT", 2, [128, 4, 128], BF16)
                ty = SB(st, "ty", [128, D], F32)
                xr = ring(st, "xr", 2, [128, D], F32)
                B = [PS(st, "B%d" % i) for i in range(8)]

                P.dma(POOL, lambda e: e.dma_start(out=W[:, :, 0:1024], in_=w_in[:, 512:1536].rearrange("(k p) n -> p k n", p=128)),
                      writes=["W"], slot="W0")
                P.dma(POOL, lambda e: e.dma_start(out=W[:, :, 1024:1536], in_=w_in[:, 2048:2560].rearrange("(k p) n -> p k n", p=128)),
                      writes=["W"], slot="W1")
                P.dma(POOL, lambda e: e.dma_start(out=Wo[:], in_=w_out.rearrange("(k p) n -> p k n", p=128)), writes=["Wo"], slot="Wo")
                for r in range(2):
                    P.dma(SP, lambda e, r=r: e.dma_start(out=gt1[:, r, :], in_=mod_d[r, 2 * D:3 * D].partition_broadcast(128)),
                          reads=["mod_d"], writes=["gt1"], slot="gt1")
                kvn = ring(st, "kvn", 8, [128, 1024], BF16)
                stages = []
                ocn = 0
                ncn = 0
                for s in ("p", "s"):
                    sq = seqs[s]
                    r = sq["row"]
                    own_pos = {v: m for m, v in enumerate(sq["own"])}
                    items = []
                    pend = []
                    for v in range(sq["Tv"]):
                        if v in own_pos:
                            items.append((pend, v))
                            pend = []
                        else:
                            pend.append(v)
                    if pend:
                        items.append((pend, None))
                    for (others, v) in items:
                        own = v is not None
                        m = own_pos.get(v, -1)
                        oc = ocn
                        if own:
                            ocn += 1
                        i = oc
                        first = (0 in others) or (v == 0)
                        nslots = []
                        for _ in others:
                            nslots.append(ncn % 8)
                            ncn += 1

                        def sl(s=s, v=v, oc=oc):
                            x_t, xk = xt[oc % 3]
                            tb_t, tbk = tabt[oc % 6]
                            P.dma(SP, lambda e: e.dma_start(out=x_t[:], in_=X[s][v * 128:(v + 1) * 128, :]), writes=[xk], slot=xk)
                            P.dma(SP, lambda e: e.dma_start(out=tb_t[:], in_=TAB[s][v * 128:(v + 1) * 128, :]), writes=[tbk], slot=tbk)

                        def s0(oc=oc):
                            x_t, xk = xt[oc % 3]
                            norm_a(P, x_t[:], xk, xn[oc % 2][0][:], xn[oc % 2][1], oc % 4)

                        def s1(oc=oc, r=r):
                            h_t, hk = hT[oc % 2]
                            trans_b(P, xn[oc % 2][0], xn[oc % 2][1], [(B[0], "B0", ACT)], r, 0, lambda c: h_t[:, c, :], lambda c: "%s_%d" % (hk, c))

                        def s2(s=s, v=v, own=own, oc=oc, others=others, nslots=nslots):
                            for ov, ns in zip(others, nslots):
                                kv_t, kvk = kvn[ns]
                                P.dma(SP, lambda e, ov=ov, kv_t=kv_t: e.dma_start(out=kv_t[:], in_=KV[s][ov * 128:(ov + 1) * 128, :]),
                                      reads=["KV_" + s], writes=[kvk], slot=kvk)
                            if not own:
                                return
                            kv_t, kvk = kvi[oc % 8]
                            P.dma(SP, lambda e: e.dma_start(out=kv_t[:], in_=KV[s][v * 128:(v + 1) * 128, :]),
                                  reads=["KV_" + s], writes=[kvk], slot=kvk)
                            h_t, hk = hT[oc % 2]
                            tb_t, tbk = tabt[oc % 6]
                            for (bi, c0) in ((4, 512), (6, 0), (7, 1024)):
                                for k in range(NCH):
                                    P.op(PE, lambda e, bi=bi, c0=c0, k=k: e.matmul(B[bi][:, :], lhsT=h_t[:, k, :], rhs=W[:, k, c0:c0 + 512],
                                                                                  start=(k == 0), stop=(k == NCH - 1)),
                                         reads=["%s_%d" % (hk, c_) for c_ in range(NCH)] + ["W"], writes=["B%d" % bi])
                            kr_t, krk = kr[oc % 2]
                            rotary(P, B[4], "B4", tb_t, tbk, kr_t, krk, tmpr[oc % 2])
                            qr_t, qrk = qr[oc % 2]
                            sg_t, sgk = sg[oc % 4]
                            tm_t, tmk = tm[oc % 2]
                            rotary(P, B[6], "B6", tb_t, tbk, qr_t, qrk, tmpq[oc % 2])
                            P.op(ACT, lambda e: e.activation(out=sg_t[:], in_=B[7][:, :], func=AF.Silu), reads=["B7"], writes=[sgk])
                            P.op(POOL, lambda e: e.tensor_copy(out=tm_t[:, 0:4, :], in_=qr_t[:]), reads=[qrk], writes=[tmk + "_0"])
                            P.op(POOL, lambda e: e.tensor_tensor(out=tm_t[:, 4:8, :], in0=qr_t[:],
                                                                 in1=tb_t[:, 256:260].unsqueeze(2).broadcast_to([128, 4, 128]), op=ALU.mult),
                                 reads=[qrk, tbk], writes=[tmk + "_1"])
                            P.op(POOL, lambda e: e.tensor_tensor(out=tm_t[:, 8:12, :], in0=qr_t[:],
                                                                 in1=tb_t[:, 260:264].unsqueeze(2).broadcast_to([128, 4, 128]), op=ALU.mult),
                                 reads=[qrk, tbk], writes=[tmk + "_2"])
                            P.op(POOL, lambda e: e.tensor_copy(out=tm_t[:, 12:16, :], in_=kr_t[:]), reads=[krk], writes=[tmk + "_3"])

                        def s3(s=s, own=own, oc=oc, m=m, first=first, others=others, nslots=nslots):
                            if first:
                                P.op(POOL, lambda e: e.memset(S8[:, 0:4, :], 0.0), writes=["S8f"])
                            seq_t = [kvn[ns] for ns in nslots] + ([kvi[oc % 8]] if own else [])
                            for ti_, (kv_t, kvk) in enumerate(seq_t):
                                is_own = own and ti_ == len(seq_t) - 1
                                if is_own:
                                    sf_t, sfk = Sfb[oc % 3]
                                    P.op(ACT, lambda e, sf_t=sf_t: e.activation(out=sf_t[:], in_=S8[:, 0:4, :], func=AF.Copy), reads=["S8f"], writes=[sfk])
                                for h in range(4):
                                    P.op(PE, lambda e, h=h, kv_t=kv_t: e.matmul(B[5][:, h * 128:(h + 1) * 128], lhsT=kv_t[:, h * 128:(h + 1) * 128],
                                                                             rhs=kv_t[:, 512 + h * 128:512 + (h + 1) * 128], start=True, stop=True),
                                         reads=[kvk, "B5"], writes=["B5"])
                                state_update(P, S8, 0, B[5], "B5", "S8f")
                            if own:
                                tm_t, tmk = tm[oc % 2]
                                tq_t, tqk = TQ[oc % 3]
                                sp_t, spk = snp[oc % 3]
                                P.dma(SP, lambda e: e.dma_start(out=sp_t[:].rearrange("p h d -> p (h d)"), in_=SNAP[s][m * 128:(m + 1) * 128, :]),
                                      reads=["SNAP_" + s], writes=[spk], slot=spk)
                                for half in range(2):
                                    bv = B[2 + half][:, :].bitcast(BF16)
                                    for j in range(8):
                                        P.op(PE, lambda e, half=half, j=j, bv=bv: e.transpose(out=bv[:, j * 128:(j + 1) * 128],
                                                                                            in_=tm_t[:, half * 8 + j, :], identity=ident[:]),
                                             reads=["%s_%d" % (tmk, (half * 8 + j) // 4), "ident"], writes=["B%d" % (2 + half)])
                                P.op(ACT, lambda e: e.activation(out=tq_t[:, 0:8, :], in_=B[2][:, :].bitcast(BF16).rearrange("p (a b) -> p a b", a=8),
                                                                 func=AF.Copy), reads=["B2"], writes=[tqk + "_a"])
                                P.op(DVE, lambda e: e.tensor_copy(out=tq_t[:, 8:16, :], in_=B[3][:, :].bitcast(BF16).rearrange("p (a b) -> p a b", a=8)),
                                     reads=["B3"], writes=[tqk + "_b"])

                        def s4(oc=oc):
                            tq_t, tqk = TQ[oc % 3]
                            st_t, stk = sT[oc % 2]
                            for h in range(4):
                                P.op(PE, lambda e, h=h: e.matmul(B[1][:, h * 128:(h + 1) * 128], lhsT=tq_t[:, 12 + h, :], rhs=tq_t[:, h, :],
                                                                 start=True, stop=True), reads=[tqk + "_a", tqk + "_b"], writes=["B1"])
                            P.op(DVE, lambda e: e.tensor_tensor(out=st_t[:], in0=B[1][:, :].rearrange("p (h i) -> p h i", h=4), in1=maskT[:],
                                                                op=ALU.mult), reads=["B1", "maskT"], writes=[stk])

                        def s5(i=i, oc=oc):
                            tq_t, tqk = TQ[oc % 3]
                            st_t, stk = sT[oc % 2]
                            sf_t, sfk = Sfb[oc % 3]
                            sp_t, spk = snp[oc % 3]
                            kv_t, kvk = kvi[i % 8]
                            sg_t, sgk = sg[oc % 4]
                            rb_t, rbk = rb[oc % 2]
                            for h in range(4):
                                o = B[6][:, h * 128:(h + 1) * 128]
                                P.op(PE, lambda e, h=h, o=o: e.matmul(o, lhsT=st_t[:, h, :], rhs=kv_t[:, 512 + h * 128:512 + (h + 1) * 128],
                                                                      start=True, stop=False), reads=[stk, kvk], writes=["B6"])
                                P.op(PE, lambda e, h=h, o=o: e.matmul(o, lhsT=tq_t[:, 4 + h, :], rhs=sf_t[:, h, :], start=False, stop=False),
                                     reads=[tqk + "_a", sfk], writes=["B6"])
                                P.op(PE, lambda e, h=h, o=o: e.matmul(o, lhsT=tq_t[:, 8 + h, :], rhs=sp_t[:, h, :], start=False, stop=True),
                                     reads=[tqk + "_b", spk], writes=["B6"])
                            for h in range(4):
                                P.op(ACT, lambda e, h=h: e.activation(out=junk[:, 0:128], in_=B[6][:, h * 128:(h + 1) * 128], func=AF.Square,
                                                                      accum_out=stat[:, 0, 4 + h:5 + h]), reads=["B6"], writes=["ssg%d" % h])
                            P.op(DVE, lambda e: e.tensor_scalar(out=stat[:, 1, 4:8], in0=stat[:, 0, 4:8], scalar1=1.0 / 128, scalar2=EPS,
                                                                op0=ALU.mult, op1=ALU.add), reads=["ssg0", "ssg1", "ssg2", "ssg3"], writes=["msg"])
                            P.op(POOL, lambda e: e.tensor_tensor(out=stat[:, 2, 4:8], in0=stat[:, 1, 4:8], in1=mh[:, 0:1].broadcast_to([128, 4]),
                                                                 op=ALU.pow), reads=["msg", "mh"], writes=["rsg"])
                            P.op(DVE, lambda e: e.tensor_tensor(out=t1[:].rearrange("p (h d) -> p h d", h=4),
                                                                in0=B[6][:, :].rearrange("p (h d) -> p h d", h=4),
                                                                in1=stat[:, 2, 4:8].unsqueeze(2).broadcast_to([128, 4, 128]), op=ALU.mult),
                                 reads=["B6", "rsg"], writes=["t1"])
                            P.op(DVE, lambda e: e.tensor_tensor(out=rb_t[:], in0=t1[:], in1=sg_t[:], op=ALU.mult), reads=["t1", sgk], writes=[rbk])

                        def s6(s=s, v=v, oc=oc, m=m):
                            rb_t, rbk = rb[oc % 2]
                            rT_t, rTk = rT[oc % 2]
                            f_t, fk = fT[oc % 2]
                            xr_t, xrk = xr[oc % 2]
                            b1v = B[1][:, :].bitcast(BF16)
                            for h in range(4):
                                P.op(PE, lambda e, h=h: e.transpose(out=b1v[:, h * 128:(h + 1) * 128], in_=rb_t[:, h * 128:(h + 1) * 128],
                                                                    identity=ident[:]), reads=[rbk, "ident"], writes=["B1"])
                            P.op(ACT, lambda e: e.activation(out=rT_t[:], in_=b1v[:, 0:512].rearrange("p (h d) -> p h d", h=4), func=AF.Copy),
                                 reads=["B1"], writes=[rTk])
                            P.dma(SP, lambda e: e.dma_start(out=f_t[:], in_=MIX[s][:, m * 128:(m + 1) * 128].rearrange("(g d) t -> d g t", d=128)),
                                  reads=["MIX_" + s], writes=[fk], slot=fk)
                            P.dma(SP, lambda e: e.dma_start(out=xr_t[:], in_=X[s][v * 128:(v + 1) * 128, :]), writes=[xrk], slot=xrk)

                        def s7(s=s, r=r, oc=oc, m=m):
                            rT_t, rTk = rT[oc % 2]
                            f_t, fk = fT[oc % 2]
                            xr_t, xrk = xr[oc % 2]
                            for half in range(2):
                                for kc in range(NCH):
                                    lhs = f_t[:, kc, :] if kc < 4 else rT_t[:, kc - 4, :]
                                    P.op(PE, lambda e, half=half, kc=kc, lhs=lhs: e.matmul(B[2 + half][:, :], lhsT=lhs,
                                                                                         rhs=Wo[:, kc, half * 512:(half + 1) * 512],
                                                                                         start=(kc == 0), stop=(kc == NCH - 1)),
                                         reads=[fk, rTk, "Wo"], writes=["B%d" % (2 + half)])
                            for half in range(2):
                                P.op(DVE, lambda e, half=half: e.tensor_tensor(out=ty[:, half * 512:(half + 1) * 512], in0=B[2 + half][:, :],
                                                                              in1=gt1[:, r, half * 512:(half + 1) * 512], op=ALU.mult),
                                     reads=["B%d" % (2 + half), "gt1"], writes=["ty_%d" % half])
                            P.op(POOL, lambda e: e.tensor_tensor(out=xr_t[:], in0=ty[:], in1=xr_t[:], op=ALU.add), reads=["ty_0", "ty_1", xrk], writes=[xrk])
                            P.dma(SP, lambda e: e.dma_start(out=X1[s][m * 128:(m + 1) * 128, :], in_=xr_t[:]),
                                  reads=[xrk], writes=["X1_" + s], slot="st_" + xrk)

                        stages.append([sl, s0, s1, s2, s3, s4, s5, s6, s7] if own else [None, None, None, s2, s3])
                run_pipeline(stages, 9, order=[0, 1, 2, 3, 4, 5, 6, 7, 8])
                print("A2", P.emit())

        with contextlib.ExitStack() as st:
            P = Phase(ctx, "B")
            W1 = SB(st, "W1", [128, NCH, DFF], BF16)
            W2 = SB(st, "W2", [128, 32, D], BF16)
            modfm_b = modfm
            stat_b = SB(st, "stat", [128, 3, 8], F32)
            junk_b = SB(st, "junk", [128, D], BF16)
            gt2 = SB(st, "gt2", [128, D], F32)
            nfb = SB(st, "nfb", [128, D], F32)
            x1a = ring(st, "x1a", 2, [128, D], F32)
            x1c = ring(st, "x1c", 2, [128, D], F32)
            xnb = ring(st, "xnb", 4, [128, D], BF16)
            h2T = ring(st, "h2T", 2, [128, NCH, 256], BF16)
            aT = SB(st, "aT", [128, 32, 256], BF16)
            rl = ring(st, "rl", 2, [128, 256], F32)
            ty = SB(st, "ty", [128, D], F32)
            yo = ring(st, "yo", 2, [128, D], F32)
            B = [PS(st, "B%d" % i) for i in range(8)]

            for q4 in range(4):
                P.dma(POOL, lambda e, q4=q4: e.dma_start(out=W1[:, :, q4 * 1024:(q4 + 1) * 1024],
                                                        in_=w_mlp_in[:, q4 * 1024:(q4 + 1) * 1024].rearrange("(k p) n -> p k n", p=128)),
                      writes=["W1_%d" % q4], slot="W1_%d" % q4)
            for q4 in range(4):
                P.dma(POOL, lambda e, q4=q4: e.dma_start(out=W2[:, q4 * 8:(q4 + 1) * 8, :],
                                                        in_=w_mlp_out[q4 * 1024:(q4 + 1) * 1024, :].rearrange("(k p) n -> p k n", p=128)),
                      writes=["W2_%d" % q4], slot="W2_%d" % q4)
            P.dma(SP, lambda e: e.dma_start(out=nfb[:], in_=nf_bc[:, :]), writes=["nfb"], slot="nfb")
            bcn = 0
            ycn = 0
            for s in ("p", "s"):
                sq = seqs[s]
                r = sq["row"]
                ntile = len(sq["own"])
                P.dma(SP, lambda e, r=r: e.dma_start(out=gt2[:], in_=mod_d[r, 5 * D:6 * D].partition_broadcast(128)),
                      reads=["mod_d"], writes=["gt2"], slot="gt2")
                stages = []
                for blk in range(0, ntile, 2):
                    tiles = list(range(blk, min(blk + 2, ntile)))
                    bi_ = bcn
                    bcn += 1

                    def s0(tiles=tiles, s=s, bi_=bi_):
                        for ti, m in enumerate(tiles):
                            x_t, xk = x1a[ti]
                            xb_ = xnb[2 * (bi_ % 2) + ti]
                            P.dma(SP, lambda e, x_t=x_t, m=m: e.dma_start(out=x_t[:], in_=X1[s][m * 128:(m + 1) * 128, :]),
                                  reads=["X1_" + s], writes=[xk], slot=xk)
                            norm_a(P, x_t[:], xk, xb_[0][:], xb_[1], ti, stat=stat_b, junk=junk_b)

                    def s1(tiles=tiles, bi_=bi_, r=r):
                        h_t, hk = h2T[bi_ % 2]
                        for ti, m in enumerate(tiles):
                            xb_ = xnb[2 * (bi_ % 2) + ti]
                            trans_b(P, xb_[0], xb_[1], [(B[ti], "B%d" % ti, ACT if ti == 0 else DVE)], r, 2,
                                    lambda c, ti=ti: h_t[:, c, ti * 128:(ti + 1) * 128], lambda c, ti=ti: "%s_%d_%d" % (hk, c, ti), modfm=modfm_b)

                    def s2(tiles=tiles, bi_=bi_):
                        h_t, hk = h2T[bi_ % 2]
                        NW = len(tiles) * 128
                        for f in range(32):
                            bi = 2 + (f % 4)
                            bk = "B%d" % bi
                            rl_t, rlk = rl[f % 2]
                            for k in range(NCH):
                                P.op(PE, lambda e, f=f, k=k, bi=bi: e.matmul(B[bi][:, 0:NW], lhsT=W1[:, k, f * 128:(f + 1) * 128],
                                                                            rhs=h_t[:, k, 0:NW], start=(k == 0), stop=(k == NCH - 1)),
                                     reads=["W1_%d" % (f // 8)] + ["%s_%d_%d" % (hk, c_, ti) for ti in range(len(tiles)) for c_ in range(NCH)], writes=[bk])
                            P.op(ACT, lambda e, bi=bi, rl_t=rl_t: e.activation(out=rl_t[:, 0:NW], in_=B[bi][:, 0:NW], func=AF.Relu),
                                 reads=[bk], writes=[rlk])
                            P.op(DVE, lambda e, f=f, bi=bi, rl_t=rl_t: e.tensor_tensor(out=aT[:, f, 0:NW], in0=B[bi][:, 0:NW], in1=rl_t[:, 0:NW],
                                                                                     op=ALU.mult), reads=[bk, rlk], writes=["aT_%d" % f])

                    def s3(tiles=tiles, s=s):
                        nonlocal ycn
                        for ti, m in enumerate(tiles):
                            x_t, xk = x1c[ti]
                            P.dma(SP, lambda e, x_t=x_t, m=m: e.dma_start(out=x_t[:], in_=X1[s][m * 128:(m + 1) * 128, :]),
                                  reads=["X1_" + s], writes=[xk], slot=xk)
                            for half in range(2):
                                bi = 6 + half
                                for f in range(32):
                                    P.op(PE, lambda e, f=f, ti=ti, half=half, bi=bi: e.matmul(B[bi][:, :], lhsT=aT[:, f, ti * 128:(ti + 1) * 128],
                                                                                             rhs=W2[:, f, half * 512:(half + 1) * 512],
                                                                                             start=(f == 0), stop=(f == 31)),
                                         reads=["aT_%d" % f, "W2_%d" % (f // 8)], writes=["B%d" % bi])
                                P.op(DVE, lambda e, half=half, bi=bi: e.tensor_tensor(out=ty[:, half * 512:(half + 1) * 512], in0=B[bi][:, :],
                                                                                     in1=gt2[:, half * 512:(half + 1) * 512], op=ALU.mult),
                                     reads=["B%d" % bi, "gt2"], writes=["ty_%d" % half])
                            P.op(DVE, lambda e, x_t=x_t: e.tensor_tensor(out=ty[:], in0=ty[:], in1=x_t[:], op=ALU.add),
                                 reads=["ty_0", "ty_1", xk], writes=["ty_0", "ty_1"])
                            y_t, yk = yo[ycn % 2]
                            ycn += 1
                            col = 2 + ti
                            ssc, msc, rsc = stat_b[:, 0, col:col + 1], stat_b[:, 1, col:col + 1], stat_b[:, 2, col:col + 1]
                            P.op(ACT, lambda e, ssc=ssc: e.activation(out=junk_b[:], in_=ty[:], func=AF.Square, accum_out=ssc),
                                 reads=["ty_0", "ty_1"], writes=["ss%d" % col])
                            P.op(DVE, lambda e, ssc=ssc, msc=msc: e.tensor_scalar(out=msc, in0=ssc, scalar1=1.0 / D, scalar2=EPS,
                                                                                 op0=ALU.mult, op1=ALU.add),
                                 reads=["ss%d" % col], writes=["ms%d" % col])
                            P.op(POOL, lambda e, msc=msc, rsc=rsc: e.tensor_tensor(out=rsc, in0=msc, in1=mh[:, 0:1], op=ALU.pow),
                                 reads=["ms%d" % col, "mh"], writes=["rs%d" % col])
                            P.op(DVE, lambda e, y_t=y_t, rsc=rsc: e.scalar_tensor_tensor(out=y_t[:], in0=ty[:], scalar=rsc, in1=nfb[:],
                                                                                        op0=ALU.mult, op1=ALU.mult),
                                 reads=["ty_0", "ty_1", "rs%d" % col, "nfb"], writes=[yk])
                            P.dma(SP, lambda e, y_t=y_t, m=m: e.dma_start(out=Y[s][m * 128:(m + 1) * 128, :], in_=y_t[:]),
                                  reads=[yk], writes=["Y_" + s], slot="st_" + yk)

                    stages.append([s0, s1, s2, s3])
                run_pipeline(stages, 4, order=[0, 1, 3, 2])
            print("B", P.emit())
    return nc


def make_in_maps(inputs, Tp, Ts):
    f = lambda a: np.ascontiguousarray(np.asarray(a, dtype=np.float32))
    xp, xs = f(inputs["x_prompt"]), f(inputs["x_sample"])
    cp, cs = f(inputs["c_prompt"]), f(inputs["c_sample"])
    NOWN = Tp // 4
    Tvp = Tp + 3
    maskT, g8 = _mask_tables()
    idx = np.arange(128)
    cs128 = np.concatenate([np.cos(2 * np.pi * np.outer(idx, idx) / 128), np.sin(2 * np.pi * np.outer(idx, idx) / 128)], axis=1)
    cs128 = (cs128 / np.sqrt(128.0)).astype(np.float32)
    nm_fm = np.stack([f(inputs["norm_mix"])[0].reshape(NCH, 128).T, f(inputs["norm_mlp"])[0].reshape(NCH, 128).T], axis=1)
    nf_bc = np.ascontiguousarray(np.broadcast_to(f(inputs["norm_final"])[None, :], (128, D)))
    ada_b2 = np.ascontiguousarray(np.broadcast_to(f(inputs["ada_b"])[0][None, :], (2, 6 * D)))
    tab_s = _tile_tables([t * 128 for t in range(Ts)], [1.0] * Ts)
    f1_s, c2_s = _fft_tables(Ts, list(range(Ts)), list(range(128)))
    shared = dict(
        ada_w=f(inputs["ada_w"])[0], ada_b2=ada_b2, nm_fm=np.ascontiguousarray(nm_fm), nf_bc=nf_bc,
        w_in=f(inputs["w_in"])[0], w_fnet=f(inputs["w_fnet"])[0], w_out=f(inputs["w_out"])[0],
        w_mlp_in=f(inputs["w_mlp_in"])[0], w_mlp_out=f(inputs["w_mlp_out"])[0],
        ident=np.eye(128, dtype=np.float32), cs128=cs128, maskT=maskT, g8=g8,
        tab_s=tab_s, f1_s=f1_s, c2_s=c2_s,
    )
    per_j = {}
    for j in range(4):
        pad = 3 - j
        vmap = [-1] * pad + list(range(Tp)) + [-1] * j
        tab_p = _tile_tables([max(t, 0) * 128 for t in vmap], [1.0 if t >= 0 else 0.0 for t in vmap])
        kpt = 128 // Tp
        f1_p, c2_p = _fft_tables(Tp, vmap, [(4 * m + j) * kpt + i for m in range(NOWN) for i in range(kpt)])
        per_j[j] = dict(tab_p=tab_p, f1_p=f1_p, c2_p=c2_p)
    xv_cache = {}
    in_maps = []
    for core in range(8):
        b, j = core // 4, core % 4
        if (b, j) not in xv_cache:
            xv = np.zeros((Tvp * 128, D), np.float32)
            xv[(3 - j) * 128:(3 - j) * 128 + Tp * 128] = xp[b]
            xv_cache[(b, j)] = xv
        c2T = np.stack([cp[b].reshape(NCH, 128).T, cs[core].reshape(NCH, 128).T], axis=2)
        m = dict(shared)
        m.update(per_j[j])
        m.update(xv=xv_cache[(b, j)], xs=xs[core], c2T=np.ascontiguousarray(c2T))
        in_maps.append(m)
    return in_maps


_NC_CACHE = {}


def run(inputs, Tp, Ts):
    key = (Tp, Ts)
    if key not in _NC_CACHE:
        _NC_CACHE[key] = build(Tp, Ts)
    nc = _NC_CACHE[key]
    in_maps = make_in_maps(inputs, Tp, Ts)
    res = run_bass_kernel_spmd(nc, in_maps, core_ids=list(range(8)))
    NOWN = Tp // 4
    B = inputs["x_prompt"].shape[0]
    yp = np.zeros((B, Tp * 128, D), np.float32)
    ys = np.zeros((8, Ts * 128, D), np.float32)
    for core in range(8):
        b, j = core // 4, core % 4
        o = np.asarray(res.results[core]["yp"]).reshape(NOWN, 128, D)
        yp[b].reshape(Tp, 128, D)[j::4] = o
        ys[core] = np.asarray(res.results[core]["ys"])
    return yp, ys


def kernel(x_prompt, x_sample, c_prompt, c_sample, ada_w, ada_b, norm_mix, w_in, w_fnet, w_out,
           norm_mlp, w_mlp_in, w_mlp_out, norm_final):
    inputs = dict(x_prompt=x_prompt, x_sample=x_sample, c_prompt=c_prompt, c_sample=c_sample, ada_w=ada_w,
                  ada_b=ada_b, norm_mix=norm_mix, w_in=w_in, w_fnet=w_fnet, w_out=w_out, norm_mlp=norm_mlp,
                  w_mlp_in=w_mlp_in, w_mlp_out=w_mlp_out, norm_final=norm_final)
    Tp = np.asarray(x_prompt).shape[1] // 128
    Ts = np.asarray(x_sample).shape[1] // 128
    yp, ys = run(inputs, Tp, Ts)
    return (yp, ys)
```

```python
import contextlib
import os as _os
import re as _re
import numpy as np
import concourse.bass as bass
import concourse.mybir as mybir
from concourse.bass_utils import run_bass_kernel_spmd

F32 = mybir.dt.float32
BF16 = mybir.dt.bfloat16
AF = mybir.ActivationFunctionType
ALU = mybir.AluOpType

D = 1024
NCH = 8
DFF = 4096
EPS = 1e-6
_COARSE = bool(_os.environ.get("K_COARSE"))
PE, ACT, DVE, POOL, SP = "pe", "act", "dve", "pool", "sp"
COMPUTE = (PE, ACT, DVE, POOL)


class Op:
    __slots__ = ("eng", "fn", "deps", "is_dma", "slot", "needs_inc", "tok")


class Ctx:
    def __init__(self, nc, stack, n_dma_sems=80):
        self.nc = nc
        self.eng_sem = {e: stack.enter_context(nc.semaphore("sem_" + e)) for e in COMPUTE}
        self.eng_cnt = {e: 0 for e in COMPUTE}
        self.dma_sems = [stack.enter_context(nc.semaphore("semd%d" % i)) for i in range(n_dma_sems)]
        self.dma_cnt = [0] * n_dma_sems
        self.slot_map = {}
        self.last_writer = {}
        self.readers = {}
        self.waited = {e: {} for e in (PE, ACT, DVE, POOL, SP)}

    def slot_id(self, key):
        if key not in self.slot_map:
            assert len(self.slot_map) < len(self.dma_sems), "out of dma sems"
            self.slot_map[key] = len(self.slot_map)
        return self.slot_map[key]


class Phase:
    def __init__(self, ctx, name):
        self.ctx = ctx
        self.name = name
        self.ops = []

    def _add(self, eng, fn, reads, writes, is_dma=False, slot=None):
        c = self.ctx
        if _COARSE:
            pat = _os.environ.get("K_COARSE")
            reads = [_re.sub(pat, "", k) for k in reads]
            writes = [_re.sub(pat, "", k) for k in writes]
        op = Op()
        op.eng, op.fn, op.is_dma, op.slot = eng, fn, is_dma, slot
        op.needs_inc = is_dma
        op.tok = None
        deps = []
        for b in reads:
            w = c.last_writer.get(b)
            if w is not None:
                deps.append(w)
        for b in writes:
            w = c.last_writer.get(b)
            if w is not None:
                deps.append(w)
            last = {}
            for rd in c.readers.get(b, ()):
                if rd.is_dma:
                    deps.append(rd)
                else:
                    last[rd.eng] = rd
            deps.extend(last.values())
        op.deps = deps
        for b in reads:
            c.readers.setdefault(b, []).append(op)
        for b in writes:
            c.last_writer[b] = op
            c.readers[b] = []
        self.ops.append(op)
        return op

    def op(self, eng, fn, reads=(), writes=()):
        return self._add(eng, fn, reads, writes)

    def dma(self, queue, fn, reads=(), writes=(), slot=None):
        return self._add(queue, fn, reads, writes, is_dma=True, slot=self.ctx.slot_id(slot))

    def emit(self):
        c = self.ctx
        nc = c.nc
        for op in self.ops:
            for d in op.deps:
                if d.is_dma:
                    continue
                if d.eng == PE and op.eng == PE and not op.is_dma:
                    continue
                d.needs_inc = True
        for op in self.ops:
            if op.is_dma:
                c.dma_cnt[op.slot] += 16
                op.tok = (c.dma_sems[op.slot], c.dma_cnt[op.slot], ("d", op.slot))
            elif op.needs_inc:
                c.eng_cnt[op.eng] += 1
                op.tok = (c.eng_sem[op.eng], c.eng_cnt[op.eng], ("e", op.eng))
        per_eng = {e: [] for e in (PE, ACT, DVE, POOL, SP)}
        last_dma_tok = {}
        for op in self.ops:
            waits = {}
            for d in op.deps:
                if d.tok is None:
                    continue
                if (not d.is_dma) and d.eng == PE and op.eng == PE and not op.is_dma:
                    continue
                sem, val, key = d.tok
                if c.waited[op.eng].get(key, 0) >= val:
                    continue
                if key not in waits or waits[key][1] < val:
                    waits[key] = (sem, val)
            for key, (sem, val) in waits.items():
                c.waited[op.eng][key] = val
            per_eng[op.eng].append((op, list(waits.values())))
            if op.is_dma:
                last_dma_tok[op.tok[2]] = op.tok
        final_waits = []
        for key, (sem, val, _) in last_dma_tok.items():
            if c.waited[SP].get(key, 0) < val:
                c.waited[SP][key] = val
                final_waits.append((sem, val))
        with nc.Block() as block:
            def run(eng_name):
                def body(e):
                    for op, waits in per_eng[eng_name]:
                        for sem, val in waits:
                            e.wait_ge(sem, val)
                        ins = op.fn(e)
                        if op.is_dma:
                            ins.then_inc(op.tok[0], 16)
                        elif op.needs_inc:
                            ins.then_inc(op.tok[0], 1)
                    if eng_name == SP:
                        for sem, val in final_waits:
                            e.wait_ge(sem, val)
                return body
            block.sync(run(SP))
            if per_eng[ACT]:
                block.scalar(run(ACT))
            if per_eng[DVE]:
                block.vector(run(DVE))
            if per_eng[POOL]:
                block.gpsimd(run(POOL))
            if per_eng[PE]:
                block.tensor(run(PE))
        self.ops = []
        return {e: len(v) for e, v in per_eng.items()}


def _gammas():
    h = np.arange(4, dtype=np.float64)
    lf = np.log1p(-np.exp2(-5.0 - 0.0 - h))
    lb = np.log1p(-np.exp2(-5.0 - 0.5 - h))
    return lf, lb


def _tile_tables(pos0_list, valid_list):
    lf, lb = _gammas()
    inv = 10000.0 ** (-(np.arange(64, dtype=np.float64) / 64.0))
    p = np.arange(128, dtype=np.float64)
    out = np.zeros((len(pos0_list), 128, 276), np.float32)
    ks = 128.0 ** -0.5
    for i, (pos0, valid) in enumerate(zip(pos0_list, valid_list)):
        pos = (pos0 + np.arange(128)).astype(np.float64)
        ang = pos[:, None] * inv[None, :]
        out[i, :, 0:64] = np.cos(ang)
        out[i, :, 64:128] = np.cos(ang)
        out[i, :, 128:192] = -np.sin(ang)
        out[i, :, 192:256] = np.sin(ang)
        out[i, :, 256:260] = np.exp(lf[None, :] * (p[:, None] + 1.0))
        out[i, :, 260:264] = np.exp(lb[None, :] * (128.0 - p[:, None]))
        out[i, :, 264:268] = ks * np.exp(lf[None, :] * (127.0 - p[:, None])) * valid
        out[i, :, 268:272] = ks * np.exp(lb[None, :] * p[:, None]) * valid
        out[i, :, 272] = valid
    return out.reshape(len(pos0_list) * 128, 276)


def _mask_tables():
    lf, lb = _gammas()
    j = np.arange(128, dtype=np.float64)[:, None]
    i = np.arange(128, dtype=np.float64)[None, :]
    m = np.zeros((128, 4, 128), np.float32)
    for h in range(4):
        f = np.exp(lf[h] * np.maximum(i - j, 0.0))
        b = np.exp(lb[h] * np.maximum(j - i, 0.0))
        m[:, h, :] = (128.0 ** -0.5) * np.where(j <= i, f, b)
    g8 = np.zeros((128, 8), np.float32)
    g8[:, 0:4] = np.exp(lf * 128.0)[None, :]
    g8[:, 4:8] = np.exp(lb * 128.0)[None, :]
    return m, g8


def _fft_tables(T, vmap, k1_list):
    N = 128 * T
    k2 = np.arange(T, dtype=np.float64)
    f1 = np.zeros((T, 4 * T), np.float32)
    for v_, t in enumerate(vmap):
        if t < 0:
            continue
        v = v_ % T
        a = 2 * np.pi * ((t * k2) % T) / T
        f1[v, 0:T] = np.cos(a)
        f1[v, T:2 * T] = -np.sin(a)
        f1[v, 2 * T:3 * T] = -np.sin(a)
        f1[v, 3 * T:4 * T] = -np.cos(a)
    r = np.arange(128, dtype=np.float64)[:, None, None]
    k1 = np.asarray(k1_list, dtype=np.float64)[None, None, :]
    kk = k2[None, :, None]
    ph = (r * (k1 * T + kk)) % N
    al = 2 * np.pi * ph / N
    sc = 1.0 / np.sqrt(N)
    c2k = np.stack([np.cos(al) * sc, np.sin(al) * sc], axis=2).astype(np.float32)
    return f1, np.ascontiguousarray(c2k)


def build(Tp, Ts):
    NOWN = Tp // 4
    Tvp = Tp + 3
    own_v = [3 + 4 * m for m in range(NOWN)]
    seqs = {
        "p": dict(T=Tp, Tv=Tvp, own=own_v, NO=NOWN * 128 // Tp, row=0),
        "s": dict(T=Ts, Tv=Ts, own=list(range(Ts)), NO=128, row=1),
    }
    nc = bass.Bass("TRN2", target_bir_lowering=False)
    din = lambda n, s, d=F32: nc.dram_tensor(n, list(s), d, kind="ExternalInput").ap()
    dout = lambda n, s, d=F32: nc.dram_tensor(n, list(s), d, kind="ExternalOutput").ap()
    dscr = lambda n, s, d: nc.dram_tensor(n, list(s), d, kind="Internal").ap()

    X = {"p": din("xv", [Tvp * 128, D]), "s": din("xs", [Ts * 128, D])}
    TAB = {"p": din("tab_p", [Tvp * 128, 276]), "s": din("tab_s", [Ts * 128, 276])}
    F1 = {"p": din("f1_p", [Tp, 4 * Tp]), "s": din("f1_s", [Ts, 4 * Ts])}
    C2 = {"p": din("c2_p", [128, Tp, 2, NOWN * 128 // Tp]), "s": din("c2_s", [128, Ts, 2, 128])}
    c2T = din("c2T", [128, NCH, 2])
    ada_w = din("ada_w", [D, 6 * D])
    ada_b2 = din("ada_b2", [2, 6 * D])
    nm_fm = din("nm_fm", [128, 2, NCH])
    nf_bc = din("nf_bc", [128, D])
    w_in = din("w_in", [D, 2560])
    w_fnet = din("w_fnet", [4, 128, 128])
    w_out = din("w_out", [D, D])
    w_mlp_in = din("w_mlp_in", [D, DFF])
    w_mlp_out = din("w_mlp_out", [DFF, D])
    ident_d = din("ident", [128, 128])
    cs_d = din("cs128", [128, 256])
    mask_d = din("maskT", [128, 4, 128])
    g8_d = din("g8", [128, 8])
    Y = {"p": dout("yp", [NOWN * 128, D]), "s": dout("ys", [Ts * 128, D])}

    PQ = {"p": dscr("pq_p", [Tvp * 128, 1024], BF16), "s": dscr("pq_s", [Ts * 128, 1024], BF16)}
    MIX = {"p": dscr("mix_p", [512, NOWN * 128], BF16), "s": dscr("mix_s", [512, Ts * 128], BF16)}
    X1 = {"p": dscr("x1_p", [NOWN * 128, D], F32), "s": dscr("x1_s", [Ts * 128, D], F32)}
    wab_d = dscr("wab", [D, 1024], BF16)
    mod_d = dscr("mod", [2, 6 * D], F32)

    with contextlib.ExitStack() as gs:
        ctx = Ctx(nc, gs)
        uid = [0]

        def SB(st, n, s, d):
            uid[0] += 1
            return st.enter_context(nc.sbuf_tensor("%s_u%d" % (n, uid[0]), list(s), d))

        def PS(st, n):
            uid[0] += 1
            return st.enter_context(nc.psum_tensor("%s_u%d" % (n, uid[0]), [128, 512], F32))
        ident = SB(gs, "ident", [128, 128], BF16)
        mh = SB(gs, "mh", [128, 1], F32)
        epsb = SB(gs, "epsb", [128, 1], F32)
        modfm = SB(gs, "modfm", [128, 2, 4, NCH], F32)
        nmt = SB(gs, "nmt", [128, 2, NCH], F32)

        with contextlib.ExitStack() as st:
            P = Phase(ctx, "setup")
            c2 = SB(st, "c2", [128, NCH, 2], F32)
            scb = SB(st, "scb", [128, NCH, 2], BF16)
            adab = SB(st, "adab", [2, 6 * D], F32)
            modsb = SB(st, "modsb", [2, 6 * D], F32)
            aw = [SB(st, "aw%d" % i, [128, NCH, 512], BF16) for i in range(2)]
            wfb = SB(st, "wfb", [128, 4, 128], BF16)
            csb = SB(st, "csb", [128, 256], BF16)
            wcws = SB(st, "wcws", [128, 4, 256], BF16)
            wub = SB(st, "wub", [128, NCH, 512], BF16)
            wuT = SB(st, "wuT", [128, 4, NCH, 128], BF16)
            wabs = SB(st, "wabs", [128, NCH, 1024], BF16)
            pm = [PS(st, "pm%d" % i) for i in range(2)]
            pw = PS(st, "pw")
            pt = [PS(st, "ptb%d" % i) for i in range(2)]
            pab = [PS(st, "pab%d" % i) for i in range(2)]

            P.dma(POOL, lambda e: e.dma_start(out=ident[:], in_=ident_d[:, :]), writes=["ident"], slot="ident")
            P.op(POOL, lambda e: e.memset(mh[:], -0.5), writes=["mh"])
            P.op(POOL, lambda e: e.memset(epsb[:], EPS), writes=["epsb"])
            P.dma(SP, lambda e: e.dma_start(out=c2[:], in_=c2T[:, :, :]), writes=["c2"], slot="c2")
            P.dma(SP, lambda e: e.dma_start(out=adab[:], in_=ada_b2[:, :]), writes=["adab"], slot="adab")
            P.op(ACT, lambda e: e.activation(out=scb[:], in_=c2[:], func=AF.Silu), reads=["c2"], writes=["scb"])
            for nb in range(12):
                a = aw[nb % 2]
                ak = "aw%d" % (nb % 2)
                P.dma(POOL, lambda e, a=a, nb=nb: e.dma_start(
                    out=a[:], in_=ada_w[:, nb * 512:(nb + 1) * 512].rearrange("(k p) n -> p k n", p=128)),
                    writes=[ak], slot=ak)
                pmb = pm[nb % 2]
                pk = "pm%d" % (nb % 2)
                for k in range(NCH):
                    P.op(PE, lambda e, a=a, k=k, pmb=pmb: e.matmul(pmb[0:2, :], lhsT=scb[:, k, :], rhs=a[:, k, :],
                                                                 start=(k == 0), stop=(k == NCH - 1)),
                         reads=["scb", ak], writes=[pk])
                P.op(DVE, lambda e, nb=nb, pmb=pmb: e.tensor_tensor(out=modsb[:, nb * 512:(nb + 1) * 512], in0=pmb[0:2, :],
                                                                   in1=adab[:, nb * 512:(nb + 1) * 512], op=ALU.add),
                     reads=[pk, "adab"], writes=["modsb"])
            P.dma(SP, lambda e: e.dma_start(out=mod_d[:, :], in_=modsb[:]), reads=["modsb"], writes=["mod_d"], slot="mod_d")
            id2 = SB(st, "id2", [2, 2], F32)
            pmf = PS(st, "pmf")
            P.dma(SP, lambda e: e.dma_start(out=id2[:], in_=ident_d[0:2, 0:2]), writes=["id2"], slot="id2")
            P.dma(SP, lambda e: e.dma_start(out=nmt[:], in_=nm_fm[:, :, :]), writes=["nmt"], slot="nmt")
            cols_ = [1, 0, 4, 3]
            for qi, q in enumerate(cols_):
                for k in range(NCH):
                    j = qi * NCH + k
                    P.op(PE, lambda e, q=q, k=k, j=j: e.transpose(out=pmf[:, j * 2:(j + 1) * 2],
                                                                  in_=modsb[0:2, q * D + k * 128:q * D + (k + 1) * 128], identity=id2[:]),
                         reads=["modsb", "id2"], writes=["pmf"])
            P.op(DVE, lambda e: e.tensor_copy(out=modfm[:], in_=pmf[:, 0:64].rearrange("p (q k r) -> p r q k", q=4, k=NCH, r=2)),
                 reads=["pmf"], writes=["modfm"])
            for r in range(2):
                for (qi, ni) in ((0, 0), (2, 1)):
                    P.op(DVE, lambda e, r=r, qi=qi, ni=ni: e.scalar_tensor_tensor(
                        out=modfm[:, r, qi, :], in0=modfm[:, r, qi, :], scalar=1.0, in1=nmt[:, ni, :],
                        op0=ALU.add, op1=ALU.mult), reads=["modfm", "nmt"], writes=["modfm"])
            P.dma(POOL, lambda e: e.dma_start(out=wfb[:], in_=w_fnet.rearrange("g c d -> c g d")), writes=["wfb"], slot="wfb")
            P.dma(POOL, lambda e: e.dma_start(out=csb[:], in_=cs_d[:, :]), writes=["csb"], slot="csb")
            P.dma(POOL, lambda e: e.dma_start(out=wub[:], in_=w_in[:, 0:512].rearrange("(k p) n -> p k n", p=128)),
                  writes=["wub"], slot="wub")
            for g in range(4):
                for half in range(2):
                    P.op(PE, lambda e, g=g, half=half: e.matmul(pw[:, 0:128], lhsT=csb[:, half * 128:(half + 1) * 128],
                                                              rhs=wfb[:, g, :], start=True, stop=True),
                         reads=["csb", "wfb"], writes=["pw"])
                    P.op(DVE, lambda e, g=g, half=half: e.tensor_copy(out=wcws[:, g, half * 128:(half + 1) * 128], in_=pw[:, 0:128]),
                         reads=["pw"], writes=["wcws"])
            for k in range(NCH):
                ptb = pt[k % 2]
                pk = "ptb%d" % (k % 2)
                ptv = ptb[:, :].bitcast(BF16)
                for g in range(4):
                    P.op(PE, lambda e, k=k, g=g, ptv=ptv: e.transpose(out=ptv[:, g * 128:(g + 1) * 128],
                                                                    in_=wub[:, k, g * 128:(g + 1) * 128], identity=ident[:]),
                         reads=["wub", "ident"], writes=[pk])
                P.op(ACT, lambda e, k=k, ptv=ptv: e.activation(out=wuT[:, :, k, :],
                                                              in_=ptv[:, 0:512].rearrange("p (g d) -> p g d", g=4), func=AF.Copy),
                     reads=[pk], writes=["wuT"])
            for k in range(NCH):
                for gp in range(2):
                    pb = pab[(k * 2 + gp) % 2]
                    pk = "pab%d" % ((k * 2 + gp) % 2)
                    for gi in range(2):
                        g = gp * 2 + gi
                        P.op(PE, lambda e, k=k, g=g, gi=gi, pb=pb: e.matmul(pb[:, gi * 256:(gi + 1) * 256], lhsT=wuT[:, g, k, :],
                                                                           rhs=wcws[:, g, :], start=True, stop=True),
                             reads=["wuT", "wcws"], writes=[pk])
                    P.op(DVE if gp == 0 else ACT,
                         (lambda e, k=k, gp=gp, pb=pb: e.tensor_copy(out=wabs[:, k, gp * 512:(gp + 1) * 512], in_=pb[:, :])) if gp == 0 else
                         (lambda e, k=k, gp=gp, pb=pb: e.activation(out=wabs[:, k, gp * 512:(gp + 1) * 512], in_=pb[:, :], func=AF.Copy)),
                         reads=[pk], writes=["wabs"])
            P.dma(SP, lambda e: e.dma_start(out=wab_d.rearrange("(k p) n -> p k n", p=128), in_=wabs[:]),
                  reads=["wabs"], writes=["wab_d"], slot="wab_d")
            print("setup", P.emit())

        def ring(st, name, n, shape, dt):
            return [(SB(st, "%s%d" % (name, i), shape, dt), "%s%d" % (name, i)) for i in range(n)]

        def run_pipeline(stage_lists, nstages, order=None):
            n = len(stage_lists)
            order = order or list(range(nstages - 1, -1, -1))
            for t in range(n + nstages - 1):
                for k in order:
                    i = t - k
                    if 0 <= i < n and k < len(stage_lists[i]) and stage_lists[i][k] is not None:
                        stage_lists[i][k]()

        SNAP = {s: dscr("snapd_" + s, [len(seqs[s]["own"]) * 128, 512], BF16) for s in seqs}
        KV = {s: dscr("kvd_" + s, [seqs[s]["Tv"] * 128, 1024], BF16) for s in seqs}
        _lf, _lb = _gammas()
        G128 = [float(np.exp(_lf[h] * 128.0)) for h in range(4)] + [float(np.exp(_lb[h] * 128.0)) for h in range(4)]

        def state_update(P, S8, half, Bsrc, Bk, skey):
            for h in range(4):
                hh = half * 4 + h
                P.op(DVE, lambda e, h=h, hh=hh: e.scalar_tensor_tensor(out=S8[:, hh, :], in0=S8[:, hh, :], scalar=G128[hh],
                                                                      in1=Bsrc[:, h * 128:(h + 1) * 128], op0=ALU.mult, op1=ALU.add),
                     reads=[skey, Bk], writes=[skey])

        with contextlib.ExitStack() as ms:
            S8 = SB(ms, "S8", [128, 8, 128], F32)
            g8 = SB(ms, "g8", [128, 8], F32)
            maskT = SB(ms, "maskT", [128, 4, 128], F32)
            stat = SB(ms, "stat", [128, 3, 8], F32)
            junk = SB(ms, "junk", [128, D], BF16)

            def load_mod_tables(P, modfm=modfm, nmt=nmt):
                P.dma(SP, lambda e: e.dma_start(out=nmt[:], in_=nm_fm[:, :, :]), writes=["nmt"], slot="nmt")
                cols = [1, 0, 4, 3]
                for r in range(2):
                    for qi, q in enumerate(cols):
                        P.dma(SP, lambda e, r=r, qi=qi, q=q: e.dma_start(
                            out=modfm[:, r, qi, :], in_=mod_d[r, q * D:(q + 1) * D].rearrange("(k p) -> p k", p=128),
                            allow_slow_non_contiguous=True),
                            reads=["mod_d"], writes=["modfm"], slot="modfm")
                for r in range(2):
                    for (qi, ni) in ((0, 0), (2, 1)):
                        P.op(DVE, lambda e, r=r, qi=qi, ni=ni: e.scalar_tensor_tensor(
                            out=modfm[:, r, qi, :], in0=modfm[:, r, qi, :], scalar=1.0, in1=nmt[:, ni, :],
                            op0=ALU.add, op1=ALU.mult), reads=["modfm", "nmt"], writes=["modfm"])

            def norm_a(P, src_ap, src_key, xn, xnk, col, stat=stat, junk=junk, out_dt_scale=None):
                ssc, msc, rsc = stat[:, 0, col:col + 1], stat[:, 1, col:col + 1], stat[:, 2, col:col + 1]
                P.op(ACT, lambda e: e.activation(out=junk[:], in_=src_ap, func=AF.Square, accum_out=ssc),
                     reads=[src_key], writes=["ss%d" % col])
                P.op(DVE, lambda e: e.tensor_scalar(out=msc, in0=ssc, scalar1=1.0 / D, scalar2=EPS, op0=ALU.mult, op1=ALU.add),
                     reads=["ss%d" % col], writes=["ms%d" % col])
                P.op(POOL, lambda e: e.tensor_tensor(out=rsc, in0=msc, in1=mh[:, 0:1], op=ALU.pow),
                     reads=["ms%d" % col, "mh"], writes=["rs%d" % col])
                P.op(ACT, lambda e: e.activation(out=xn, in_=src_ap, func=AF.Identity, scale=rsc),
                     reads=[src_key, "rs%d" % col], writes=[xnk])

            def trans_b(P, xn, xnk, banks, r, qs, dst_fn, dstk, modfm=modfm):
                per = NCH // len(banks)
                for c in range(NCH):
                    Bt, Btk, _ = banks[c // per]
                    bv = Bt[:, :].bitcast(BF16)
                    j = c % per
                    P.op(PE, lambda e, c=c, j=j, bv=bv: e.transpose(out=bv[:, j * 128:(j + 1) * 128], in_=xn[:, c * 128:(c + 1) * 128],
                                                                  identity=ident[:]), reads=[xnk, "ident"], writes=[Btk])
                for c in range(NCH):
                    Bt, Btk, eng = banks[c // per]
                    bv = Bt[:, :].bitcast(BF16)
                    j = c % per
                    if eng == ACT:
                        P.op(ACT, lambda e, c=c, j=j, bv=bv: e.activation(out=dst_fn(c), in_=bv[:, j * 128:(j + 1) * 128], func=AF.Identity,
                                                                        scale=modfm[:, r, qs, c:c + 1], bias=modfm[:, r, qs + 1, c:c + 1]),
                             reads=[Btk, "modfm"], writes=[dstk(c)])
                    else:
                        P.op(DVE, lambda e, c=c, j=j, bv=bv: e.tensor_scalar(out=dst_fn(c), in0=bv[:, j * 128:(j + 1) * 128],
                                                                           scalar1=modfm[:, r, qs, c:c + 1], scalar2=modfm[:, r, qs + 1, c:c + 1],
                                                                           op0=ALU.mult, op1=ALU.add),
                             reads=[Btk, "modfm"], writes=[dstk(c)])

            def rotary(P, Bk, Bkk, tabt, tabk, dst, dstk, tmp):
                (ta, tak), (tb, tbk_) = tmp[0], tmp[1]
                v = Bk[:, :].rearrange("p (h t d) -> p h t d", h=4, t=2)
                cc = tabt[:, 0:128].rearrange("p (t d) -> p t d", t=2).unsqueeze(1).broadcast_to([128, 4, 2, 64])
                sv = tabt[:, 128:256].rearrange("p (t d) -> p t d", t=2).unsqueeze(1).broadcast_to([128, 4, 2, 64])
                P.op(DVE, lambda e: e.tensor_tensor(out=ta[:].rearrange("p h (t d) -> p h t d", t=2), in0=v, in1=cc, op=ALU.mult),
                     reads=[Bkk, tabk], writes=[tak])
                P.op(DVE, lambda e: e.tensor_tensor(out=tb[:].rearrange("p h (t d) -> p h t d", t=2), in0=v[:, :, ::-1, :], in1=sv, op=ALU.mult),
                     reads=[Bkk, tabk], writes=[tbk_])
                P.op(POOL, lambda e: e.tensor_tensor(out=dst[:], in0=ta[:], in1=tb[:], op=ALU.add), reads=[tak, tbk_], writes=[dstk])

            with contextlib.ExitStack() as st:
                P = Phase(ctx, "A1")
                W = SB(st, "W_a1", [128, NCH, 2048], BF16)
                xt = ring(st, "xt", 3, [128, D], F32)
                tabt = ring(st, "tabt", 6, [128, 276], F32)
                xn = ring(st, "xn", 2, [128, D], BF16)
                hT = ring(st, "hT", 2, [128, NCH, 128], BF16)
                pqs = ring(st, "pqs", 2, [128, 1024], BF16)
                kvs = ring(st, "kvs", 2, [128, 1024], BF16)
                tmpr = [ring(st, "tmp%d_" % i, 2, [128, 4, 128], F32) for i in range(2)]
                kr = ring(st, "kr", 2, [128, 4, 128], F32)
                kh = ring(st, "kh", 2, [128, 4, 128], BF16)
                snb = ring(st, "snb", 2, [128, 4, 128], BF16)
                B = [PS(st, "B%d" % i) for i in range(8)]

                P.dma(POOL, lambda e: e.dma_start(out=W[:, :, 0:1024], in_=wab_d.rearrange("(k p) n -> p k n", p=128)),
                      reads=["wab_d"], writes=["W"], slot="W0")
                P.dma(POOL, lambda e: e.dma_start(out=W[:, :, 1024:2048], in_=w_in[:, 1024:2048].rearrange("(k p) n -> p k n", p=128)),
                      writes=["W"], slot="W1")
                P.dma(SP, lambda e: e.dma_start(out=maskT[:], in_=mask_d[:, :, :]), writes=["maskT"], slot="maskT")
                stages = []
                cnt = 0
                for s in ("p", "s"):
                    sq = seqs[s]
                    r = sq["row"]
                    own_pos = {v: m for m, v in enumerate(sq["own"])}
                    for v in range(sq["Tv"] - 1, -1, -1):
                        i = cnt
                        cnt += 1
                        first = (v == sq["Tv"] - 1)

                        def sl(i=i, s=s, v=v):
                            x_t, xk = xt[i % 3]
                            tb_t, tbk = tabt[i % 6]
                            P.dma(SP, lambda e: e.dma_start(out=x_t[:], in_=X[s][v * 128:(v + 1) * 128, :]), writes=[xk], slot=xk)
                            P.dma(SP, lambda e: e.dma_start(out=tb_t[:], in_=TAB[s][v * 128:(v + 1) * 128, :]), writes=[tbk], slot=tbk)

                        def s0(i=i, s=s, v=v):
                            x_t, xk = xt[i % 3]
                            norm_a(P, x_t[:], xk, xn[i % 2][0][:], xn[i % 2][1], i % 8)

                        def s1(i=i, r=r):
                            h_t, hk = hT[i % 2]
                            trans_b(P, xn[i % 2][0], xn[i % 2][1], [(B[0], "B0", ACT), (B[1], "B1", DVE)], r, 0, lambda c: h_t[:, c, :],
                                    lambda c: "%s_%d" % (hk, c))

                        def s2(i=i, s=s, v=v):
                            h_t, hk = hT[i % 2]
                            tb_t, tbk = tabt[i % 6]
                            pq_t, pqk = pqs[i % 2]
                            for half in range(2):
                                for k in range(NCH):
                                    P.op(PE, lambda e, half=half, k=k: e.matmul(B[2 + half][:, :], lhsT=h_t[:, k, :],
                                                                               rhs=W[:, k, half * 512:(half + 1) * 512],
                                                                               start=(k == 0), stop=(k == NCH - 1)),
                                         reads=["%s_%d" % (hk, c_) for c_ in range(NCH)] + ["W"], writes=["B%d" % (2 + half)])
                            P.op(ACT, lambda e: e.activation(out=pq_t[:, 0:512], in_=B[2][:, :], func=AF.Identity, scale=tb_t[:, 272:273]),
                                 reads=["B2", tbk], writes=[pqk + "_a"])
                            P.op(DVE, lambda e: e.tensor_scalar(out=pq_t[:, 512:1024], in0=B[3][:, :], scalar1=tb_t[:, 272:273],
                                                                scalar2=None, op0=ALU.mult), reads=["B3", tbk], writes=[pqk + "_b"])
                            P.dma(SP, lambda e: e.dma_start(out=PQ[s][v * 128:(v + 1) * 128, :], in_=pq_t[:]),
                                  reads=[pqk + "_a", pqk + "_b"], writes=["PQ_" + s], slot="st_" + pqk)
                            bk = 4 if i % 2 == 0 else 7
                            for (bi, c0) in ((bk, 1024), (5, 1536)):
                                for k in range(NCH):
                                    P.op(PE, lambda e, bi=bi, c0=c0, k=k: e.matmul(B[bi][:, :], lhsT=h_t[:, k, :], rhs=W[:, k, c0:c0 + 512],
                                                                                  start=(k == 0), stop=(k == NCH - 1)),
                                         reads=["%s_%d" % (hk, c_) for c_ in range(NCH)] + ["W"], writes=["B%d" % bi])
                            v_t, vk = kvs[i % 2]
                            P.op(ACT, lambda e: e.activation(out=v_t[:, 512:1024], in_=B[5][:, :], func=AF.Copy), reads=["B5"], writes=[vk + "_v"])
                            kr_t, krk = kr[i % 2]
                            rotary(P, B[bk], "B%d" % bk, tb_t, tbk, kr_t, krk, tmpr[i % 2])
                            kh_t, khk = kh[i % 2]
                            P.op(POOL, lambda e: e.tensor_tensor(out=kh_t[:], in0=kr_t[:],
                                                                 in1=tb_t[:, 268:272].unsqueeze(2).broadcast_to([128, 4, 128]), op=ALU.mult),
                                 reads=[krk, tbk], writes=[khk])
                            P.op(POOL, lambda e: e.tensor_tensor(out=v_t[:, 0:512].rearrange("p (h d) -> p h d", h=4), in0=kr_t[:],
                                                                 in1=tb_t[:, 264:268].unsqueeze(2).broadcast_to([128, 4, 128]), op=ALU.mult),
                                 reads=[krk, tbk], writes=[vk + "_k"])
                            P.dma(SP, lambda e: e.dma_start(out=KV[s][v * 128:(v + 1) * 128, :], in_=v_t[:]),
                                  reads=[vk + "_k", vk + "_v"], writes=["KV_" + s], slot="st_" + vk)

                        def s3(i=i, s=s, v=v, first=first, own_pos=own_pos):
                            kh_t, khk = kh[i % 2]
                            v_t, vk = kvs[i % 2]
                            if first:
                                P.op(POOL, lambda e: e.memset(S8[:, 4:8, :], 0.0), writes=["S8b"])
                            for h in range(4):
                                P.op(PE, lambda e, h=h: e.matmul(B[6][:, h * 128:(h + 1) * 128], lhsT=kh_t[:, h, :],
                                                                 rhs=v_t[:, 512 + h * 128:512 + (h + 1) * 128], start=True, stop=True),
                                     reads=[khk, vk + "_v"], writes=["B6"])
                            if v in own_pos:
                                m = own_pos[v]
                                sn_t, snk = snb[m % 2]
                                P.op(ACT, lambda e: e.activation(out=sn_t[:], in_=S8[:, 4:8, :], func=AF.Copy), reads=["S8b"], writes=[snk])
                                P.dma(SP, lambda e: e.dma_start(out=SNAP[s][m * 128:(m + 1) * 128, :], in_=sn_t[:].rearrange("p h d -> p (h d)")),
                                      reads=[snk], writes=["SNAP_" + s], slot="st_" + snk)
                            state_update(P, S8, 1, B[6], "B6", "S8b")

                        stages.append([sl, s0, s1, s2, s3])
                run_pipeline(stages, 5, order=[0, 1, 2, 3, 4])
                print("A1", P.emit())

            for s in ("p", "s"):
                sq = seqs[s]
                T, Tv, NO = sq["T"], sq["Tv"], sq["NO"]
                Ka = T
                Kb = Tv - T
                NB = min(64, 1024 // (2 * T))
                KB2 = max(1, min(T, 512 // NO))
                with contextlib.ExitStack() as st:
                    P = Phase(ctx, "F" + s)
                    Za_r = ring(st, "Za", 2, [Ka, 128, 256], BF16)
                    zbp = ring(st, "zbp", 2, [max(Kb, 1), 4, 256], BF16)
                    f1a = SB(st, "f1a", [Ka, 4 * T], BF16)
                    c2s = SB(st, "c2s", [128, T, 2, NO], BF16)
                    Yp = SB(st, "Yp", [128, 2, 64, T], BF16)
                    f2t = SB(st, "f2t", [64, NO, T], BF16)
                    YB = [st.enter_context(nc.psum_tensor("YB%d_%s" % (i, s), [128, 1024], F32)) for i in range(2)]
                    OB = [PS(st, "OB%d" % i) for i in range(4)]
                    P.dma(POOL, lambda e: e.dma_start(out=f1a[:], in_=F1[s][0:Ka, :]), writes=["f1a"], slot="f1a")
                    cstep = max(1, 1024 // (2 * NO))
                    for t0_ in range(0, T, cstep):
                        t1_ = min(T, t0_ + cstep)
                        P.dma(POOL, lambda e, t0_=t0_, t1_=t1_: e.dma_start(out=c2s[:, t0_:t1_, :, :], in_=C2[s][:, t0_:t1_, :, :]),
                              writes=["c2s"], slot="c2s")

                    def load_group(g):
                        Za, Zak = Za_r[g % 2]
                        for z0 in range(0, Ka, 16):
                            z1 = min(Ka, z0 + 16)
                            P.dma(SP, lambda e, z0=z0, z1=z1: e.dma_start(
                                out=Za[z0:z1, :, :], in_=PQ[s][z0 * 128:z1 * 128, g * 256:(g + 1) * 256].rearrange("(t r) c -> t r c", r=128)),
                                reads=["PQ_" + s], writes=["%s_%d" % (Zak, z0 // 16)], slot=Zak)
                    def fold_group(g):
                        Za, Zak = Za_r[g % 2]
                        if Kb:
                            for pc in range(32):
                                zt, zk = zbp[pc % 2]
                                P.dma(SP, lambda e, pc=pc, zt=zt: e.dma_start(
                                    out=zt[:], in_=PQ[s][Ka * 128:Tv * 128, g * 256:(g + 1) * 256].rearrange(
                                        "(t r) c -> t r c", r=128)[:, pc * 4:(pc + 1) * 4, :]),
                                    reads=["PQ_" + s], writes=[zk], slot=zk)
                                P.op(DVE, lambda e, pc=pc, zt=zt: e.tensor_tensor(out=Za[0:Kb, pc * 4:(pc + 1) * 4, :],
                                                                                 in0=Za[0:Kb, pc * 4:(pc + 1) * 4, :], in1=zt[:], op=ALU.add),
                                     reads=["%s_%d" % (Zak, z_) for z_ in range((Ka + 15) // 16)] + [zk], writes=[Zak + "_0"])

                    ybc = 0
                    obc = 0
                    load_group(0)
                    fold_group(0)
                    for g in range(4):
                        if g + 1 < 4:
                            load_group(g + 1)
                        Za, Zak = Za_r[g % 2]
                        for dh in range(2):
                            for b0 in range(0, 64, NB):
                                yb = YB[ybc % 2]
                                ybk = "YB%d" % (ybc % 2)
                                for i in range(NB):
                                    dp = dh * 64 + b0 + i
                                    for pl in range(2):
                                        P.op(PE, lambda e, pl=pl, dp=dp, i=i, yb=yb, Za=Za: e.matmul(
                                            yb[:, i * 2 * T:(i + 1) * 2 * T], lhsT=Za[:, :, pl * 128 + dp],
                                            rhs=f1a[:, pl * 2 * T:(pl + 1) * 2 * T], start=(pl == 0), stop=(pl == 1)),
                                             reads=["%s_%d" % (Zak, z_) for z_ in range((Ka + 15) // 16)] + ["f1a"], writes=[ybk])
                                yv = yb[:, 0:NB * 2 * T].rearrange("p (n c t) -> p n c t", n=NB, c=2)
                                for c in range(2):
                                    dst = Yp[:, c, b0:b0 + NB, :]
                                    src = yv[:, :, c, :]
                                    if ybc % 2 == 0:
                                        P.op(ACT, lambda e, src=src, dst=dst: e.activation(out=dst, in_=src, func=AF.Copy),
                                             reads=[ybk], writes=["Yp_%d_%d" % (b0 // NB, c)])
                                    else:
                                        P.op(DVE, lambda e, src=src, dst=dst: e.tensor_copy(out=dst, in_=src),
                                             reads=[ybk], writes=["Yp_%d_%d" % (b0 // NB, c)])
                                ybc += 1
                            ypk = ["Yp_%d_%d" % (b, c) for b in range(64 // NB) for c in range(2)]
                            for k0 in range(0, T, KB2):
                                ob = OB[obc % 4]
                                obk = "OB%d" % (obc % 4)
                                obc += 1
                                for kk in range(KB2):
                                    k2 = k0 + kk
                                    for c in range(2):
                                        P.op(PE, lambda e, c=c, k2=k2, kk=kk, ob=ob: e.matmul(
                                            ob[0:64, kk * NO:(kk + 1) * NO], lhsT=Yp[:, c, :, k2], rhs=c2s[:, k2, c, :],
                                            start=(c == 0), stop=(c == 1)), reads=ypk + ["c2s"], writes=[obk])
                                src = ob[0:64, 0:KB2 * NO].rearrange("p (k m) -> p m k", k=KB2)
                                if obc % 2:
                                    P.op(ACT, lambda e, k0=k0, src=src: e.activation(out=f2t[:, :, k0:k0 + KB2], in_=src, func=AF.Copy),
                                         reads=[obk], writes=["f2_%d" % (k0 // KB2)])
                                else:
                                    P.op(DVE, lambda e, k0=k0, src=src: e.tensor_copy(out=f2t[:, :, k0:k0 + KB2], in_=src),
                                         reads=[obk], writes=["f2_%d" % (k0 // KB2)])
                            P.dma(SP, lambda e, g=g, dh=dh: e.dma_start(
                                out=MIX[s][g * 128 + dh * 64:g * 128 + dh * 64 + 64, :], in_=f2t[:].rearrange("p m k -> p (m k)")),
                                reads=["f2_%d" % b for b in range((T + KB2 - 1) // KB2)], writes=["MIX_" + s], slot="st_f2")
                        if g + 1 < 4:
                            fold_group(g + 1)
                    print("F" + s, P.emit())

            with contextlib.ExitStack() as st:
                P = Phase(ctx, "A2")
                W = SB(st, "W_a2", [128, NCH, 1536], BF16)
                Wo = SB(st, "Wo", [128, NCH, D], BF16)
                gt1 = SB(st, "gt1", [128, 2, D], F32)
                kvi = ring(st, "kvi", 8, [128, 1024], BF16)
                xt = ring(st, "xt", 3, [128, D], F32)
                tabt = ring(st, "tabt", 6, [128, 276], F32)
                xn = ring(st, "xn", 2, [128, D], BF16)
                hT = ring(st, "hT", 2, [128, NCH, 128], BF16)
                tmpr = [ring(st, "tmp%d_" % i, 2, [128, 4, 128], F32) for i in range(2)]
                tmpq = [ring(st, "tmq%d_" % i, 2, [128, 4, 128], F32) for i in range(2)]
                kr = ring(st, "kr", 2, [128, 4, 128], F32)
                qr = ring(st, "qr", 2, [128, 4, 128], F32)
                tm = ring(st, "tm", 2, [128, 16, 128], BF16)
                TQ = ring(st, "TQ", 3, [128, 16, 128], BF16)
                sT = ring(st, "sT", 2, [128, 4, 128], BF16)
                Sfb = ring(st, "Sfb", 3, [128, 4, 128], BF16)
                snp = ring(st, "snp", 3, [128, 4, 128], BF16)
                sg = ring(st, "sg", 4, [128, 512], F32)
                t1 = SB(st, "t1", [128, 512], F32)
                rb = ring(st, "rb", 2, [128, 512], BF16)
                rT = ring(st, "rT", 2, [128, 4, 128], BF16)
                fT = ring(st, "fT", 2, [128, 4, 128], BF16)
                ty = SB(st, "ty", [128, D], F32)
                xr = ring(st, "xr", 2, [128, D], F32)
                B = [PS(st, "B%d" % i) for i in range(8)]

                P.dma(POOL, lambda e: e.dma_start(out=W[:, :, 0:1024], in_=w_in[:, 512:1536].rearrange("(k p) n -> p k n", p=128)),
                      writes=["W"], slot="W0")
                P.dma(POOL, lambda e: e.dma_start(out=W[:, :, 1024:1536], in_=w_in[:, 2048:2560].rearrange("(k p) n -> p k n", p=128)),
                      writes=["W"], slot="W1")
                P.dma(POOL, lambda e: e.dma_start(out=Wo[:], in_=w_out.rearrange("(k p) n -> p k n", p=128)), writes=["Wo"], slot="Wo")
                for r in range(2):
                    P.dma(SP, lambda e, r=r: e.dma_start(out=gt1[:, r, :], in_=mod_d[r, 2 * D:3 * D].partition_broadcast(128)),
                          reads=["mod_d"], writes=["gt1"], slot="gt1")
                kvn = ring(st, "kvn", 8, [128, 1024], BF16)
                stages = []
                ocn = 0
                ncn = 0
                for s in ("p", "s"):
                    sq = seqs[s]
                    r = sq["row"]
                    own_pos = {v: m for m, v in enumerate(sq["own"])}
                    items = []
                    pend = []
                    for v in range(sq["Tv"]):
                        if v in own_pos:
                            items.append((pend, v))
                            pend = []
                        else:
                            pend.append(v)
                    if pend:
                        items.append((pend, None))
                    for (others, v) in items:
                        own = v is not None
                        m = own_pos.get(v, -1)
                        oc = ocn
                        if own:
                            ocn += 1
                        i = oc
                        first = (0 in others) or (v == 0)
                        nslots = []
                        for _ in others:
                            nslots.append(ncn % 8)
                            ncn += 1

                        def sl(s=s, v=v, oc=oc):
                            x_t, xk = xt[oc % 3]
                            tb_t, tbk = tabt[oc % 6]
                            P.dma(SP, lambda e: e.dma_start(out=x_t[:], in_=X[s][v * 128:(v + 1) * 128, :]), writes=[xk], slot=xk)
                            P.dma(SP, lambda e: e.dma_start(out=tb_t[:], in_=TAB[s][v * 128:(v + 1) * 128, :]), writes=[tbk], slot=tbk)

                        def s0(oc=oc):
                            x_t, xk = xt[oc % 3]
                            norm_a(P, x_t[:], xk, xn[oc % 2][0][:], xn[oc % 2][1], oc % 4)

                        def s1(oc=oc, r=r):
                            h_t, hk = hT[oc % 2]
                            trans_b(P, xn[oc % 2][0], xn[oc % 2][1], [(B[0], "B0", ACT)], r, 0, lambda c: h_t[:, c, :], lambda c: "%s_%d" % (hk, c))

                        def s2(s=s, v=v, own=own, oc=oc, others=others, nslots=nslots):
                            for ov, ns in zip(others, nslots):
                                kv_t, kvk = kvn[ns]
                                P.dma(SP, lambda e, ov=ov, kv_t=kv_t: e.dma_start(out=kv_t[:], in_=KV[s][ov * 128:(ov + 1) * 128, :]),
                                      reads=["KV_" + s], writes=[kvk], slot=kvk)
                            if not own:
                                return
                            kv_t, kvk = kvi[oc % 8]
                            P.dma(SP, lambda e: e.dma_start(out=kv_t[:], in_=KV[s][v * 128:(v + 1) * 128, :]),
                                  reads=["KV_" + s], writes=[kvk], slot=kvk)
                            h_t, hk = hT[oc % 2]
                            tb_t, tbk = tabt[oc % 6]
                            for (bi, c0) in ((4, 512), (6, 0), (7, 1024)):
                                for k in range(NCH):
                                    P.op(PE, lambda e, bi=bi, c0=c0, k=k: e.matmul(B[bi][:, :], lhsT=h_t[:, k, :], rhs=W[:, k, c0:c0 + 512],
                                                                                  start=(k == 0), stop=(k == NCH - 1)),
                                         reads=["%s_%d" % (hk, c_) for c_ in range(NCH)] + ["W"], writes=["B%d" % bi])
                            kr_t, krk = kr[oc % 2]
                            rotary(P, B[4], "B4", tb_t, tbk, kr_t, krk, tmpr[oc % 2])
                            qr_t, qrk = qr[oc % 2]
                            sg_t, sgk = sg[oc % 4]
                            tm_t, tmk = tm[oc % 2]
                            rotary(P, B[6], "B6", tb_t, tbk, qr_t, qrk, tmpq[oc % 2])
                            P.op(ACT, lambda e: e.activation(out=sg_t[:], in_=B[7][:, :], func=AF.Silu), reads=["B7"], writes=[sgk])
                            P.op(POOL, lambda e: e.tensor_copy(out=tm_t[:, 0:4, :], in_=qr_t[:]), reads=[qrk], writes=[tmk + "_0"])
                            P.op(POOL, lambda e: e.tensor_tensor(out=tm_t[:, 4:8, :], in0=qr_t[:],
                                                                 in1=tb_t[:, 256:260].unsqueeze(2).broadcast_to([128, 4, 128]), op=ALU.mult),
                                 reads=[qrk, tbk], writes=[tmk + "_1"])
                            P.op(POOL, lambda e: e.tensor_tensor(out=tm_t[:, 8:12, :], in0=qr_t[:],
                                                                 in1=tb_t[:, 260:264].unsqueeze(2).broadcast_to([128, 4, 128]), op=ALU.mult),
                                 reads=[qrk, tbk], writes=[tmk + "_2"])
                            P.op(POOL, lambda e: e.tensor_copy(out=tm_t[:, 12:16, :], in_=kr_t[:]), reads=[krk], writes=[tmk + "_3"])

                        def s3(s=s, own=own, oc=oc, m=m, first=first, others=others, nslots=nslots):
                            if first:
                                P.op(POOL, lambda e: e.memset(S8[:, 0:4, :], 0.0), writes=["S8f"])
                            seq_t = [kvn[ns] for ns in nslots] + ([kvi[oc % 8]] if own else [])
                            for ti_, (kv_t, kvk) in enumerate(seq_t):
                                is_own = own and ti_ == len(seq_t) - 1
                                if is_own:
                                    sf_t, sfk = Sfb[oc % 3]
                                    P.op(ACT, lambda e, sf_t=sf_t: e.activation(out=sf_t[:], in_=S8[:, 0:4, :], func=AF.Copy), reads=["S8f"], writes=[sfk])
                                kb_ = 5 if ti_ % 2 == 0 else 4
                                for h in range(4):
                                    P.op(PE, lambda e, h=h, kv_t=kv_t, kb_=kb_: e.matmul(B[kb_][:, h * 128:(h + 1) * 128], lhsT=kv_t[:, h * 128:(h + 1) * 128],
                                                                                      rhs=kv_t[:, 512 + h * 128:512 + (h + 1) * 128], start=True, stop=True),
                                         reads=[kvk, "B%d" % kb_], writes=["B%d" % kb_])
                                state_update(P, S8, 0, B[kb_], "B%d" % kb_, "S8f")
                            if own:
                                tm_t, tmk = tm[oc % 2]
                                tq_t, tqk = TQ[oc % 3]
                                sp_t, spk = snp[oc % 3]
                                P.dma(SP, lambda e: e.dma_start(out=sp_t[:].rearrange("p h d -> p (h d)"), in_=SNAP[s][m * 128:(m + 1) * 128, :]),
                                      reads=["SNAP_" + s], writes=[spk], slot=spk)
                                for half in range(2):
                                    bv = B[2 + half][:, :].bitcast(BF16)
                                    for j in range(8):
                                        P.op(PE, lambda e, half=half, j=j, bv=bv: e.transpose(out=bv[:, j * 128:(j + 1) * 128],
                                                                                            in_=tm_t[:, half * 8 + j, :], identity=ident[:]),
                                             reads=["%s_%d" % (tmk, (half * 8 + j) // 4), "ident"], writes=["B%d" % (2 + half)])
                                P.op(ACT, lambda e: e.activation(out=tq_t[:, 0:8, :], in_=B[2][:, :].bitcast(BF16).rearrange("p (a b) -> p a b", a=8),
                                                                 func=AF.Copy), reads=["B2"], writes=[tqk + "_a"])
                                P.op(DVE, lambda e: e.tensor_copy(out=tq_t[:, 8:16, :], in_=B[3][:, :].bitcast(BF16).rearrange("p (a b) -> p a b", a=8)),
                                     reads=["B3"], writes=[tqk + "_b"])

                        def s4(oc=oc):
                            tq_t, tqk = TQ[oc % 3]
                            st_t, stk = sT[oc % 2]
                            for h in range(4):
                                P.op(PE, lambda e, h=h: e.matmul(B[1][:, h * 128:(h + 1) * 128], lhsT=tq_t[:, 12 + h, :], rhs=tq_t[:, h, :],
                                                                 start=True, stop=True), reads=[tqk + "_a", tqk + "_b"], writes=["B1"])
                            P.op(DVE, lambda e: e.tensor_tensor(out=st_t[:], in0=B[1][:, :].rearrange("p (h i) -> p h i", h=4), in1=maskT[:],
                                                                op=ALU.mult), reads=["B1", "maskT"], writes=[stk])

                        def s5(i=i, oc=oc):
                            tq_t, tqk = TQ[oc % 3]
                            st_t, stk = sT[oc % 2]
                            sf_t, sfk = Sfb[oc % 3]
                            sp_t, spk = snp[oc % 3]
                            kv_t, kvk = kvi[i % 8]
                            sg_t, sgk = sg[oc % 4]
                            rb_t, rbk = rb[oc % 2]
                            for h in range(4):
                                o = B[6][:, h * 128:(h + 1) * 128]
                                P.op(PE, lambda e, h=h, o=o: e.matmul(o, lhsT=st_t[:, h, :], rhs=kv_t[:, 512 + h * 128:512 + (h + 1) * 128],
                                                                      start=True, stop=False), reads=[stk, kvk], writes=["B6"])
                                P.op(PE, lambda e, h=h, o=o: e.matmul(o, lhsT=tq_t[:, 4 + h, :], rhs=sf_t[:, h, :], start=False, stop=False),
                                     reads=[tqk + "_a", sfk], writes=["B6"])
                                P.op(PE, lambda e, h=h, o=o: e.matmul(o, lhsT=tq_t[:, 8 + h, :], rhs=sp_t[:, h, :], start=False, stop=True),
                                     reads=[tqk + "_b", spk], writes=["B6"])
                            for h in range(4):
                                P.op(ACT, lambda e, h=h: e.activation(out=junk[:, 0:128], in_=B[6][:, h * 128:(h + 1) * 128], func=AF.Square,
                                                                      accum_out=stat[:, 0, 4 + h:5 + h]), reads=["B6"], writes=["ssg%d" % h])
                            P.op(DVE, lambda e: e.tensor_scalar(out=stat[:, 1, 4:8], in0=stat[:, 0, 4:8], scalar1=1.0 / 128, scalar2=EPS,
                                                                op0=ALU.mult, op1=ALU.add), reads=["ssg0", "ssg1", "ssg2", "ssg3"], writes=["msg"])
                            P.op(POOL, lambda e: e.tensor_tensor(out=stat[:, 2, 4:8], in0=stat[:, 1, 4:8], in1=mh[:, 0:1].broadcast_to([128, 4]),
                                                                 op=ALU.pow), reads=["msg", "mh"], writes=["rsg"])
                            P.op(DVE, lambda e: e.tensor_tensor(out=t1[:].rearrange("p (h d) -> p h d", h=4),
                                                                in0=B[6][:, :].rearrange("p (h d) -> p h d", h=4),
                                                                in1=stat[:, 2, 4:8].unsqueeze(2).broadcast_to([128, 4, 128]), op=ALU.mult),
                                 reads=["B6", "rsg"], writes=["t1"])
                            P.op(DVE, lambda e: e.tensor_tensor(out=rb_t[:], in0=t1[:], in1=sg_t[:], op=ALU.mult), reads=["t1", sgk], writes=[rbk])

                        def s6(s=s, v=v, oc=oc, m=m):
                            rb_t, rbk = rb[oc % 2]
                            rT_t, rTk = rT[oc % 2]
                            f_t, fk = fT[oc % 2]
                            xr_t, xrk = xr[oc % 2]
                            b1v = B[1][:, :].bitcast(BF16)
                            for h in range(4):
                                P.op(PE, lambda e, h=h: e.transpose(out=b1v[:, h * 128:(h + 1) * 128], in_=rb_t[:, h * 128:(h + 1) * 128],
                                                                    identity=ident[:]), reads=[rbk, "ident"], writes=["B1"])
                            P.op(ACT, lambda e: e.activation(out=rT_t[:], in_=b1v[:, 0:512].rearrange("p (h d) -> p h d", h=4), func=AF.Copy),
                                 reads=["B1"], writes=[rTk])
                            P.dma(SP, lambda e: e.dma_start(out=f_t[:], in_=MIX[s][:, m * 128:(m + 1) * 128].rearrange("(g d) t -> d g t", d=128)),
                                  reads=["MIX_" + s], writes=[fk], slot=fk)
                            P.dma(SP, lambda e: e.dma_start(out=xr_t[:], in_=X[s][v * 128:(v + 1) * 128, :]), writes=[xrk], slot=xrk)

                        def s7(s=s, r=r, oc=oc, m=m):
                            rT_t, rTk = rT[oc % 2]
                            f_t, fk = fT[oc % 2]
                            xr_t, xrk = xr[oc % 2]
                            for half in range(2):
                                for kc in range(NCH):
                                    lhs = f_t[:, kc, :] if kc < 4 else rT_t[:, kc - 4, :]
                                    P.op(PE, lambda e, half=half, kc=kc, lhs=lhs: e.matmul(B[2 + half][:, :], lhsT=lhs,
                                                                                         rhs=Wo[:, kc, half * 512:(half + 1) * 512],
                                                                                         start=(kc == 0), stop=(kc == NCH - 1)),
                                         reads=[fk, rTk, "Wo"], writes=["B%d" % (2 + half)])
                            for half in range(2):
                                P.op(DVE, lambda e, half=half: e.tensor_tensor(out=ty[:, half * 512:(half + 1) * 512], in0=B[2 + half][:, :],
                                                                              in1=gt1[:, r, half * 512:(half + 1) * 512], op=ALU.mult),
                                     reads=["B%d" % (2 + half), "gt1"], writes=["ty_%d" % half])
                            P.op(POOL, lambda e: e.tensor_tensor(out=xr_t[:], in0=ty[:], in1=xr_t[:], op=ALU.add), reads=["ty_0", "ty_1", xrk], writes=[xrk])
                            P.dma(SP, lambda e: e.dma_start(out=X1[s][m * 128:(m + 1) * 128, :], in_=xr_t[:]),
                                  reads=[xrk], writes=["X1_" + s], slot="st_" + xrk)

                        stages.append([sl, s0, s1, s2, s3, s4, s5, s6, s7] if own else [None, None, None, s2, s3])
                run_pipeline(stages, 9, order=[0, 1, 2, 3, 4, 5, 6, 7, 8])
                print("A2", P.emit())

        with contextlib.ExitStack() as st:
            P = Phase(ctx, "B")
            W1 = SB(st, "W1", [128, NCH, DFF], BF16)
            W2 = SB(st, "W2", [128, 32, D], BF16)
            modfm_b = modfm
            stat_b = SB(st, "stat", [128, 3, 8], F32)
            junk_b = SB(st, "junk", [128, D], BF16)
            gt2 = SB(st, "gt2", [128, D], F32)
            nfb = SB(st, "nfb", [128, D], F32)
            x1a = ring(st, "x1a", 2, [128, D], F32)
            x1c = ring(st, "x1c", 2, [128, D], F32)
            xnb = ring(st, "xnb", 4, [128, D], BF16)
            h2T = ring(st, "h2T", 2, [128, NCH, 256], BF16)
            aT = SB(st, "aT", [128, 32, 256], BF16)
            rl = ring(st, "rl", 2, [128, 256], F32)
            ty = SB(st, "ty", [128, D], F32)
            yo = ring(st, "yo", 2, [128, D], F32)
            B = [PS(st, "B%d" % i) for i in range(8)]

            for q4 in range(4):
                P.dma(POOL, lambda e, q4=q4: e.dma_start(out=W1[:, :, q4 * 1024:(q4 + 1) * 1024],
                                                        in_=w_mlp_in[:, q4 * 1024:(q4 + 1) * 1024].rearrange("(k p) n -> p k n", p=128)),
                      writes=["W1_%d" % q4], slot="W1_%d" % q4)
            for q4 in range(4):
                P.dma(POOL, lambda e, q4=q4: e.dma_start(out=W2[:, q4 * 8:(q4 + 1) * 8, :],
                                                        in_=w_mlp_out[q4 * 1024:(q4 + 1) * 1024, :].rearrange("(k p) n -> p k n", p=128)),
                      writes=["W2_%d" % q4], slot="W2_%d" % q4)
            P.dma(SP, lambda e: e.dma_start(out=nfb[:], in_=nf_bc[:, :]), writes=["nfb"], slot="nfb")
            bcn = 0
            ycn = 0
            for s in ("p", "s"):
                sq = seqs[s]
                r = sq["row"]
                ntile = len(sq["own"])
                P.dma(SP, lambda e, r=r: e.dma_start(out=gt2[:], in_=mod_d[r, 5 * D:6 * D].partition_broadcast(128)),
                      reads=["mod_d"], writes=["gt2"], slot="gt2")
                stages = []
                for blk in range(0, ntile, 2):
                    tiles = list(range(blk, min(blk + 2, ntile)))
                    bi_ = bcn
                    bcn += 1

                    def s0(tiles=tiles, s=s, bi_=bi_):
                        for ti, m in enumerate(tiles):
                            x_t, xk = x1a[ti]
                            xb_ = xnb[2 * (bi_ % 2) + ti]
                            P.dma(SP, lambda e, x_t=x_t, m=m: e.dma_start(out=x_t[:], in_=X1[s][m * 128:(m + 1) * 128, :]),
                                  reads=["X1_" + s], writes=[xk], slot=xk)
                            norm_a(P, x_t[:], xk, xb_[0][:], xb_[1], ti, stat=stat_b, junk=junk_b)

                    def s1(tiles=tiles, bi_=bi_, r=r):
                        h_t, hk = h2T[bi_ % 2]
                        for ti, m in enumerate(tiles):
                            xb_ = xnb[2 * (bi_ % 2) + ti]
                            trans_b(P, xb_[0], xb_[1], [(B[ti], "B%d" % ti, ACT if ti == 0 else DVE)], r, 2,
                                    lambda c, ti=ti: h_t[:, c, ti * 128:(ti + 1) * 128], lambda c, ti=ti: "%s_%d_%d" % (hk, c, ti), modfm=modfm_b)

                    def s2(tiles=tiles, bi_=bi_):
                        h_t, hk = h2T[bi_ % 2]
                        NW = len(tiles) * 128
                        for f in range(32):
                            bi = 2 + (f % 4)
                            bk = "B%d" % bi
                            rl_t, rlk = rl[f % 2]
                            for k in range(NCH):
                                P.op(PE, lambda e, f=f, k=k, bi=bi: e.matmul(B[bi][:, 0:NW], lhsT=W1[:, k, f * 128:(f + 1) * 128],
                                                                            rhs=h_t[:, k, 0:NW], start=(k == 0), stop=(k == NCH - 1)),
                                     reads=["W1_%d" % (f // 8)] + ["%s_%d_%d" % (hk, c_, ti) for ti in range(len(tiles)) for c_ in range(NCH)], writes=[bk])
                            P.op(ACT, lambda e, bi=bi, rl_t=rl_t: e.activation(out=rl_t[:, 0:NW], in_=B[bi][:, 0:NW], func=AF.Relu),
                                 reads=[bk], writes=[rlk])
                            P.op(DVE, lambda e, f=f, bi=bi, rl_t=rl_t: e.tensor_tensor(out=aT[:, f, 0:NW], in0=B[bi][:, 0:NW], in1=rl_t[:, 0:NW],
                                                                                     op=ALU.mult), reads=[bk, rlk], writes=["aT_%d" % f])

                    def s3(tiles=tiles, s=s):
                        nonlocal ycn
                        for ti, m in enumerate(tiles):
                            x_t, xk = x1c[ti]
                            P.dma(SP, lambda e, x_t=x_t, m=m: e.dma_start(out=x_t[:], in_=X1[s][m * 128:(m + 1) * 128, :]),
                                  reads=["X1_" + s], writes=[xk], slot=xk)
                            for half in range(2):
                                bi = 6 + half
                                for f in range(32):
                                    P.op(PE, lambda e, f=f, ti=ti, half=half, bi=bi: e.matmul(B[bi][:, :], lhsT=aT[:, f, ti * 128:(ti + 1) * 128],
                                                                                             rhs=W2[:, f, half * 512:(half + 1) * 512],
                                                                                             start=(f == 0), stop=(f == 31)),
                                         reads=["aT_%d" % f, "W2_%d" % (f // 8)], writes=["B%d" % bi])
                                P.op(DVE, lambda e, half=half, bi=bi: e.tensor_tensor(out=ty[:, half * 512:(half + 1) * 512], in0=B[bi][:, :],
                                                                                     in1=gt2[:, half * 512:(half + 1) * 512], op=ALU.mult),
                                     reads=["B%d" % bi, "gt2"], writes=["ty_%d" % half])
                            P.op(DVE, lambda e, x_t=x_t: e.tensor_tensor(out=ty[:], in0=ty[:], in1=x_t[:], op=ALU.add),
                                 reads=["ty_0", "ty_1", xk], writes=["ty_0", "ty_1"])
                            y_t, yk = yo[ycn % 2]
                            ycn += 1
                            col = 2 + ti
                            ssc, msc, rsc = stat_b[:, 0, col:col + 1], stat_b[:, 1, col:col + 1], stat_b[:, 2, col:col + 1]
                            P.op(ACT, lambda e, ssc=ssc: e.activation(out=junk_b[:], in_=ty[:], func=AF.Square, accum_out=ssc),
                                 reads=["ty_0", "ty_1"], writes=["ss%d" % col])
                            P.op(DVE, lambda e, ssc=ssc, msc=msc: e.tensor_scalar(out=msc, in0=ssc, scalar1=1.0 / D, scalar2=EPS,
                                                                                 op0=ALU.mult, op1=ALU.add),
                                 reads=["ss%d" % col], writes=["ms%d" % col])
                            P.op(POOL, lambda e, msc=msc, rsc=rsc: e.tensor_tensor(out=rsc, in0=msc, in1=mh[:, 0:1], op=ALU.pow),
                                 reads=["ms%d" % col, "mh"], writes=["rs%d" % col])
                            P.op(DVE, lambda e, y_t=y_t, rsc=rsc: e.scalar_tensor_tensor(out=y_t[:], in0=ty[:], scalar=rsc, in1=nfb[:],
                                                                                        op0=ALU.mult, op1=ALU.mult),
                                 reads=["ty_0", "ty_1", "rs%d" % col, "nfb"], writes=[yk])
                            P.dma(SP, lambda e, y_t=y_t, m=m: e.dma_start(out=Y[s][m * 128:(m + 1) * 128, :], in_=y_t[:]),
                                  reads=[yk], writes=["Y_" + s], slot="st_" + yk)

                    stages.append([s0, s1, s2, s3])
                run_pipeline(stages, 4, order=[0, 1, 3, 2])
            print("B", P.emit())
    return nc


def make_in_maps(inputs, Tp, Ts):
    f = lambda a: np.ascontiguousarray(np.asarray(a, dtype=np.float32))
    xp, xs = f(inputs["x_prompt"]), f(inputs["x_sample"])
    cp, cs = f(inputs["c_prompt"]), f(inputs["c_sample"])
    NOWN = Tp // 4
    Tvp = Tp + 3
    maskT, g8 = _mask_tables()
    idx = np.arange(128)
    cs128 = np.concatenate([np.cos(2 * np.pi * np.outer(idx, idx) / 128), np.sin(2 * np.pi * np.outer(idx, idx) / 128)], axis=1)
    cs128 = (cs128 / np.sqrt(128.0)).astype(np.float32)
    nm_fm = np.stack([f(inputs["norm_mix"])[0].reshape(NCH, 128).T, f(inputs["norm_mlp"])[0].reshape(NCH, 128).T], axis=1)
    nf_bc = np.ascontiguousarray(np.broadcast_to(f(inputs["norm_final"])[None, :], (128, D)))
    ada_b2 = np.ascontiguousarray(np.broadcast_to(f(inputs["ada_b"])[0][None, :], (2, 6 * D)))
    tab_s = _tile_tables([t * 128 for t in range(Ts)], [1.0] * Ts)
    f1_s, c2_s = _fft_tables(Ts, list(range(Ts)), list(range(128)))
    shared = dict(
        ada_w=f(inputs["ada_w"])[0], ada_b2=ada_b2, nm_fm=np.ascontiguousarray(nm_fm), nf_bc=nf_bc,
        w_in=f(inputs["w_in"])[0], w_fnet=f(inputs["w_fnet"])[0], w_out=f(inputs["w_out"])[0],
        w_mlp_in=f(inputs["w_mlp_in"])[0], w_mlp_out=f(inputs["w_mlp_out"])[0],
        ident=np.eye(128, dtype=np.float32), cs128=cs128, maskT=maskT, g8=g8,
        tab_s=tab_s, f1_s=f1_s, c2_s=c2_s,
    )
    per_j = {}
    for j in range(4):
        pad = 3 - j
        vmap = [-1] * pad + list(range(Tp)) + [-1] * j
        tab_p = _tile_tables([max(t, 0) * 128 for t in vmap], [1.0 if t >= 0 else 0.0 for t in vmap])
        kpt = 128 // Tp
        f1_p, c2_p = _fft_tables(Tp, vmap, [(4 * m + j) * kpt + i for m in range(NOWN) for i in range(kpt)])
        per_j[j] = dict(tab_p=tab_p, f1_p=f1_p, c2_p=c2_p)
    xv_cache = {}
    in_maps = []
    for core in range(8):
        b, j = core // 4, core % 4
        if (b, j) not in xv_cache:
            xv = np.zeros((Tvp * 128, D), np.float32)
            xv[(3 - j) * 128:(3 - j) * 128 + Tp * 128] = xp[b]
            xv_cache[(b, j)] = xv
        c2T = np.stack([cp[b].reshape(NCH, 128).T, cs[core].reshape(NCH, 128).T], axis=2)
        m = dict(shared)
        m.update(per_j[j])
        m.update(xv=xv_cache[(b, j)], xs=xs[core], c2T=np.ascontiguousarray(c2T))
        in_maps.append(m)
    return in_maps


_NC_CACHE = {}


def run(inputs, Tp, Ts):
    key = (Tp, Ts)
    if key not in _NC_CACHE:
        _NC_CACHE[key] = build(Tp, Ts)
    nc = _NC_CACHE[key]
    in_maps = make_in_maps(inputs, Tp, Ts)
    res = run_bass_kernel_spmd(nc, in_maps, core_ids=list(range(8)))
    NOWN = Tp // 4
    B = inputs["x_prompt"].shape[0]
    yp = np.zeros((B, Tp * 128, D), np.float32)
    ys = np.zeros((8, Ts * 128, D), np.float32)
    for core in range(8):
        b, j = core // 4, core % 4
        o = np.asarray(res.results[core]["yp"]).reshape(NOWN, 128, D)
        yp[b].reshape(Tp, 128, D)[j::4] = o
        ys[core] = np.asarray(res.results[core]["ys"])
    return yp, ys


def kernel(x_prompt, x_sample, c_prompt, c_sample, ada_w, ada_b, norm_mix, w_in, w_fnet, w_out,
           norm_mlp, w_mlp_in, w_mlp_out, norm_final):
    inputs = dict(x_prompt=x_prompt, x_sample=x_sample, c_prompt=c_prompt, c_sample=c_sample, ada_w=ada_w,
                  ada_b=ada_b, norm_mix=norm_mix, w_in=w_in, w_fnet=w_fnet, w_out=w_out, norm_mlp=norm_mlp,
                  w_mlp_in=w_mlp_in, w_mlp_out=w_mlp_out, norm_final=norm_final)
    Tp = np.asarray(x_prompt).shape[1] // 128
    Ts = np.asarray(x_sample).shape[1] // 128
    yp, ys = run(inputs, Tp, Ts)
    return (yp, ys)
```

```python
import contextlib
import os as _os
import re as _re
import numpy as np
import concourse.bass as bass
import concourse.mybir as mybir
from concourse.bass_utils import run_bass_kernel_spmd

F32 = mybir.dt.float32
BF16 = mybir.dt.bfloat16
AF = mybir.ActivationFunctionType
ALU = mybir.AluOpType

D = 1024
NCH = 8
DFF = 4096
EPS = 1e-6
_COARSE = bool(_os.environ.get("K_COARSE"))
PE, ACT, DVE, POOL, SP = "pe", "act", "dve", "pool", "sp"
COMPUTE = (PE, ACT, DVE, POOL)


class Op:
    __slots__ = ("eng", "fn", "deps", "is_dma", "slot", "needs_inc", "tok")


class Ctx:
    def __init__(self, nc, stack, n_dma_sems=80):
        self.nc = nc
        self.eng_sem = {e: stack.enter_context(nc.semaphore("sem_" + e)) for e in COMPUTE}
        self.eng_cnt = {e: 0 for e in COMPUTE}
        self.dma_sems = [stack.enter_context(nc.semaphore("semd%d" % i)) for i in range(n_dma_sems)]
        self.dma_cnt = [0] * n_dma_sems
        self.slot_map = {}
        self.last_writer = {}
        self.readers = {}
        self.waited = {e: {} for e in (PE, ACT, DVE, POOL, SP)}

    def slot_id(self, key):
        if key not in self.slot_map:
            assert len(self.slot_map) < len(self.dma_sems), "out of dma sems"
            self.slot_map[key] = len(self.slot_map)
        return self.slot_map[key]


class Phase:
    def __init__(self, ctx, name):
        self.ctx = ctx
        self.name = name
        self.ops = []

    def _add(self, eng, fn, reads, writes, is_dma=False, slot=None):
        c = self.ctx
        if _COARSE:
            pat = _os.environ.get("K_COARSE")
            reads = [_re.sub(pat, "", k) for k in reads]
            writes = [_re.sub(pat, "", k) for k in writes]
        op = Op()
        op.eng, op.fn, op.is_dma, op.slot = eng, fn, is_dma, slot
        op.needs_inc = is_dma
        op.tok = None
        deps = []
        for b in reads:
            w = c.last_writer.get(b)
            if w is not None:
                deps.append(w)
        for b in writes:
            w = c.last_writer.get(b)
            if w is not None:
                deps.append(w)
            last = {}
            for rd in c.readers.get(b, ()):
                if rd.is_dma:
                    deps.append(rd)
                else:
                    last[rd.eng] = rd
            deps.extend(last.values())
        op.deps = deps
        for b in reads:
            c.readers.setdefault(b, []).append(op)
        for b in writes:
            c.last_writer[b] = op
            c.readers[b] = []
        self.ops.append(op)
        return op

    def op(self, eng, fn, reads=(), writes=()):
        return self._add(eng, fn, reads, writes)

    def dma(self, queue, fn, reads=(), writes=(), slot=None):
        return self._add(queue, fn, reads, writes, is_dma=True, slot=self.ctx.slot_id(slot))

    def emit(self):
        c = self.ctx
        nc = c.nc
        for op in self.ops:
            for d in op.deps:
                if d.is_dma:
                    continue
                if d.eng == PE and op.eng == PE and not op.is_dma:
                    continue
                d.needs_inc = True
        for op in self.ops:
            if op.is_dma:
                c.dma_cnt[op.slot] += 16
                op.tok = (c.dma_sems[op.slot], c.dma_cnt[op.slot], ("d", op.slot))
            elif op.needs_inc:
                c.eng_cnt[op.eng] += 1
                op.tok = (c.eng_sem[op.eng], c.eng_cnt[op.eng], ("e", op.eng))
        per_eng = {e: [] for e in (PE, ACT, DVE, POOL, SP)}
        last_dma_tok = {}
        for op in self.ops:
            waits = {}
            for d in op.deps:
                if d.tok is None:
                    continue
                if (not d.is_dma) and d.eng == PE and op.eng == PE and not op.is_dma:
                    continue
                sem, val, key = d.tok
                if c.waited[op.eng].get(key, 0) >= val:
                    continue
                if key not in waits or waits[key][1] < val:
                    waits[key] = (sem, val)
            for key, (sem, val) in waits.items():
                c.waited[op.eng][key] = val
            per_eng[op.eng].append((op, list(waits.values())))
            if op.is_dma:
                last_dma_tok[op.tok[2]] = op.tok
        final_waits = []
        for key, (sem, val, _) in last_dma_tok.items():
            if c.waited[SP].get(key, 0) < val:
                c.waited[SP][key] = val
                final_waits.append((sem, val))
        with nc.Block() as block:
            def run(eng_name):
                def body(e):
                    for op, waits in per_eng[eng_name]:
                        for sem, val in waits:
                            e.wait_ge(sem, val)
                        ins = op.fn(e)
                        if op.is_dma:
                            ins.then_inc(op.tok[0], 16)
                        elif op.needs_inc:
                            ins.then_inc(op.tok[0], 1)
                    if eng_name == SP:
                        for sem, val in final_waits:
                            e.wait_ge(sem, val)
                return body
            block.sync(run(SP))
            if per_eng[ACT]:
                block.scalar(run(ACT))
            if per_eng[DVE]:
                block.vector(run(DVE))
            if per_eng[POOL]:
                block.gpsimd(run(POOL))
            if per_eng[PE]:
                block.tensor(run(PE))
        self.ops = []
        return {e: len(v) for e, v in per_eng.items()}


def _gammas():
    h = np.arange(4, dtype=np.float64)
    lf = np.log1p(-np.exp2(-5.0 - 0.0 - h))
    lb = np.log1p(-np.exp2(-5.0 - 0.5 - h))
    return lf, lb


def _tile_tables(pos0_list, valid_list):
    lf, lb = _gammas()
    inv = 10000.0 ** (-(np.arange(64, dtype=np.float64) / 64.0))
    p = np.arange(128, dtype=np.float64)
    out = np.zeros((len(pos0_list), 128, 276), np.float32)
    ks = 128.0 ** -0.5
    for i, (pos0, valid) in enumerate(zip(pos0_list, valid_list)):
        pos = (pos0 + np.arange(128)).astype(np.float64)
        ang = pos[:, None] * inv[None, :]
        out[i, :, 0:64] = np.cos(ang)
        out[i, :, 64:128] = np.cos(ang)
        out[i, :, 128:192] = -np.sin(ang)
        out[i, :, 192:256] = np.sin(ang)
        out[i, :, 256:260] = np.exp(lf[None, :] * (p[:, None] + 1.0))
        out[i, :, 260:264] = np.exp(lb[None, :] * (128.0 - p[:, None]))
        out[i, :, 264:268] = ks * np.exp(lf[None, :] * (127.0 - p[:, None])) * valid
        out[i, :, 268:272] = ks * np.exp(lb[None, :] * p[:, None]) * valid
        out[i, :, 272] = valid
    return out.reshape(len(pos0_list) * 128, 276)


def _mask_tables():
    lf, lb = _gammas()
    j = np.arange(128, dtype=np.float64)[:, None]
    i = np.arange(128, dtype=np.float64)[None, :]
    m = np.zeros((128, 4, 128), np.float32)
    for h in range(4):
        f = np.exp(lf[h] * np.maximum(i - j, 0.0))
        b = np.exp(lb[h] * np.maximum(j - i, 0.0))
        m[:, h, :] = (128.0 ** -0.5) * np.where(j <= i, f, b)
    g8 = np.zeros((128, 8), np.float32)
    g8[:, 0:4] = np.exp(lf * 128.0)[None, :]
    g8[:, 4:8] = np.exp(lb * 128.0)[None, :]
    return m, g8


def _fft_tables(T, vmap, k1_list):
    N = 128 * T
    k2 = np.arange(T, dtype=np.float64)
    f1 = np.zeros((T, 4 * T), np.float32)
    for v_, t in enumerate(vmap):
        if t < 0:
            continue
        v = v_ % T
        a = 2 * np.pi * ((t * k2) % T) / T
        f1[v, 0:T] = np.cos(a)
        f1[v, T:2 * T] = -np.sin(a)
        f1[v, 2 * T:3 * T] = -np.sin(a)
        f1[v, 3 * T:4 * T] = -np.cos(a)
    r = np.arange(128, dtype=np.float64)[:, None, None]
    k1 = np.asarray(k1_list, dtype=np.float64)[None, None, :]
    kk = k2[None, :, None]
    ph = (r * (k1 * T + kk)) % N
    al = 2 * np.pi * ph / N
    sc = 1.0 / np.sqrt(N)
    c2k = np.stack([np.cos(al) * sc, np.sin(al) * sc], axis=2).astype(np.float32)
    return f1, np.ascontiguousarray(c2k)


def build(Tp, Ts):
    NOWN = Tp // 4
    Tvp = Tp + 3
    own_v = [3 + 4 * m for m in range(NOWN)]
    seqs = {
        "p": dict(T=Tp, Tv=Tvp, own=own_v, NO=NOWN * 128 // Tp, row=0),
        "s": dict(T=Ts, Tv=Ts, own=list(range(Ts)), NO=128, row=1),
    }
    nc = bass.Bass("TRN2", target_bir_lowering=False)
    din = lambda n, s, d=F32: nc.dram_tensor(n, list(s), d, kind="ExternalInput").ap()
    dout = lambda n, s, d=F32: nc.dram_tensor(n, list(s), d, kind="ExternalOutput").ap()
    dscr = lambda n, s, d: nc.dram_tensor(n, list(s), d, kind="Internal").ap()

    X = {"p": din("xv", [Tvp * 128, D]), "s": din("xs", [Ts * 128, D])}
    TAB = {"p": din("tab_p", [Tvp * 128, 276]), "s": din("tab_s", [Ts * 128, 276])}
    F1 = {"p": din("f1_p", [Tp, 4 * Tp]), "s": din("f1_s", [Ts, 4 * Ts])}
    C2 = {"p": din("c2_p", [128, Tp, 2, NOWN * 128 // Tp]), "s": din("c2_s", [128, Ts, 2, 128])}
    c2T = din("c2T", [128, NCH, 2])
    ada_w = din("ada_w", [D, 6 * D])
    ada_b2 = din("ada_b2", [2, 6 * D])
    nm_fm = din("nm_fm", [128, 2, NCH])
    nf_bc = din("nf_bc", [128, D])
    w_in = din("w_in", [D, 2560])
    w_fnet = din("w_fnet", [4, 128, 128])
    w_out = din("w_out", [D, D])
    w_mlp_in = din("w_mlp_in", [D, DFF])
    w_mlp_out = din("w_mlp_out", [DFF, D])
    ident_d = din("ident", [128, 128])
    cs_d = din("cs128", [128, 256])
    mask_d = din("maskT", [128, 4, 128])
    g8_d = din("g8", [128, 8])
    Y = {"p": dout("yp", [NOWN * 128, D]), "s": dout("ys", [Ts * 128, D])}

    PQ = {"p": dscr("pq_p", [Tvp * 128, 1024], BF16), "s": dscr("pq_s", [Ts * 128, 1024], BF16)}
    MIX = {"p": dscr("mix_p", [512, NOWN * 128], BF16), "s": dscr("mix_s", [512, Ts * 128], BF16)}
    X1 = {"p": dscr("x1_p", [NOWN * 128, D], F32), "s": dscr("x1_s", [Ts * 128, D], F32)}
    wab_d = dscr("wab", [D, 1024], BF16)
    mod_d = dscr("mod", [2, 6 * D], F32)

    with contextlib.ExitStack() as gs:
        ctx = Ctx(nc, gs)
        uid = [0]

        def SB(st, n, s, d):
            uid[0] += 1
            return st.enter_context(nc.sbuf_tensor("%s_u%d" % (n, uid[0]), list(s), d))

        def PS(st, n):
            uid[0] += 1
            return st.enter_context(nc.psum_tensor("%s_u%d" % (n, uid[0]), [128, 512], F32))
        ident = SB(gs, "ident", [128, 128], BF16)
        mh = SB(gs, "mh", [128, 1], F32)
        epsb = SB(gs, "epsb", [128, 1], F32)
        modfm = SB(gs, "modfm", [128, 2, 4, NCH], F32)
        nmt = SB(gs, "nmt", [128, 2, NCH], F32)

        with contextlib.ExitStack() as st:
            P = Phase(ctx, "setup")
            c2 = SB(st, "c2", [128, NCH, 2], F32)
            scb = SB(st, "scb", [128, NCH, 2], BF16)
            adab = SB(st, "adab", [2, 6 * D], F32)
            modsb = SB(st, "modsb", [2, 6 * D], F32)
            aw = [SB(st, "aw%d" % i, [128, NCH, 512], BF16) for i in range(2)]
            wfb = SB(st, "wfb", [128, 4, 128], BF16)
            csb = SB(st, "csb", [128, 256], BF16)
            wcws = SB(st, "wcws", [128, 4, 256], BF16)
            wub = SB(st, "wub", [128, NCH, 512], BF16)
            wuT = SB(st, "wuT", [128, 4, NCH, 128], BF16)
            wabs = SB(st, "wabs", [128, NCH, 1024], BF16)
            pm = [PS(st, "pm%d" % i) for i in range(2)]
            pw = PS(st, "pw")
            pt = [PS(st, "ptb%d" % i) for i in range(2)]
            pab = [PS(st, "pab%d" % i) for i in range(2)]

            P.dma(POOL, lambda e: e.dma_start(out=ident[:], in_=ident_d[:, :]), writes=["ident"], slot="ident")
            P.op(POOL, lambda e: e.memset(mh[:], -0.5), writes=["mh"])
            P.op(POOL, lambda e: e.memset(epsb[:], EPS), writes=["epsb"])
            P.dma(SP, lambda e: e.dma_start(out=c2[:], in_=c2T[:, :, :]), writes=["c2"], slot="c2")
            P.dma(SP, lambda e: e.dma_start(out=adab[:], in_=ada_b2[:, :]), writes=["adab"], slot="adab")
            P.op(ACT, lambda e: e.activation(out=scb[:], in_=c2[:], func=AF.Silu), reads=["c2"], writes=["scb"])
            for nb in range(12):
                a = aw[nb % 2]
                ak = "aw%d" % (nb % 2)
                P.dma(POOL, lambda e, a=a, nb=nb: e.dma_start(
                    out=a[:], in_=ada_w[:, nb * 512:(nb + 1) * 512].rearrange("(k p) n -> p k n", p=128)),
                    writes=[ak], slot=ak)
                pmb = pm[nb % 2]
                pk = "pm%d" % (nb % 2)
                for k in range(NCH):
                    P.op(PE, lambda e, a=a, k=k, pmb=pmb: e.matmul(pmb[0:2, :], lhsT=scb[:, k, :], rhs=a[:, k, :],
                                                                 start=(k == 0), stop=(k == NCH - 1)),
                         reads=["scb", ak], writes=[pk])
                P.op(DVE, lambda e, nb=nb, pmb=pmb: e.tensor_tensor(out=modsb[:, nb * 512:(nb + 1) * 512], in0=pmb[0:2, :],
                                                                   in1=adab[:, nb * 512:(nb + 1) * 512], op=ALU.add),
                     reads=[pk, "adab"], writes=["modsb"])
            P.dma(SP, lambda e: e.dma_start(out=mod_d[:, :], in_=modsb[:]), reads=["modsb"], writes=["mod_d"], slot="mod_d")
            id2 = SB(st, "id2", [2, 2], F32)
            pmf = PS(st, "pmf")
            P.dma(SP, lambda e: e.dma_start(out=id2[:], in_=ident_d[0:2, 0:2]), writes=["id2"], slot="id2")
            P.dma(SP, lambda e: e.dma_start(out=nmt[:], in_=nm_fm[:, :, :]), writes=["nmt"], slot="nmt")
            cols_ = [1, 0, 4, 3]
            for qi, q in enumerate(cols_):
                for k in range(NCH):
                    j = qi * NCH + k
                    P.op(PE, lambda e, q=q, k=k, j=j: e.transpose(out=pmf[:, j * 2:(j + 1) * 2],
                                                                  in_=modsb[0:2, q * D + k * 128:q * D + (k + 1) * 128], identity=id2[:]),
                         reads=["modsb", "id2"], writes=["pmf"])
            P.op(DVE, lambda e: e.tensor_copy(out=modfm[:], in_=pmf[:, 0:64].rearrange("p (q k r) -> p r q k", q=4, k=NCH, r=2)),
                 reads=["pmf"], writes=["modfm"])
            for r in range(2):
                for (qi, ni) in ((0, 0), (2, 1)):
                    P.op(DVE, lambda e, r=r, qi=qi, ni=ni: e.scalar_tensor_tensor(
                        out=modfm[:, r, qi, :], in0=modfm[:, r, qi, :], scalar=1.0, in1=nmt[:, ni, :],
                        op0=ALU.add, op1=ALU.mult), reads=["modfm", "nmt"], writes=["modfm"])
            P.dma(POOL, lambda e: e.dma_start(out=wfb[:], in_=w_fnet.rearrange("g c d -> c g d")), writes=["wfb"], slot="wfb")
            P.dma(POOL, lambda e: e.dma_start(out=csb[:], in_=cs_d[:, :]), writes=["csb"], slot="csb")
            P.dma(POOL, lambda e: e.dma_start(out=wub[:], in_=w_in[:, 0:512].rearrange("(k p) n -> p k n", p=128)),
                  writes=["wub"], slot="wub")
            for g in range(4):
                for half in range(2):
                    P.op(PE, lambda e, g=g, half=half: e.matmul(pw[:, 0:128], lhsT=csb[:, half * 128:(half + 1) * 128],
                                                              rhs=wfb[:, g, :], start=True, stop=True),
                         reads=["csb", "wfb"], writes=["pw"])
                    P.op(DVE, lambda e, g=g, half=half: e.tensor_copy(out=wcws[:, g, half * 128:(half + 1) * 128], in_=pw[:, 0:128]),
                         reads=["pw"], writes=["wcws"])
            for k in range(NCH):
                ptb = pt[k % 2]
                pk = "ptb%d" % (k % 2)
                ptv = ptb[:, :].bitcast(BF16)
                for g in range(4):
                    P.op(PE, lambda e, k=k, g=g, ptv=ptv: e.transpose(out=ptv[:, g * 128:(g + 1) * 128],
                                                                    in_=wub[:, k, g * 128:(g + 1) * 128], identity=ident[:]),
                         reads=["wub", "ident"], writes=[pk])
                P.op(ACT, lambda e, k=k, ptv=ptv: e.activation(out=wuT[:, :, k, :],
                                                              in_=ptv[:, 0:512].rearrange("p (g d) -> p g d", g=4), func=AF.Copy),
                     reads=[pk], writes=["wuT"])
            for k in range(NCH):
                for gp in range(2):
                    pb = pab[(k * 2 + gp) % 2]
                    pk = "pab%d" % ((k * 2 + gp) % 2)
                    for gi in range(2):
                        g = gp * 2 + gi
                        P.op(PE, lambda e, k=k, g=g, gi=gi, pb=pb: e.matmul(pb[:, gi * 256:(gi + 1) * 256], lhsT=wuT[:, g, k, :],
                                                                           rhs=wcws[:, g, :], start=True, stop=True),
                             reads=["wuT", "wcws"], writes=[pk])
                    P.op(DVE if gp == 0 else ACT,
                         (lambda e, k=k, gp=gp, pb=pb: e.tensor_copy(out=wabs[:, k, gp * 512:(gp + 1) * 512], in_=pb[:, :])) if gp == 0 else
                         (lambda e, k=k, gp=gp, pb=pb: e.activation(out=wabs[:, k, gp * 512:(gp + 1) * 512], in_=pb[:, :], func=AF.Copy)),
                         reads=[pk], writes=["wabs"])
            P.dma(SP, lambda e: e.dma_start(out=wab_d.rearrange("(k p) n -> p k n", p=128), in_=wabs[:]),
                  reads=["wabs"], writes=["wab_d"], slot="wab_d")
            print("setup", P.emit())

        def ring(st, name, n, shape, dt):
            return [(SB(st, "%s%d" % (name, i), shape, dt), "%s%d" % (name, i)) for i in range(n)]

        def run_pipeline(stage_lists, nstages, order=None):
            n = len(stage_lists)
            order = order or list(range(nstages - 1, -1, -1))
            for t in range(n + nstages - 1):
                for k in order:
                    i = t - k
                    if 0 <= i < n and k < len(stage_lists[i]) and stage_lists[i][k] is not None:
                        stage_lists[i][k]()

        SNAP = {s: dscr("snapd_" + s, [len(seqs[s]["own"]) * 128, 512], BF16) for s in seqs}
        KV = {s: dscr("kvd_" + s, [seqs[s]["Tv"] * 128, 1024], BF16) for s in seqs}
        _lf, _lb = _gammas()
        G128 = [float(np.exp(_lf[h] * 128.0)) for h in range(4)] + [float(np.exp(_lb[h] * 128.0)) for h in range(4)]

        def state_update(P, S8, half, Bsrc, Bk, skey):
            for h in range(4):
                hh = half * 4 + h
                P.op(DVE, lambda e, h=h, hh=hh: e.scalar_tensor_tensor(out=S8[:, hh, :], in0=S8[:, hh, :], scalar=G128[hh],
                                                                      in1=Bsrc[:, h * 128:(h + 1) * 128], op0=ALU.mult, op1=ALU.add),
                     reads=["%s_%d" % (skey, h), Bk], writes=["%s_%d" % (skey, h)])

        with contextlib.ExitStack() as ms:
            S8 = SB(ms, "S8", [128, 8, 128], F32)
            g8 = SB(ms, "g8", [128, 8], F32)
            maskT = SB(ms, "maskT", [128, 4, 128], F32)
            stat = SB(ms, "stat", [128, 3, 8], F32)
            junk = SB(ms, "junk", [128, D], BF16)

            def load_mod_tables(P, modfm=modfm, nmt=nmt):
                P.dma(SP, lambda e: e.dma_start(out=nmt[:], in_=nm_fm[:, :, :]), writes=["nmt"], slot="nmt")
                cols = [1, 0, 4, 3]
                for r in range(2):
                    for qi, q in enumerate(cols):
                        P.dma(SP, lambda e, r=r, qi=qi, q=q: e.dma_start(
                            out=modfm[:, r, qi, :], in_=mod_d[r, q * D:(q + 1) * D].rearrange("(k p) -> p k", p=128),
                            allow_slow_non_contiguous=True),
                            reads=["mod_d"], writes=["modfm"], slot="modfm")
                for r in range(2):
                    for (qi, ni) in ((0, 0), (2, 1)):
                        P.op(DVE, lambda e, r=r, qi=qi, ni=ni: e.scalar_tensor_tensor(
                            out=modfm[:, r, qi, :], in0=modfm[:, r, qi, :], scalar=1.0, in1=nmt[:, ni, :],
                            op0=ALU.add, op1=ALU.mult), reads=["modfm", "nmt"], writes=["modfm"])

            def norm_a(P, src_ap, src_key, xn, xnk, col, stat=stat, junk=junk, out_dt_scale=None):
                ssc, msc, rsc = stat[:, 0, col:col + 1], stat[:, 1, col:col + 1], stat[:, 2, col:col + 1]
                P.op(ACT, lambda e: e.activation(out=junk[:], in_=src_ap, func=AF.Square, accum_out=ssc),
                     reads=[src_key], writes=["ss%d" % col])
                P.op(DVE, lambda e: e.tensor_scalar(out=msc, in0=ssc, scalar1=1.0 / D, scalar2=EPS, op0=ALU.mult, op1=ALU.add),
                     reads=["ss%d" % col], writes=["ms%d" % col])
                P.op(POOL, lambda e: e.tensor_tensor(out=rsc, in0=msc, in1=mh[:, 0:1], op=ALU.pow),
                     reads=["ms%d" % col, "mh"], writes=["rs%d" % col])
                P.op(ACT, lambda e: e.activation(out=xn, in_=src_ap, func=AF.Identity, scale=rsc),
                     reads=[src_key, "rs%d" % col], writes=[xnk])

            def trans_b(P, xn, xnk, banks, r, qs, dst_fn, dstk, modfm=modfm):
                per = NCH // len(banks)
                for c in range(NCH):
                    Bt, Btk, _ = banks[c // per]
                    bv = Bt[:, :].bitcast(BF16)
                    j = c % per
                    P.op(PE, lambda e, c=c, j=j, bv=bv: e.transpose(out=bv[:, j * 128:(j + 1) * 128], in_=xn[:, c * 128:(c + 1) * 128],
                                                                  identity=ident[:]), reads=[xnk, "ident"], writes=[Btk])
                for c in range(NCH):
                    Bt, Btk, eng = banks[c // per]
                    bv = Bt[:, :].bitcast(BF16)
                    j = c % per
                    if eng == ACT:
                        P.op(ACT, lambda e, c=c, j=j, bv=bv: e.activation(out=dst_fn(c), in_=bv[:, j * 128:(j + 1) * 128], func=AF.Identity,
                                                                        scale=modfm[:, r, qs, c:c + 1], bias=modfm[:, r, qs + 1, c:c + 1]),
                             reads=[Btk, "modfm"], writes=[dstk(c)])
                    else:
                        P.op(DVE, lambda e, c=c, j=j, bv=bv: e.tensor_scalar(out=dst_fn(c), in0=bv[:, j * 128:(j + 1) * 128],
                                                                           scalar1=modfm[:, r, qs, c:c + 1], scalar2=modfm[:, r, qs + 1, c:c + 1],
                                                                           op0=ALU.mult, op1=ALU.add),
                             reads=[Btk, "modfm"], writes=[dstk(c)])

            def rotary(P, Bk, Bkk, tabt, tabk, dst, dstk, tmp):
                (ta, tak), (tb, tbk_) = tmp[0], tmp[1]
                v = Bk[:, :].rearrange("p (h t d) -> p h t d", h=4, t=2)
                cc = tabt[:, 0:128].rearrange("p (t d) -> p t d", t=2).unsqueeze(1).broadcast_to([128, 4, 2, 64])
                sv = tabt[:, 128:256].rearrange("p (t d) -> p t d", t=2).unsqueeze(1).broadcast_to([128, 4, 2, 64])
                P.op(DVE, lambda e: e.tensor_tensor(out=ta[:].rearrange("p h (t d) -> p h t d", t=2), in0=v, in1=cc, op=ALU.mult),
                     reads=[Bkk, tabk], writes=[tak])
                P.op(DVE, lambda e: e.tensor_tensor(out=tb[:].rearrange("p h (t d) -> p h t d", t=2), in0=v[:, :, ::-1, :], in1=sv, op=ALU.mult),
                     reads=[Bkk, tabk], writes=[tbk_])
                P.op(POOL, lambda e: e.tensor_tensor(out=dst[:], in0=ta[:], in1=tb[:], op=ALU.add), reads=[tak, tbk_], writes=[dstk])

            with contextlib.ExitStack() as st:
                P = Phase(ctx, "A1")
                W = SB(st, "W_a1", [128, NCH, 2048], BF16)
                xt = ring(st, "xt", 3, [128, D], F32)
                tabt = ring(st, "tabt", 6, [128, 276], F32)
                xn = ring(st, "xn", 2, [128, D], BF16)
                hT = ring(st, "hT", 2, [128, NCH, 128], BF16)
                pqs = ring(st, "pqs", 2, [128, 1024], BF16)
                kvs = ring(st, "kvs", 2, [128, 1024], BF16)
                tmpr = [ring(st, "tmp%d_" % i, 2, [128, 4, 128], F32) for i in range(2)]
                kr = ring(st, "kr", 2, [128, 4, 128], F32)
                kh = ring(st, "kh", 2, [128, 4, 128], BF16)
                snb = ring(st, "snb", 2, [128, 4, 128], BF16)
                B = [PS(st, "B%d" % i) for i in range(8)]

                P.dma(POOL, lambda e: e.dma_start(out=W[:, :, 0:1024], in_=wab_d.rearrange("(k p) n -> p k n", p=128)),
                      reads=["wab_d"], writes=["W"], slot="W0")
                P.dma(POOL, lambda e: e.dma_start(out=W[:, :, 1024:2048], in_=w_in[:, 1024:2048].rearrange("(k p) n -> p k n", p=128)),
                      writes=["W"], slot="W1")
                P.dma(SP, lambda e: e.dma_start(out=maskT[:], in_=mask_d[:, :, :]), writes=["maskT"], slot="maskT")
                stages = []
                cnt = 0
                for s in ("p", "s"):
                    sq = seqs[s]
                    r = sq["row"]
                    own_pos = {v: m for m, v in enumerate(sq["own"])}
                    for v in range(sq["Tv"] - 1, -1, -1):
                        i = cnt
                        cnt += 1
                        first = (v == sq["Tv"] - 1)

                        def sl(i=i, s=s, v=v):
                            x_t, xk = xt[i % 3]
                            tb_t, tbk = tabt[i % 6]
                            P.dma(SP, lambda e: e.dma_start(out=x_t[:], in_=X[s][v * 128:(v + 1) * 128, :]), writes=[xk], slot=xk)
                            P.dma(SP, lambda e: e.dma_start(out=tb_t[:], in_=TAB[s][v * 128:(v + 1) * 128, :]), writes=[tbk], slot=tbk)

                        def s0(i=i, s=s, v=v):
                            x_t, xk = xt[i % 3]
                            norm_a(P, x_t[:], xk, xn[i % 2][0][:], xn[i % 2][1], i % 8)

                        def s1(i=i, r=r):
                            h_t, hk = hT[i % 2]
                            trans_b(P, xn[i % 2][0], xn[i % 2][1], [(B[0], "B0", ACT), (B[1], "B1", DVE)], r, 0, lambda c: h_t[:, c, :],
                                    lambda c: "%s_%d" % (hk, c))

                        def s2(i=i, s=s, v=v):
                            h_t, hk = hT[i % 2]
                            tb_t, tbk = tabt[i % 6]
                            pq_t, pqk = pqs[i % 2]
                            for half in range(2):
                                for k in range(NCH):
                                    P.op(PE, lambda e, half=half, k=k: e.matmul(B[2 + half][:, :], lhsT=h_t[:, k, :],
                                                                               rhs=W[:, k, half * 512:(half + 1) * 512],
                                                                               start=(k == 0), stop=(k == NCH - 1)),
                                         reads=["%s_%d" % (hk, c_) for c_ in range(NCH)] + ["W"], writes=["B%d" % (2 + half)])
                            P.op(ACT, lambda e: e.activation(out=pq_t[:, 0:512], in_=B[2][:, :], func=AF.Identity, scale=tb_t[:, 272:273]),
                                 reads=["B2", tbk], writes=[pqk + "_a"])
                            P.op(DVE, lambda e: e.tensor_scalar(out=pq_t[:, 512:1024], in0=B[3][:, :], scalar1=tb_t[:, 272:273],
                                                                scalar2=None, op0=ALU.mult), reads=["B3", tbk], writes=[pqk + "_b"])
                            P.dma(SP, lambda e: e.dma_start(out=PQ[s][v * 128:(v + 1) * 128, :], in_=pq_t[:]),
                                  reads=[pqk + "_a", pqk + "_b"], writes=["PQ_" + s], slot="st_" + pqk)
                            bk = 4 if i % 2 == 0 else 7
                            for (bi, c0) in ((bk, 1024), (5, 1536)):
                                for k in range(NCH):
                                    P.op(PE, lambda e, bi=bi, c0=c0, k=k: e.matmul(B[bi][:, :], lhsT=h_t[:, k, :], rhs=W[:, k, c0:c0 + 512],
                                                                                  start=(k == 0), stop=(k == NCH - 1)),
                                         reads=["%s_%d" % (hk, c_) for c_ in range(NCH)] + ["W"], writes=["B%d" % bi])
                            v_t, vk = kvs[i % 2]
                            P.op(ACT, lambda e: e.activation(out=v_t[:, 512:1024], in_=B[5][:, :], func=AF.Copy), reads=["B5"], writes=[vk + "_v"])
                            kr_t, krk = kr[i % 2]
                            rotary(P, B[bk], "B%d" % bk, tb_t, tbk, kr_t, krk, tmpr[i % 2])
                            kh_t, khk = kh[i % 2]
                            P.op(POOL, lambda e: e.tensor_tensor(out=kh_t[:], in0=kr_t[:],
                                                                 in1=tb_t[:, 268:272].unsqueeze(2).broadcast_to([128, 4, 128]), op=ALU.mult),
                                 reads=[krk, tbk], writes=[khk])
                            P.op(POOL, lambda e: e.tensor_tensor(out=v_t[:, 0:512].rearrange("p (h d) -> p h d", h=4), in0=kr_t[:],
                                                                 in1=tb_t[:, 264:268].unsqueeze(2).broadcast_to([128, 4, 128]), op=ALU.mult),
                                 reads=[krk, tbk], writes=[vk + "_k"])
                            P.dma(SP, lambda e: e.dma_start(out=KV[s][v * 128:(v + 1) * 128, :], in_=v_t[:]),
                                  reads=[vk + "_k", vk + "_v"], writes=["KV_" + s], slot="st_" + vk)

                        def s3(i=i, s=s, v=v, first=first, own_pos=own_pos):
                            kh_t, khk = kh[i % 2]
                            v_t, vk = kvs[i % 2]
                            if first:
                                P.op(POOL, lambda e: e.memset(S8[:, 4:8, :], 0.0), writes=["S8b_%d" % h_ for h_ in range(4)])
                            for h in range(4):
                                P.op(PE, lambda e, h=h: e.matmul(B[6][:, h * 128:(h + 1) * 128], lhsT=kh_t[:, h, :],
                                                                 rhs=v_t[:, 512 + h * 128:512 + (h + 1) * 128], start=True, stop=True),
                                     reads=[khk, vk + "_v"], writes=["B6"])
                            if v in own_pos:
                                m = own_pos[v]
                                sn_t, snk = snb[m % 2]
                                P.op(ACT, lambda e: e.activation(out=sn_t[:], in_=S8[:, 4:8, :], func=AF.Copy), reads=["S8b_%d" % h_ for h_ in range(4)], writes=[snk])
                                P.dma(SP, lambda e: e.dma_start(out=SNAP[s][m * 128:(m + 1) * 128, :], in_=sn_t[:].rearrange("p h d -> p (h d)")),
                                      reads=[snk], writes=["SNAP_" + s], slot="st_" + snk)
                            state_update(P, S8, 1, B[6], "B6", "S8b")

                        stages.append([sl, s0, s1, s2, s3])
                run_pipeline(stages, 5, order=[0, 1, 2, 3, 4])
                print("A1", P.emit())

            for s in ("p", "s"):
                sq = seqs[s]
                T, Tv, NO = sq["T"], sq["Tv"], sq["NO"]
                Ka = T
                Kb = Tv - T
                NB = min(64, 1024 // (2 * T))
                KB2 = max(1, min(T, 512 // NO))
                with contextlib.ExitStack() as st:
                    P = Phase(ctx, "F" + s)
                    Za_r = ring(st, "Za", 2, [Ka, 128, 256], BF16)
                    zbp = ring(st, "zbp", 2, [max(Kb, 1), 4, 256], BF16)
                    f1a = SB(st, "f1a", [Ka, 4 * T], BF16)
                    c2s = SB(st, "c2s", [128, T, 2, NO], BF16)
                    Yp = SB(st, "Yp", [128, 2, 64, T], BF16)
                    f2t = SB(st, "f2t", [64, NO, T], BF16)
                    YB = [st.enter_context(nc.psum_tensor("YB%d_%s" % (i, s), [128, 1024], F32)) for i in range(2)]
                    OB = [PS(st, "OB%d" % i) for i in range(4)]
                    P.dma(POOL, lambda e: e.dma_start(out=f1a[:], in_=F1[s][0:Ka, :]), writes=["f1a"], slot="f1a")
                    cstep = max(1, 1024 // (2 * NO))
                    for t0_ in range(0, T, cstep):
                        t1_ = min(T, t0_ + cstep)
                        P.dma(POOL, lambda e, t0_=t0_, t1_=t1_: e.dma_start(out=c2s[:, t0_:t1_, :, :], in_=C2[s][:, t0_:t1_, :, :]),
                              writes=["c2s"], slot="c2s")

                    def load_group(g):
                        Za, Zak = Za_r[g % 2]
                        for z0 in range(0, Ka, 16):
                            z1 = min(Ka, z0 + 16)
                            P.dma(SP, lambda e, z0=z0, z1=z1: e.dma_start(
                                out=Za[z0:z1, :, :], in_=PQ[s][z0 * 128:z1 * 128, g * 256:(g + 1) * 256].rearrange("(t r) c -> t r c", r=128)),
                                reads=["PQ_" + s], writes=["%s_%d" % (Zak, z0 // 16)], slot=Zak)
                    def fold_group(g):
                        Za, Zak = Za_r[g % 2]
                        if Kb:
                            for pc in range(32):
                                zt, zk = zbp[pc % 2]
                                P.dma(SP, lambda e, pc=pc, zt=zt: e.dma_start(
                                    out=zt[:], in_=PQ[s][Ka * 128:Tv * 128, g * 256:(g + 1) * 256].rearrange(
                                        "(t r) c -> t r c", r=128)[:, pc * 4:(pc + 1) * 4, :]),
                                    reads=["PQ_" + s], writes=[zk], slot=zk)
                                P.op(DVE, lambda e, pc=pc, zt=zt: e.tensor_tensor(out=Za[0:Kb, pc * 4:(pc + 1) * 4, :],
                                                                                 in0=Za[0:Kb, pc * 4:(pc + 1) * 4, :], in1=zt[:], op=ALU.add),
                                     reads=["%s_%d" % (Zak, z_) for z_ in range((Ka + 15) // 16)] + [zk], writes=[Zak + "_0"])

                    ybc = 0
                    obc = 0
                    load_group(0)
                    fold_group(0)
                    for g in range(4):
                        if g + 1 < 4:
                            load_group(g + 1)
                        Za, Zak = Za_r[g % 2]
                        for dh in range(2):
                            for b0 in range(0, 64, NB):
                                yb = YB[ybc % 2]
                                ybk = "YB%d" % (ybc % 2)
                                for i in range(NB):
                                    dp = dh * 64 + b0 + i
                                    for pl in range(2):
                                        P.op(PE, lambda e, pl=pl, dp=dp, i=i, yb=yb, Za=Za: e.matmul(
                                            yb[:, i * 2 * T:(i + 1) * 2 * T], lhsT=Za[:, :, pl * 128 + dp],
                                            rhs=f1a[:, pl * 2 * T:(pl + 1) * 2 * T], start=(pl == 0), stop=(pl == 1)),
                                             reads=["%s_%d" % (Zak, z_) for z_ in range((Ka + 15) // 16)] + ["f1a"], writes=[ybk])
                                yv = yb[:, 0:NB * 2 * T].rearrange("p (n c t) -> p n c t", n=NB, c=2)
                                for c in range(2):
                                    dst = Yp[:, c, b0:b0 + NB, :]
                                    src = yv[:, :, c, :]
                                    if ybc % 2 == 0:
                                        P.op(ACT, lambda e, src=src, dst=dst: e.activation(out=dst, in_=src, func=AF.Copy),
                                             reads=[ybk], writes=["Yp_%d_%d" % (b0 // NB, c)])
                                    else:
                                        P.op(DVE, lambda e, src=src, dst=dst: e.tensor_copy(out=dst, in_=src),
                                             reads=[ybk], writes=["Yp_%d_%d" % (b0 // NB, c)])
                                ybc += 1
                            ypk = ["Yp_%d_%d" % (b, c) for b in range(64 // NB) for c in range(2)]
                            for k0 in range(0, T, KB2):
                                ob = OB[obc % 4]
                                obk = "OB%d" % (obc % 4)
                                obc += 1
                                for kk in range(KB2):
                                    k2 = k0 + kk
                                    for c in range(2):
                                        P.op(PE, lambda e, c=c, k2=k2, kk=kk, ob=ob: e.matmul(
                                            ob[0:64, kk * NO:(kk + 1) * NO], lhsT=Yp[:, c, :, k2], rhs=c2s[:, k2, c, :],
                                            start=(c == 0), stop=(c == 1)), reads=ypk + ["c2s"], writes=[obk])
                                src = ob[0:64, 0:KB2 * NO].rearrange("p (k m) -> p m k", k=KB2)
                                if obc % 2:
                                    P.op(ACT, lambda e, k0=k0, src=src: e.activation(out=f2t[:, :, k0:k0 + KB2], in_=src, func=AF.Copy),
                                         reads=[obk], writes=["f2_%d" % (k0 // KB2)])
                                else:
                                    P.op(DVE, lambda e, k0=k0, src=src: e.tensor_copy(out=f2t[:, :, k0:k0 + KB2], in_=src),
                                         reads=[obk], writes=["f2_%d" % (k0 // KB2)])
                            P.dma(SP, lambda e, g=g, dh=dh: e.dma_start(
                                out=MIX[s][g * 128 + dh * 64:g * 128 + dh * 64 + 64, :], in_=f2t[:].rearrange("p m k -> p (m k)")),
                                reads=["f2_%d" % b for b in range((T + KB2 - 1) // KB2)], writes=["MIX_" + s], slot="st_f2")
                        if g + 1 < 4:
                            fold_group(g + 1)
                    print("F" + s, P.emit())

            with contextlib.ExitStack() as st:
                P = Phase(ctx, "A2")
                W = SB(st, "W_a2", [128, NCH, 1536], BF16)
                Wo = SB(st, "Wo", [128, NCH, D], BF16)
                gt1 = SB(st, "gt1", [128, 2, D], F32)
                kvi = ring(st, "kvi", 8, [128, 1024], BF16)
                xt = ring(st, "xt", 3, [128, D], F32)
                tabt = ring(st, "tabt", 6, [128, 276], F32)
                xn = ring(st, "xn", 2, [128, D], BF16)
                hT = ring(st, "hT", 2, [128, NCH, 128], BF16)
                tmpr = [ring(st, "tmp%d_" % i, 2, [128, 4, 128], F32) for i in range(2)]
                tmpq = [ring(st, "tmq%d_" % i, 2, [128, 4, 128], F32) for i in range(2)]
                kr = ring(st, "kr", 2, [128, 4, 128], F32)
                qr = ring(st, "qr", 2, [128, 4, 128], F32)
                tm = ring(st, "tm", 2, [128, 16, 128], BF16)
                TQ = ring(st, "TQ", 3, [128, 16, 128], BF16)
                sT = ring(st, "sT", 2, [128, 4, 128], BF16)
                Sfb = ring(st, "Sfb", 3, [128, 4, 128], BF16)
                snp = ring(st, "snp", 3, [128, 4, 128], BF16)
                sg = ring(st, "sg", 4, [128, 512], F32)
                t1 = SB(st, "t1", [128, 512], F32)
                rb = ring(st, "rb", 2, [128, 512], BF16)
                rT = ring(st, "rT", 2, [128, 4, 128], BF16)
                fT = ring(st, "fT", 2, [128, 4, 128], BF16)
                ty = SB(st, "ty", [128, D], F32)
                xr = ring(st, "xr", 2, [128, D], F32)
                B = [PS(st, "B%d" % i) for i in range(8)]

                P.dma(POOL, lambda e: e.dma_start(out=W[:, :, 0:1024], in_=w_in[:, 512:1536].rearrange("(k p) n -> p k n", p=128)),
                      writes=["W"], slot="W0")
                P.dma(POOL, lambda e: e.dma_start(out=W[:, :, 1024:1536], in_=w_in[:, 2048:2560].rearrange("(k p) n -> p k n", p=128)),
                      writes=["W"], slot="W1")
                P.dma(POOL, lambda e: e.dma_start(out=Wo[:], in_=w_out.rearrange("(k p) n -> p k n", p=128)), writes=["Wo"], slot="Wo")
                for r in range(2):
                    P.dma(SP, lambda e, r=r: e.dma_start(out=gt1[:, r, :], in_=mod_d[r, 2 * D:3 * D].partition_broadcast(128)),
                          reads=["mod_d"], writes=["gt1"], slot="gt1")
                kvn = ring(st, "kvn", 8, [128, 1024], BF16)
                stages = []
                ocn = 0
                ncn = 0
                for s in ("p", "s"):
                    sq = seqs[s]
                    r = sq["row"]
                    own_pos = {v: m for m, v in enumerate(sq["own"])}
                    items = []
                    pend = []
                    for v in range(sq["Tv"]):
                        if v in own_pos:
                            items.append((pend, v))
                            pend = []
                        else:
                            pend.append(v)
                    if pend:
                        items.append((pend, None))
                    for (others, v) in items:
                        own = v is not None
                        m = own_pos.get(v, -1)
                        oc = ocn
                        if own:
                            ocn += 1
                        i = oc
                        first = (0 in others) or (v == 0)
                        nslots = []
                        for _ in others:
                            nslots.append(ncn % 8)
                            ncn += 1

                        def sl(s=s, v=v, oc=oc):
                            x_t, xk = xt[oc % 3]
                            tb_t, tbk = tabt[oc % 6]
                            P.dma(SP, lambda e: e.dma_start(out=x_t[:], in_=X[s][v * 128:(v + 1) * 128, :]), writes=[xk], slot=xk)
                            P.dma(SP, lambda e: e.dma_start(out=tb_t[:], in_=TAB[s][v * 128:(v + 1) * 128, :]), writes=[tbk], slot=tbk)

                        def s0(oc=oc):
                            x_t, xk = xt[oc % 3]
                            norm_a(P, x_t[:], xk, xn[oc % 2][0][:], xn[oc % 2][1], oc % 4)

                        def s1(oc=oc, r=r):
                            h_t, hk = hT[oc % 2]
                            trans_b(P, xn[oc % 2][0], xn[oc % 2][1], [(B[0], "B0", ACT)], r, 0, lambda c: h_t[:, c, :], lambda c: "%s_%d" % (hk, c))

                        def s2(s=s, v=v, own=own, oc=oc, others=others, nslots=nslots):
                            for ov, ns in zip(others, nslots):
                                kv_t, kvk = kvn[ns]
                                P.dma(SP, lambda e, ov=ov, kv_t=kv_t: e.dma_start(out=kv_t[:], in_=KV[s][ov * 128:(ov + 1) * 128, :]),
                                      reads=["KV_" + s], writes=[kvk], slot=kvk)
                            if not own:
                                return
                            kv_t, kvk = kvi[oc % 8]
                            P.dma(SP, lambda e: e.dma_start(out=kv_t[:], in_=KV[s][v * 128:(v + 1) * 128, :]),
                                  reads=["KV_" + s], writes=[kvk], slot=kvk)
                            h_t, hk = hT[oc % 2]
                            tb_t, tbk = tabt[oc % 6]
                            for (bi, c0) in ((4, 512), (6, 0), (7, 1024)):
                                for k in range(NCH):
                                    P.op(PE, lambda e, bi=bi, c0=c0, k=k: e.matmul(B[bi][:, :], lhsT=h_t[:, k, :], rhs=W[:, k, c0:c0 + 512],
                                                                                  start=(k == 0), stop=(k == NCH - 1)),
                                         reads=["%s_%d" % (hk, c_) for c_ in range(NCH)] + ["W"], writes=["B%d" % bi])
                            kr_t, krk = kr[oc % 2]
                            rotary(P, B[4], "B4", tb_t, tbk, kr_t, krk, tmpr[oc % 2])
                            qr_t, qrk = qr[oc % 2]
                            sg_t, sgk = sg[oc % 4]
                            tm_t, tmk = tm[oc % 2]
                            rotary(P, B[6], "B6", tb_t, tbk, qr_t, qrk, tmpq[oc % 2])
                            P.op(ACT, lambda e: e.activation(out=sg_t[:], in_=B[7][:, :], func=AF.Silu), reads=["B7"], writes=[sgk])
                            P.op(POOL, lambda e: e.tensor_copy(out=tm_t[:, 0:4, :], in_=qr_t[:]), reads=[qrk], writes=[tmk + "_0"])
                            P.op(POOL, lambda e: e.tensor_tensor(out=tm_t[:, 4:8, :], in0=qr_t[:],
                                                                 in1=tb_t[:, 256:260].unsqueeze(2).broadcast_to([128, 4, 128]), op=ALU.mult),
                                 reads=[qrk, tbk], writes=[tmk + "_1"])
                            P.op(POOL, lambda e: e.tensor_tensor(out=tm_t[:, 8:12, :], in0=qr_t[:],
                                                                 in1=tb_t[:, 260:264].unsqueeze(2).broadcast_to([128, 4, 128]), op=ALU.mult),
                                 reads=[qrk, tbk], writes=[tmk + "_2"])
                            P.op(POOL, lambda e: e.tensor_copy(out=tm_t[:, 12:16, :], in_=kr_t[:]), reads=[krk], writes=[tmk + "_3"])

                        def s3(s=s, own=own, oc=oc, m=m, first=first, others=others, nslots=nslots):
                            if first:
                                P.op(POOL, lambda e: e.memset(S8[:, 0:4, :], 0.0), writes=["S8f_%d" % h_ for h_ in range(4)])
                            seq_t = [kvn[ns] for ns in nslots] + ([kvi[oc % 8]] if own else [])
                            for ti_, (kv_t, kvk) in enumerate(seq_t):
                                is_own = own and ti_ == len(seq_t) - 1
                                if is_own:
                                    sf_t, sfk = Sfb[oc % 3]
                                    P.op(ACT, lambda e, sf_t=sf_t: e.activation(out=sf_t[:], in_=S8[:, 0:4, :], func=AF.Copy), reads=["S8f_%d" % h_ for h_ in range(4)], writes=[sfk])
                                for h in range(4):
                                    P.op(PE, lambda e, h=h, kv_t=kv_t: e.matmul(B[5][:, h * 128:(h + 1) * 128], lhsT=kv_t[:, h * 128:(h + 1) * 128],
                                                                             rhs=kv_t[:, 512 + h * 128:512 + (h + 1) * 128], start=True, stop=True),
                                         reads=[kvk, "B5"], writes=["B5"])
                                state_update(P, S8, 0, B[5], "B5", "S8f")
                            if own:
                                tm_t, tmk = tm[oc % 2]
                                tq_t, tqk = TQ[oc % 3]
                                sp_t, spk = snp[oc % 3]
                                P.dma(SP, lambda e: e.dma_start(out=sp_t[:].rearrange("p h d -> p (h d)"), in_=SNAP[s][m * 128:(m + 1) * 128, :]),
                                      reads=["SNAP_" + s], writes=[spk], slot=spk)
                                for half in range(2):
                                    bv = B[2 + half][:, :].bitcast(BF16)
                                    for j in range(8):
                                        P.op(PE, lambda e, half=half, j=j, bv=bv: e.transpose(out=bv[:, j * 128:(j + 1) * 128],
                                                                                            in_=tm_t[:, half * 8 + j, :], identity=ident[:]),
                                             reads=["%s_%d" % (tmk, (half * 8 + j) // 4), "ident"], writes=["B%d" % (2 + half)])
                                P.op(ACT, lambda e: e.activation(out=tq_t[:, 0:8, :], in_=B[2][:, :].bitcast(BF16).rearrange("p (a b) -> p a b", a=8),
                                                                 func=AF.Copy), reads=["B2"], writes=[tqk + "_a"])
                                P.op(DVE, lambda e: e.tensor_copy(out=tq_t[:, 8:16, :], in_=B[3][:, :].bitcast(BF16).rearrange("p (a b) -> p a b", a=8)),
                                     reads=["B3"], writes=[tqk + "_b"])

                        def s4(oc=oc):
                            tq_t, tqk = TQ[oc % 3]
                            st_t, stk = sT[oc % 2]
                            for h in range(4):
                                P.op(PE, lambda e, h=h: e.matmul(B[1][:, h * 128:(h + 1) * 128], lhsT=tq_t[:, 12 + h, :], rhs=tq_t[:, h, :],
                                                                 start=True, stop=True), reads=[tqk + "_a", tqk + "_b"], writes=["B1"])
                            P.op(DVE, lambda e: e.tensor_tensor(out=st_t[:], in0=B[1][:, :].rearrange("p (h i) -> p h i", h=4), in1=maskT[:],
                                                                op=ALU.mult), reads=["B1", "maskT"], writes=[stk])

                        def s5(i=i, oc=oc):
                            tq_t, tqk = TQ[oc % 3]
                            st_t, stk = sT[oc % 2]
                            sf_t, sfk = Sfb[oc % 3]
                            sp_t, spk = snp[oc % 3]
                            kv_t, kvk = kvi[i % 8]
                            sg_t, sgk = sg[oc % 4]
                            rb_t, rbk = rb[oc % 2]
                            for h in range(4):
                                o = B[6][:, h * 128:(h + 1) * 128]
                                P.op(PE, lambda e, h=h, o=o: e.matmul(o, lhsT=st_t[:, h, :], rhs=kv_t[:, 512 + h * 128:512 + (h + 1) * 128],
                                                                      start=True, stop=False), reads=[stk, kvk], writes=["B6"])
                                P.op(PE, lambda e, h=h, o=o: e.matmul(o, lhsT=tq_t[:, 4 + h, :], rhs=sf_t[:, h, :], start=False, stop=False),
                                     reads=[tqk + "_a", sfk], writes=["B6"])
                                P.op(PE, lambda e, h=h, o=o: e.matmul(o, lhsT=tq_t[:, 8 + h, :], rhs=sp_t[:, h, :], start=False, stop=True),
                                     reads=[tqk + "_b", spk], writes=["B6"])
                            for h in range(4):
                                P.op(ACT, lambda e, h=h: e.activation(out=junk[:, 0:128], in_=B[6][:, h * 128:(h + 1) * 128], func=AF.Square,
                                                                      accum_out=stat[:, 0, 4 + h:5 + h]), reads=["B6"], writes=["ssg%d" % h])
                            P.op(DVE, lambda e: e.tensor_scalar(out=stat[:, 1, 4:8], in0=stat[:, 0, 4:8], scalar1=1.0 / 128, scalar2=EPS,
                                                                op0=ALU.mult, op1=ALU.add), reads=["ssg0", "ssg1", "ssg2", "ssg3"], writes=["msg"])
                            P.op(POOL, lambda e: e.tensor_tensor(out=stat[:, 2, 4:8], in0=stat[:, 1, 4:8], in1=mh[:, 0:1].broadcast_to([128, 4]),
                                                                 op=ALU.pow), reads=["msg", "mh"], writes=["rsg"])
                            P.op(DVE, lambda e: e.tensor_tensor(out=t1[:].rearrange("p (h d) -> p h d", h=4),
                                                                in0=B[6][:, :].rearrange("p (h d) -> p h d", h=4),
                                                                in1=stat[:, 2, 4:8].unsqueeze(2).broadcast_to([128, 4, 128]), op=ALU.mult),
                                 reads=["B6", "rsg"], writes=["t1"])
                            P.op(DVE, lambda e: e.tensor_tensor(out=rb_t[:], in0=t1[:], in1=sg_t[:], op=ALU.mult), reads=["t1", sgk], writes=[rbk])

                        def s6(s=s, v=v, oc=oc, m=m):
                            rb_t, rbk = rb[oc % 2]
                            rT_t, rTk = rT[oc % 2]
                            f_t, fk = fT[oc % 2]
                            xr_t, xrk = xr[oc % 2]
                            b1v = B[1][:, :].bitcast(BF16)
                            for h in range(4):
                                P.op(PE, lambda e, h=h: e.transpose(out=b1v[:, h * 128:(h + 1) * 128], in_=rb_t[:, h * 128:(h + 1) * 128],
                                                                    identity=ident[:]), reads=[rbk, "ident"], writes=["B1"])
                            P.op(ACT, lambda e: e.activation(out=rT_t[:], in_=b1v[:, 0:512].rearrange("p (h d) -> p h d", h=4), func=AF.Copy),
                                 reads=["B1"], writes=[rTk])
                            P.dma(SP, lambda e: e.dma_start(out=f_t[:], in_=MIX[s][:, m * 128:(m + 1) * 128].rearrange("(g d) t -> d g t", d=128)),
                                  reads=["MIX_" + s], writes=[fk], slot=fk)
                            P.dma(SP, lambda e: e.dma_start(out=xr_t[:], in_=X[s][v * 128:(v + 1) * 128, :]), writes=[xrk], slot=xrk)

                        def s7(s=s, r=r, oc=oc, m=m):
                            rT_t, rTk = rT[oc % 2]
                            f_t, fk = fT[oc % 2]
                            xr_t, xrk = xr[oc % 2]
                            for half in range(2):
                                for kc in range(NCH):
                                    lhs = f_t[:, kc, :] if kc < 4 else rT_t[:, kc - 4, :]
                                    P.op(PE, lambda e, half=half, kc=kc, lhs=lhs: e.matmul(B[2 + half][:, :], lhsT=lhs,
                                                                                         rhs=Wo[:, kc, half * 512:(half + 1) * 512],
                                                                                         start=(kc == 0), stop=(kc == NCH - 1)),
                                         reads=[fk, rTk, "Wo"], writes=["B%d" % (2 + half)])
                            for half in range(2):
                                P.op(DVE, lambda e, half=half: e.tensor_tensor(out=ty[:, half * 512:(half + 1) * 512], in0=B[2 + half][:, :],
                                                                              in1=gt1[:, r, half * 512:(half + 1) * 512], op=ALU.mult),
                                     reads=["B%d" % (2 + half), "gt1"], writes=["ty_%d" % half])
                            P.op(POOL, lambda e: e.tensor_tensor(out=xr_t[:], in0=ty[:], in1=xr_t[:], op=ALU.add), reads=["ty_0", "ty_1", xrk], writes=[xrk])
                            P.dma(SP, lambda e: e.dma_start(out=X1[s][m * 128:(m + 1) * 128, :], in_=xr_t[:]),
                                  reads=[xrk], writes=["X1_" + s], slot="st_" + xrk)

                        stages.append([sl, s0, s1, s2, s3, s4, s5, s6, s7] if own else [None, None, None, s2, s3])
                run_pipeline(stages, 9, order=[0, 1, 2, 3, 4, 5, 6, 7, 8])
                print("A2", P.emit())

        with contextlib.ExitStack() as st:
            P = Phase(ctx, "B")
            W1 = SB(st, "W1", [128, NCH, DFF], BF16)
            W2 = SB(st, "W2", [128, 32, D], BF16)
            modfm_b = modfm
            stat_b = SB(st, "stat", [128, 3, 8], F32)
            junk_b = SB(st, "junk", [128, D], BF16)
            gt2 = SB(st, "gt2", [128, D], F32)
            nfb = SB(st, "nfb", [128, D], F32)
            x1a = ring(st, "x1a", 2, [128, D], F32)
            x1c = ring(st, "x1c", 2, [128, D], F32)
            xnb = ring(st, "xnb", 4, [128, D], BF16)
            h2T = ring(st, "h2T", 2, [128, NCH, 256], BF16)
            aT = SB(st, "aT", [128, 32, 256], BF16)
            rl = ring(st, "rl", 2, [128, 256], F32)
            ty = SB(st, "ty", [128, D], F32)
            yo = ring(st, "yo", 2, [128, D], F32)
            B = [PS(st, "B%d" % i) for i in range(8)]

            for q4 in range(4):
                P.dma(POOL, lambda e, q4=q4: e.dma_start(out=W1[:, :, q4 * 1024:(q4 + 1) * 1024],
                                                        in_=w_mlp_in[:, q4 * 1024:(q4 + 1) * 1024].rearrange("(k p) n -> p k n", p=128)),
                      writes=["W1_%d" % q4], slot="W1_%d" % q4)
            for q4 in range(4):
                P.dma(POOL, lambda e, q4=q4: e.dma_start(out=W2[:, q4 * 8:(q4 + 1) * 8, :],
                                                        in_=w_mlp_out[q4 * 1024:(q4 + 1) * 1024, :].rearrange("(k p) n -> p k n", p=128)),
                      writes=["W2_%d" % q4], slot="W2_%d" % q4)
            P.dma(SP, lambda e: e.dma_start(out=nfb[:], in_=nf_bc[:, :]), writes=["nfb"], slot="nfb")
            bcn = 0
            ycn = 0
            for s in ("p", "s"):
                sq = seqs[s]
                r = sq["row"]
                ntile = len(sq["own"])
                P.dma(SP, lambda e, r=r: e.dma_start(out=gt2[:], in_=mod_d[r, 5 * D:6 * D].partition_broadcast(128)),
                      reads=["mod_d"], writes=["gt2"], slot="gt2")
                stages = []
                for blk in range(0, ntile, 2):
                    tiles = list(range(blk, min(blk + 2, ntile)))
                    bi_ = bcn
                    bcn += 1

                    def s0(tiles=tiles, s=s, bi_=bi_):
                        for ti, m in enumerate(tiles):
                            x_t, xk = x1a[ti]
                            xb_ = xnb[2 * (bi_ % 2) + ti]
                            P.dma(SP, lambda e, x_t=x_t, m=m: e.dma_start(out=x_t[:], in_=X1[s][m * 128:(m + 1) * 128, :]),
                                  reads=["X1_" + s], writes=[xk], slot=xk)
                            norm_a(P, x_t[:], xk, xb_[0][:], xb_[1], ti, stat=stat_b, junk=junk_b)

                    def s1(tiles=tiles, bi_=bi_, r=r):
                        h_t, hk = h2T[bi_ % 2]
                        for ti, m in enumerate(tiles):
                            xb_ = xnb[2 * (bi_ % 2) + ti]
                            trans_b(P, xb_[0], xb_[1], [(B[ti], "B%d" % ti, ACT if ti == 0 else DVE)], r, 2,
                                    lambda c, ti=ti: h_t[:, c, ti * 128:(ti + 1) * 128], lambda c, ti=ti: "%s_%d_%d" % (hk, c, ti), modfm=modfm_b)

                    def s2(tiles=tiles, bi_=bi_):
                        h_t, hk = h2T[bi_ % 2]
                        NW = len(tiles) * 128
                        for f in range(32):
                            bi = 2 + (f % 4)
                            bk = "B%d" % bi
                            rl_t, rlk = rl[f % 2]
                            for k in range(NCH):
                                P.op(PE, lambda e, f=f, k=k, bi=bi: e.matmul(B[bi][:, 0:NW], lhsT=W1[:, k, f * 128:(f + 1) * 128],
                                                                            rhs=h_t[:, k, 0:NW], start=(k == 0), stop=(k == NCH - 1)),
                                     reads=["W1_%d" % (f // 8)] + ["%s_%d_%d" % (hk, c_, ti) for ti in range(len(tiles)) for c_ in range(NCH)], writes=[bk])
                            P.op(ACT, lambda e, bi=bi, rl_t=rl_t: e.activation(out=rl_t[:, 0:NW], in_=B[bi][:, 0:NW], func=AF.Relu),
                                 reads=[bk], writes=[rlk])
                            P.op(DVE, lambda e, f=f, bi=bi, rl_t=rl_t: e.tensor_tensor(out=aT[:, f, 0:NW], in0=B[bi][:, 0:NW], in1=rl_t[:, 0:NW],
                                                                                     op=ALU.mult), reads=[bk, rlk], writes=["aT_%d" % f])

                    def s3(tiles=tiles, s=s):
                        nonlocal ycn
                        for ti, m in enumerate(tiles):
                            x_t, xk = x1c[ti]
                            P.dma(SP, lambda e, x_t=x_t, m=m: e.dma_start(out=x_t[:], in_=X1[s][m * 128:(m + 1) * 128, :]),
                                  reads=["X1_" + s], writes=[xk], slot=xk)
                            for half in range(2):
                                bi = 6 + half
                                for f in range(32):
                                    P.op(PE, lambda e, f=f, ti=ti, half=half, bi=bi: e.matmul(B[bi][:, :], lhsT=aT[:, f, ti * 128:(ti + 1) * 128],
                                                                                             rhs=W2[:, f, half * 512:(half + 1) * 512],
                                                                                             start=(f == 0), stop=(f == 31)),
                                         reads=["aT_%d" % f, "W2_%d" % (f // 8)], writes=["B%d" % bi])
                                P.op(DVE, lambda e, half=half, bi=bi: e.tensor_tensor(out=ty[:, half * 512:(half + 1) * 512], in0=B[bi][:, :],
                                                                                     in1=gt2[:, half * 512:(half + 1) * 512], op=ALU.mult),
                                     reads=["B%d" % bi, "gt2"], writes=["ty_%d" % half])
                            P.op(DVE, lambda e, x_t=x_t: e.tensor_tensor(out=ty[:], in0=ty[:], in1=x_t[:], op=ALU.add),
                                 reads=["ty_0", "ty_1", xk], writes=["ty_0", "ty_1"])
                            y_t, yk = yo[ycn % 2]
                            ycn += 1
                            col = 2 + ti
                            ssc, msc, rsc = stat_b[:, 0, col:col + 1], stat_b[:, 1, col:col + 1], stat_b[:, 2, col:col + 1]
                            P.op(ACT, lambda e, ssc=ssc: e.activation(out=junk_b[:], in_=ty[:], func=AF.Square, accum_out=ssc),
                                 reads=["ty_0", "ty_1"], writes=["ss%d" % col])
                            P.op(DVE, lambda e, ssc=ssc, msc=msc: e.tensor_scalar(out=msc, in0=ssc, scalar1=1.0 / D, scalar2=EPS,
                                                                                 op0=ALU.mult, op1=ALU.add),
                                 reads=["ss%d" % col], writes=["ms%d" % col])
                            P.op(POOL, lambda e, msc=msc, rsc=rsc: e.tensor_tensor(out=rsc, in0=msc, in1=mh[:, 0:1], op=ALU.pow),
                                 reads=["ms%d" % col, "mh"], writes=["rs%d" % col])
                            P.op(DVE, lambda e, y_t=y_t, rsc=rsc: e.scalar_tensor_tensor(out=y_t[:], in0=ty[:], scalar=rsc, in1=nfb[:],
                                                                                        op0=ALU.mult, op1=ALU.mult),
                                 reads=["ty_0", "ty_1", "rs%d" % col, "nfb"], writes=[yk])
                            P.dma(SP, lambda e, y_t=y_t, m=m: e.dma_start(out=Y[s][m * 128:(m + 1) * 128, :], in_=y_t[:]),
                                  reads=[yk], writes=["Y_" + s], slot="st_" + yk)

                    stages.append([s0, s1, s2, s3])
                run_pipeline(stages, 4, order=[0, 1, 3, 2])
            print("B", P.emit())
    return nc


def make_in_maps(inputs, Tp, Ts):
    f = lambda a: np.ascontiguousarray(np.asarray(a, dtype=np.float32))
    xp, xs = f(inputs["x_prompt"]), f(inputs["x_sample"])
    cp, cs = f(inputs["c_prompt"]), f(inputs["c_sample"])
    NOWN = Tp // 4
    Tvp = Tp + 3
    maskT, g8 = _mask_tables()
    idx = np.arange(128)
    cs128 = np.concatenate([np.cos(2 * np.pi * np.outer(idx, idx) / 128), np.sin(2 * np.pi * np.outer(idx, idx) / 128)], axis=1)
    cs128 = (cs128 / np.sqrt(128.0)).astype(np.float32)
    nm_fm = np.stack([f(inputs["norm_mix"])[0].reshape(NCH, 128).T, f(inputs["norm_mlp"])[0].reshape(NCH, 128).T], axis=1)
    nf_bc = np.ascontiguousarray(np.broadcast_to(f(inputs["norm_final"])[None, :], (128, D)))
    ada_b2 = np.ascontiguousarray(np.broadcast_to(f(inputs["ada_b"])[0][None, :], (2, 6 * D)))
    tab_s = _tile_tables([t * 128 for t in range(Ts)], [1.0] * Ts)
    f1_s, c2_s = _fft_tables(Ts, list(range(Ts)), list(range(128)))
    shared = dict(
        ada_w=f(inputs["ada_w"])[0], ada_b2=ada_b2, nm_fm=np.ascontiguousarray(nm_fm), nf_bc=nf_bc,
        w_in=f(inputs["w_in"])[0], w_fnet=f(inputs["w_fnet"])[0], w_out=f(inputs["w_out"])[0],
        w_mlp_in=f(inputs["w_mlp_in"])[0], w_mlp_out=f(inputs["w_mlp_out"])[0],
        ident=np.eye(128, dtype=np.float32), cs128=cs128, maskT=maskT, g8=g8,
        tab_s=tab_s, f1_s=f1_s, c2_s=c2_s,
    )
    per_j = {}
    for j in range(4):
        pad = 3 - j
        vmap = [-1] * pad + list(range(Tp)) + [-1] * j
        tab_p = _tile_tables([max(t, 0) * 128 for t in vmap], [1.0 if t >= 0 else 0.0 for t in vmap])
        kpt = 128 // Tp
        f1_p, c2_p = _fft_tables(Tp, vmap, [(4 * m + j) * kpt + i for m in range(NOWN) for i in range(kpt)])
        per_j[j] = dict(tab_p=tab_p, f1_p=f1_p, c2_p=c2_p)
    xv_cache = {}
    in_maps = []
    for core in range(8):
        b, j = core // 4, core % 4
        if (b, j) not in xv_cache:
            xv = np.zeros((Tvp * 128, D), np.float32)
            xv[(3 - j) * 128:(3 - j) * 128 + Tp * 128] = xp[b]
            xv_cache[(b, j)] = xv
        c2T = np.stack([cp[b].reshape(NCH, 128).T, cs[core].reshape(NCH, 128).T], axis=2)
        m = dict(shared)
        m.update(per_j[j])
        m.update(xv=xv_cache[(b, j)], xs=xs[core], c2T=np.ascontiguousarray(c2T))
        in_maps.append(m)
    return in_maps


_NC_CACHE = {}


def run(inputs, Tp, Ts):
    key = (Tp, Ts)
    if key not in _NC_CACHE:
        _NC_CACHE[key] = build(Tp, Ts)
    nc = _NC_CACHE[key]
    in_maps = make_in_maps(inputs, Tp, Ts)
    res = run_bass_kernel_spmd(nc, in_maps, core_ids=list(range(8)))
    NOWN = Tp // 4
    B = inputs["x_prompt"].shape[0]
    yp = np.zeros((B, Tp * 128, D), np.float32)
    ys = np.zeros((8, Ts * 128, D), np.float32)
    for core in range(8):
        b, j = core // 4, core % 4
        o = np.asarray(res.results[core]["yp"]).reshape(NOWN, 128, D)
        yp[b].reshape(Tp, 128, D)[j::4] = o
        ys[core] = np.asarray(res.results[core]["ys"])
    return yp, ys


def kernel(x_prompt, x_sample, c_prompt, c_sample, ada_w, ada_b, norm_mix, w_in, w_fnet, w_out,
           norm_mlp, w_mlp_in, w_mlp_out, norm_final):
    inputs = dict(x_prompt=x_prompt, x_sample=x_sample, c_prompt=c_prompt, c_sample=c_sample, ada_w=ada_w,
                  ada_b=ada_b, norm_mix=norm_mix, w_in=w_in, w_fnet=w_fnet, w_out=w_out, norm_mlp=norm_mlp,
                  w_mlp_in=w_mlp_in, w_mlp_out=w_mlp_out, norm_final=norm_final)
    Tp = np.asarray(x_prompt).shape[1] // 128
    Ts = np.asarray(x_sample).shape[1] // 128
    yp, ys = run(inputs, Tp, Ts)
    return (yp, ys)
```

```python
import contextlib
import os as _os
import re as _re
import numpy as np
import concourse.bass as bass
import concourse.mybir as mybir
from concourse.bass_utils import run_bass_kernel_spmd

F32 = mybir.dt.float32
BF16 = mybir.dt.bfloat16
AF = mybir.ActivationFunctionType
ALU = mybir.AluOpType

D = 1024
NCH = 8
DFF = 4096
EPS = 1e-6
_COARSE = bool(_os.environ.get("K_COARSE"))
PE, ACT, DVE, POOL, SP = "pe", "act", "dve", "pool", "sp"
COMPUTE = (PE, ACT, DVE, POOL)


class Op:
    __slots__ = ("eng", "fn", "deps", "is_dma", "slot", "needs_inc", "tok")


class Ctx:
    def __init__(self, nc, stack, n_dma_sems=80):
        self.nc = nc
        self.eng_sem = {e: stack.enter_context(nc.semaphore("sem_" + e)) for e in COMPUTE}
        self.eng_cnt = {e: 0 for e in COMPUTE}
        self.dma_sems = [stack.enter_context(nc.semaphore("semd%d" % i)) for i in range(n_dma_sems)]
        self.dma_cnt = [0] * n_dma_sems
        self.slot_map = {}
        self.last_writer = {}
        self.readers = {}
        self.waited = {e: {} for e in (PE, ACT, DVE, POOL, SP)}

    def slot_id(self, key):
        if key not in self.slot_map:
            assert len(self.slot_map) < len(self.dma_sems), "out of dma sems"
            self.slot_map[key] = len(self.slot_map)
        return self.slot_map[key]


class Phase:
    def __init__(self, ctx, name):
        self.ctx = ctx
        self.name = name
        self.ops = []

    def _add(self, eng, fn, reads, writes, is_dma=False, slot=None):
        c = self.ctx
        if _COARSE:
            pat = _os.environ.get("K_COARSE")
            reads = [_re.sub(pat, "", k) for k in reads]
            writes = [_re.sub(pat, "", k) for k in writes]
        op = Op()
        op.eng, op.fn, op.is_dma, op.slot = eng, fn, is_dma, slot
        op.needs_inc = is_dma
        op.tok = None
        deps = []
        for b in reads:
            w = c.last_writer.get(b)
            if w is not None:
                deps.append(w)
        for b in writes:
            w = c.last_writer.get(b)
            if w is not None:
                deps.append(w)
            last = {}
            for rd in c.readers.get(b, ()):
                if rd.is_dma:
                    deps.append(rd)
                else:
                    last[rd.eng] = rd
            deps.extend(last.values())
        op.deps = deps
        for b in reads:
            c.readers.setdefault(b, []).append(op)
        for b in writes:
            c.last_writer[b] = op
            c.readers[b] = []
        self.ops.append(op)
        return op

    def op(self, eng, fn, reads=(), writes=()):
        return self._add(eng, fn, reads, writes)

    def dma(self, queue, fn, reads=(), writes=(), slot=None):
        return self._add(queue, fn, reads, writes, is_dma=True, slot=self.ctx.slot_id(slot))

    def emit(self):
        c = self.ctx
        nc = c.nc
        for op in self.ops:
            for d in op.deps:
                if d.is_dma:
                    continue
                if d.eng == PE and op.eng == PE and not op.is_dma:
                    continue
                d.needs_inc = True
        for op in self.ops:
            if op.is_dma:
                c.dma_cnt[op.slot] += 16
                op.tok = (c.dma_sems[op.slot], c.dma_cnt[op.slot], ("d", op.slot))
            elif op.needs_inc:
                c.eng_cnt[op.eng] += 1
                op.tok = (c.eng_sem[op.eng], c.eng_cnt[op.eng], ("e", op.eng))
        per_eng = {e: [] for e in (PE, ACT, DVE, POOL, SP)}
        last_dma_tok = {}
        for op in self.ops:
            waits = {}
            for d in op.deps:
                if d.tok is None:
                    continue
                if (not d.is_dma) and d.eng == PE and op.eng == PE and not op.is_dma:
                    continue
                sem, val, key = d.tok
                if c.waited[op.eng].get(key, 0) >= val:
                    continue
                if key not in waits or waits[key][1] < val:
                    waits[key] = (sem, val)
            for key, (sem, val) in waits.items():
                c.waited[op.eng][key] = val
            per_eng[op.eng].append((op, list(waits.values())))
            if op.is_dma:
                last_dma_tok[op.tok[2]] = op.tok
        final_waits = []
        for key, (sem, val, _) in last_dma_tok.items():
            if c.waited[SP].get(key, 0) < val:
                c.waited[SP][key] = val
                final_waits.append((sem, val))
        with nc.Block() as block:
            def run(eng_name):
                def body(e):
                    for op, waits in per_eng[eng_name]:
                        for sem, val in waits:
                            e.wait_ge(sem, val)
                        ins = op.fn(e)
                        if op.is_dma:
                            ins.then_inc(op.tok[0], 16)
                        elif op.needs_inc:
                            ins.then_inc(op.tok[0], 1)
                    if eng_name == SP:
                        for sem, val in final_waits:
                            e.wait_ge(sem, val)
                return body
            block.sync(run(SP))
            if per_eng[ACT]:
                block.scalar(run(ACT))
            if per_eng[DVE]:
                block.vector(run(DVE))
            if per_eng[POOL]:
                block.gpsimd(run(POOL))
            if per_eng[PE]:
                block.tensor(run(PE))
        self.ops = []
        return {e: len(v) for e, v in per_eng.items()}


def _gammas():
    h = np.arange(4, dtype=np.float64)
    lf = np.log1p(-np.exp2(-5.0 - 0.0 - h))
    lb = np.log1p(-np.exp2(-5.0 - 0.5 - h))
    return lf, lb


def _tile_tables(pos0_list, valid_list):
    lf, lb = _gammas()
    inv = 10000.0 ** (-(np.arange(64, dtype=np.float64) / 64.0))
    p = np.arange(128, dtype=np.float64)
    out = np.zeros((len(pos0_list), 128, 276), np.float32)
    ks = 128.0 ** -0.5
    for i, (pos0, valid) in enumerate(zip(pos0_list, valid_list)):
        pos = (pos0 + np.arange(128)).astype(np.float64)
        ang = pos[:, None] * inv[None, :]
        out[i, :, 0:64] = np.cos(ang)
        out[i, :, 64:128] = np.cos(ang)
        out[i, :, 128:192] = -np.sin(ang)
        out[i, :, 192:256] = np.sin(ang)
        out[i, :, 256:260] = np.exp(lf[None, :] * (p[:, None] + 1.0))
        out[i, :, 260:264] = np.exp(lb[None, :] * (128.0 - p[:, None]))
        out[i, :, 264:268] = ks * np.exp(lf[None, :] * (127.0 - p[:, None])) * valid
        out[i, :, 268:272] = ks * np.exp(lb[None, :] * p[:, None]) * valid
        out[i, :, 272] = valid
    return out.reshape(len(pos0_list) * 128, 276)


def _mask_tables():
    lf, lb = _gammas()
    j = np.arange(128, dtype=np.float64)[:, None]
    i = np.arange(128, dtype=np.float64)[None, :]
    m = np.zeros((128, 4, 128), np.float32)
    for h in range(4):
        f = np.exp(lf[h] * np.maximum(i - j, 0.0))
        b = np.exp(lb[h] * np.maximum(j - i, 0.0))
        m[:, h, :] = (128.0 ** -0.5) * np.where(j <= i, f, b)
    g8 = np.zeros((128, 8), np.float32)
    g8[:, 0:4] = np.exp(lf * 128.0)[None, :]
    g8[:, 4:8] = np.exp(lb * 128.0)[None, :]
    return m, g8


def _fft_tables(T, vmap, k1_list):
    N = 128 * T
    k2 = np.arange(T, dtype=np.float64)
    f1 = np.zeros((T, 4 * T), np.float32)
    for v_, t in enumerate(vmap):
        if t < 0:
            continue
        v = v_ % T
        a = 2 * np.pi * ((t * k2) % T) / T
        f1[v, 0:T] = np.cos(a)
        f1[v, T:2 * T] = -np.sin(a)
        f1[v, 2 * T:3 * T] = -np.sin(a)
        f1[v, 3 * T:4 * T] = -np.cos(a)
    r = np.arange(128, dtype=np.float64)[:, None, None]
    k1 = np.asarray(k1_list, dtype=np.float64)[None, None, :]
    kk = k2[None, :, None]
    ph = (r * (k1 * T + kk)) % N
    al = 2 * np.pi * ph / N
    sc = 1.0 / np.sqrt(N)
    c2k = np.stack([np.cos(al) * sc, np.sin(al) * sc], axis=2).astype(np.float32)
    return f1, np.ascontiguousarray(c2k)


def build(Tp, Ts):
    NOWN = Tp // 4
    Tvp = Tp + 3
    own_v = [3 + 4 * m for m in range(NOWN)]
    seqs = {
        "p": dict(T=Tp, Tv=Tvp, own=own_v, NO=NOWN * 128 // Tp, row=0),
        "s": dict(T=Ts, Tv=Ts, own=list(range(Ts)), NO=128, row=1),
    }
    nc = bass.Bass("TRN2", target_bir_lowering=False)
    din = lambda n, s, d=F32: nc.dram_tensor(n, list(s), d, kind="ExternalInput").ap()
    dout = lambda n, s, d=F32: nc.dram_tensor(n, list(s), d, kind="ExternalOutput").ap()
    dscr = lambda n, s, d: nc.dram_tensor(n, list(s), d, kind="Internal").ap()

    X = {"p": din("xv", [Tvp * 128, D]), "s": din("xs", [Ts * 128, D])}
    TAB = {"p": din("tab_p", [Tvp * 128, 276]), "s": din("tab_s", [Ts * 128, 276])}
    F1 = {"p": din("f1_p", [Tp, 4 * Tp]), "s": din("f1_s", [Ts, 4 * Ts])}
    C2 = {"p": din("c2_p", [128, Tp, 2, NOWN * 128 // Tp]), "s": din("c2_s", [128, Ts, 2, 128])}
    c2T = din("c2T", [128, NCH, 2])
    ada_w = din("ada_w", [D, 6 * D])
    ada_b2 = din("ada_b2", [2, 6 * D])
    nm_fm = din("nm_fm", [128, 2, NCH])
    nf_bc = din("nf_bc", [128, D])
    w_in = din("w_in", [D, 2560])
    w_fnet = din("w_fnet", [4, 128, 128])
    w_out = din("w_out", [D, D])
    w_mlp_in = din("w_mlp_in", [D, DFF])
    w_mlp_out = din("w_mlp_out", [DFF, D])
    ident_d = din("ident", [128, 128])
    cs_d = din("cs128", [128, 256])
    mask_d = din("maskT", [128, 4, 128])
    g8_d = din("g8", [128, 8])
    Y = {"p": dout("yp", [NOWN * 128, D]), "s": dout("ys", [Ts * 128, D])}

    PQ = {"p": dscr("pq_p", [Tvp * 128, 1024], BF16), "s": dscr("pq_s", [Ts * 128, 1024], BF16)}
    MIX = {"p": dscr("mix_p", [512, NOWN * 128], BF16), "s": dscr("mix_s", [512, Ts * 128], BF16)}
    X1 = {"p": dscr("x1_p", [NOWN * 128, D], F32), "s": dscr("x1_s", [Ts * 128, D], F32)}
    wab_d = dscr("wab", [D, 1024], BF16)
    mod_d = dscr("mod", [2, 6 * D], F32)

    with contextlib.ExitStack() as gs:
        ctx = Ctx(nc, gs)
        uid = [0]

        def SB(st, n, s, d):
            uid[0] += 1
            return st.enter_context(nc.sbuf_tensor("%s_u%d" % (n, uid[0]), list(s), d))

        def PS(st, n):
            uid[0] += 1
            return st.enter_context(nc.psum_tensor("%s_u%d" % (n, uid[0]), [128, 512], F32))
        ident = SB(gs, "ident", [128, 128], BF16)
        mh = SB(gs, "mh", [128, 1], F32)
        epsb = SB(gs, "epsb", [128, 1], F32)
        modfm = SB(gs, "modfm", [128, 2, 4, NCH], F32)
        nmt = SB(gs, "nmt", [128, 2, NCH], F32)

        with contextlib.ExitStack() as st:
            P = Phase(ctx, "setup")
            c2 = SB(st, "c2", [128, NCH, 2], F32)
            scb = SB(st, "scb", [128, NCH, 2], BF16)
            adab = SB(st, "adab", [2, 6 * D], F32)
            modsb = SB(st, "modsb", [2, 6 * D], F32)
            aw = [SB(st, "aw%d" % i, [128, NCH, 512], BF16) for i in range(2)]
            wfb = SB(st, "wfb", [128, 4, 128], BF16)
            csb = SB(st, "csb", [128, 256], BF16)
            wcws = SB(st, "wcws", [128, 4, 256], BF16)
            wub = SB(st, "wub", [128, NCH, 512], BF16)
            wuT = SB(st, "wuT", [128, 4, NCH, 128], BF16)
            wabs = SB(st, "wabs", [128, NCH, 1024], BF16)
            pm = [PS(st, "pm%d" % i) for i in range(2)]
            pw = PS(st, "pw")
            pt = [PS(st, "ptb%d" % i) for i in range(2)]
            pab = [PS(st, "pab%d" % i) for i in range(2)]

            P.dma(POOL, lambda e: e.dma_start(out=ident[:], in_=ident_d[:, :]), writes=["ident"], slot="ident")
            P.op(POOL, lambda e: e.memset(mh[:], -0.5), writes=["mh"])
            P.op(POOL, lambda e: e.memset(epsb[:], EPS), writes=["epsb"])
            P.dma(SP, lambda e: e.dma_start(out=c2[:], in_=c2T[:, :, :]), writes=["c2"], slot="c2")
            P.dma(SP, lambda e: e.dma_start(out=adab[:], in_=ada_b2[:, :]), writes=["adab"], slot="adab")
            P.op(ACT, lambda e: e.activation(out=scb[:], in_=c2[:], func=AF.Silu), reads=["c2"], writes=["scb"])
            for nb in range(12):
                a = aw[nb % 2]
                ak = "aw%d" % (nb % 2)
                P.dma(POOL, lambda e, a=a, nb=nb: e.dma_start(
                    out=a[:], in_=ada_w[:, nb * 512:(nb + 1) * 512].rearrange("(k p) n -> p k n", p=128)),
                    writes=[ak], slot=ak)
                pmb = pm[nb % 2]
                pk = "pm%d" % (nb % 2)
                for k in range(NCH):
                    P.op(PE, lambda e, a=a, k=k, pmb=pmb: e.matmul(pmb[0:2, :], lhsT=scb[:, k, :], rhs=a[:, k, :],
                                                                 start=(k == 0), stop=(k == NCH - 1)),
                         reads=["scb", ak], writes=[pk])
                P.op(DVE, lambda e, nb=nb, pmb=pmb: e.tensor_tensor(out=modsb[:, nb * 512:(nb + 1) * 512], in0=pmb[0:2, :],
                                                                   in1=adab[:, nb * 512:(nb + 1) * 512], op=ALU.add),
                     reads=[pk, "adab"], writes=["modsb"])
            P.dma(SP, lambda e: e.dma_start(out=mod_d[:, :], in_=modsb[:]), reads=["modsb"], writes=["mod_d"], slot="mod_d")
            id2 = SB(st, "id2", [2, 2], F32)
            pmf = PS(st, "pmf")
            P.dma(SP, lambda e: e.dma_start(out=id2[:], in_=ident_d[0:2, 0:2]), writes=["id2"], slot="id2")
            P.dma(SP, lambda e: e.dma_start(out=nmt[:], in_=nm_fm[:, :, :]), writes=["nmt"], slot="nmt")
            cols_ = [1, 0, 4, 3]
            for qi, q in enumerate(cols_):
                for k in range(NCH):
                    j = qi * NCH + k
                    P.op(PE, lambda e, q=q, k=k, j=j: e.transpose(out=pmf[:, j * 2:(j + 1) * 2],
                                                                  in_=modsb[0:2, q * D + k * 128:q * D + (k + 1) * 128], identity=id2[:]),
                         reads=["modsb", "id2"], writes=["pmf"])
            P.op(DVE, lambda e: e.tensor_copy(out=modfm[:], in_=pmf[:, 0:64].rearrange("p (q k r) -> p r q k", q=4, k=NCH, r=2)),
                 reads=["pmf"], writes=["modfm"])
            for r in range(2):
                for (qi, ni) in ((0, 0), (2, 1)):
                    P.op(DVE, lambda e, r=r, qi=qi, ni=ni: e.scalar_tensor_tensor(
                        out=modfm[:, r, qi, :], in0=modfm[:, r, qi, :], scalar=1.0, in1=nmt[:, ni, :],
                        op0=ALU.add, op1=ALU.mult), reads=["modfm", "nmt"], writes=["modfm"])
            P.dma(POOL, lambda e: e.dma_start(out=wfb[:], in_=w_fnet.rearrange("g c d -> c g d")), writes=["wfb"], slot="wfb")
            P.dma(POOL, lambda e: e.dma_start(out=csb[:], in_=cs_d[:, :]), writes=["csb"], slot="csb")
            P.dma(POOL, lambda e: e.dma_start(out=wub[:], in_=w_in[:, 0:512].rearrange("(k p) n -> p k n", p=128)),
                  writes=["wub"], slot="wub")
            for g in range(4):
                for half in range(2):
                    P.op(PE, lambda e, g=g, half=half: e.matmul(pw[:, 0:128], lhsT=csb[:, half * 128:(half + 1) * 128],
                                                              rhs=wfb[:, g, :], start=True, stop=True),
                         reads=["csb", "wfb"], writes=["pw"])
                    P.op(DVE, lambda e, g=g, half=half: e.tensor_copy(out=wcws[:, g, half * 128:(half + 1) * 128], in_=pw[:, 0:128]),
                         reads=["pw"], writes=["wcws"])
            for k in range(NCH):
                ptb = pt[k % 2]
                pk = "ptb%d" % (k % 2)
                ptv = ptb[:, :].bitcast(BF16)
                for g in range(4):
                    P.op(PE, lambda e, k=k, g=g, ptv=ptv: e.transpose(out=ptv[:, g * 128:(g + 1) * 128],
                                                                    in_=wub[:, k, g * 128:(g + 1) * 128], identity=ident[:]),
                         reads=["wub", "ident"], writes=[pk])
                P.op(ACT, lambda e, k=k, ptv=ptv: e.activation(out=wuT[:, :, k, :],
                                                              in_=ptv[:, 0:512].rearrange("p (g d) -> p g d", g=4), func=AF.Copy),
                     reads=[pk], writes=["wuT"])
            for k in range(NCH):
                for gp in range(2):
                    pb = pab[(k * 2 + gp) % 2]
                    pk = "pab%d" % ((k * 2 + gp) % 2)
                    for gi in range(2):
                        g = gp * 2 + gi
                        P.op(PE, lambda e, k=k, g=g, gi=gi, pb=pb: e.matmul(pb[:, gi * 256:(gi + 1) * 256], lhsT=wuT[:, g, k, :],
                                                                           rhs=wcws[:, g, :], start=True, stop=True),
                             reads=["wuT", "wcws"], writes=[pk])
                    P.op(DVE if gp == 0 else ACT,
                         (lambda e, k=k, gp=gp, pb=pb: e.tensor_copy(out=wabs[:, k, gp * 512:(gp + 1) * 512], in_=pb[:, :])) if gp == 0 else
                         (lambda e, k=k, gp=gp, pb=pb: e.activation(out=wabs[:, k, gp * 512:(gp + 1) * 512], in_=pb[:, :], func=AF.Copy)),
                         reads=[pk], writes=["wabs"])
            P.dma(SP, lambda e: e.dma_start(out=wab_d.rearrange("(k p) n -> p k n", p=128), in_=wabs[:]),
                  reads=["wabs"], writes=["wab_d"], slot="wab_d")
            print("setup", P.emit())

        def ring(st, name, n, shape, dt):
            return [(SB(st, "%s%d" % (name, i), shape, dt), "%s%d" % (name, i)) for i in range(n)]

        def run_pipeline(stage_lists, nstages, order=None):
            n = len(stage_lists)
            order = order or list(range(nstages - 1, -1, -1))
            for t in range(n + nstages - 1):
                for k in order:
                    i = t - k
                    if 0 <= i < n and k < len(stage_lists[i]) and stage_lists[i][k] is not None:
                        stage_lists[i][k]()

        SNAP = {s: dscr("snapd_" + s, [len(seqs[s]["own"]) * 128, 512], BF16) for s in seqs}
        KV = {s: dscr("kvd_" + s, [seqs[s]["Tv"] * 128, 1024], BF16) for s in seqs}
        _lf, _lb = _gammas()
        G128 = [float(np.exp(_lf[h] * 128.0)) for h in range(4)] + [float(np.exp(_lb[h] * 128.0)) for h in range(4)]

        def state_update(P, S8, half, Bsrc, Bk, skey):
            for h in range(4):
                hh = half * 4 + h
                P.op(DVE, lambda e, h=h, hh=hh: e.scalar_tensor_tensor(out=S8[:, hh, :], in0=S8[:, hh, :], scalar=G128[hh],
                                                                      in1=Bsrc[:, h * 128:(h + 1) * 128], op0=ALU.mult, op1=ALU.add),
                     reads=["%s_%d" % (skey, h), Bk], writes=["%s_%d" % (skey, h)])

        with contextlib.ExitStack() as ms:
            S8 = SB(ms, "S8", [128, 8, 128], F32)
            g8 = SB(ms, "g8", [128, 8], F32)
            maskT = SB(ms, "maskT", [128, 4, 128], F32)
            stat = SB(ms, "stat", [128, 3, 8], F32)
            junk = SB(ms, "junk", [128, D], BF16)

            def load_mod_tables(P, modfm=modfm, nmt=nmt):
                P.dma(SP, lambda e: e.dma_start(out=nmt[:], in_=nm_fm[:, :, :]), writes=["nmt"], slot="nmt")
                cols = [1, 0, 4, 3]
                for r in range(2):
                    for qi, q in enumerate(cols):
                        P.dma(SP, lambda e, r=r, qi=qi, q=q: e.dma_start(
                            out=modfm[:, r, qi, :], in_=mod_d[r, q * D:(q + 1) * D].rearrange("(k p) -> p k", p=128),
                            allow_slow_non_contiguous=True),
                            reads=["mod_d"], writes=["modfm"], slot="modfm")
                for r in range(2):
                    for (qi, ni) in ((0, 0), (2, 1)):
                        P.op(DVE, lambda e, r=r, qi=qi, ni=ni: e.scalar_tensor_tensor(
                            out=modfm[:, r, qi, :], in0=modfm[:, r, qi, :], scalar=1.0, in1=nmt[:, ni, :],
                            op0=ALU.add, op1=ALU.mult), reads=["modfm", "nmt"], writes=["modfm"])

            def norm_a(P, src_ap, src_key, xn, xnk, col, stat=stat, junk=junk, out_dt_scale=None):
                ssc, msc, rsc = stat[:, 0, col:col + 1], stat[:, 1, col:col + 1], stat[:, 2, col:col + 1]
                P.op(ACT, lambda e: e.activation(out=junk[:], in_=src_ap, func=AF.Square, accum_out=ssc),
                     reads=[src_key], writes=["ss%d" % col])
                P.op(DVE, lambda e: e.tensor_scalar(out=msc, in0=ssc, scalar1=1.0 / D, scalar2=EPS, op0=ALU.mult, op1=ALU.add),
                     reads=["ss%d" % col], writes=["ms%d" % col])
                P.op(POOL, lambda e: e.tensor_tensor(out=rsc, in0=msc, in1=mh[:, 0:1], op=ALU.pow),
                     reads=["ms%d" % col, "mh"], writes=["rs%d" % col])
                P.op(ACT, lambda e: e.activation(out=xn, in_=src_ap, func=AF.Identity, scale=rsc),
                     reads=[src_key, "rs%d" % col], writes=[xnk])

            def trans_b(P, xn, xnk, banks, r, qs, dst_fn, dstk, modfm=modfm):
                per = NCH // len(banks)
                for c in range(NCH):
                    Bt, Btk, _ = banks[c // per]
                    bv = Bt[:, :].bitcast(BF16)
                    j = c % per
                    P.op(PE, lambda e, c=c, j=j, bv=bv: e.transpose(out=bv[:, j * 128:(j + 1) * 128], in_=xn[:, c * 128:(c + 1) * 128],
                                                                  identity=ident[:]), reads=[xnk, "ident"], writes=[Btk])
                for c in range(NCH):
                    Bt, Btk, eng = banks[c // per]
                    bv = Bt[:, :].bitcast(BF16)
                    j = c % per
                    if eng == ACT:
                        P.op(ACT, lambda e, c=c, j=j, bv=bv: e.activation(out=dst_fn(c), in_=bv[:, j * 128:(j + 1) * 128], func=AF.Identity,
                                                                        scale=modfm[:, r, qs, c:c + 1], bias=modfm[:, r, qs + 1, c:c + 1]),
                             reads=[Btk, "modfm"], writes=[dstk(c)])
                    else:
                        P.op(DVE, lambda e, c=c, j=j, bv=bv: e.tensor_scalar(out=dst_fn(c), in0=bv[:, j * 128:(j + 1) * 128],
                                                                           scalar1=modfm[:, r, qs, c:c + 1], scalar2=modfm[:, r, qs + 1, c:c + 1],
                                                                           op0=ALU.mult, op1=ALU.add),
                             reads=[Btk, "modfm"], writes=[dstk(c)])

            def rotary(P, Bk, Bkk, tabt, tabk, dst, dstk, tmp):
                (ta, tak), (tb, tbk_) = tmp[0], tmp[1]
                v = Bk[:, :].rearrange("p (h t d) -> p h t d", h=4, t=2)
                cc = tabt[:, 0:128].rearrange("p (t d) -> p t d", t=2).unsqueeze(1).broadcast_to([128, 4, 2, 64])
                sv = tabt[:, 128:256].rearrange("p (t d) -> p t d", t=2).unsqueeze(1).broadcast_to([128, 4, 2, 64])
                P.op(DVE, lambda e: e.tensor_tensor(out=ta[:].rearrange("p h (t d) -> p h t d", t=2), in0=v, in1=cc, op=ALU.mult),
                     reads=[Bkk, tabk], writes=[tak])
                P.op(DVE, lambda e: e.tensor_tensor(out=tb[:].rearrange("p h (t d) -> p h t d", t=2), in0=v[:, :, ::-1, :], in1=sv, op=ALU.mult),
                     reads=[Bkk, tabk], writes=[tbk_])
                P.op(POOL, lambda e: e.tensor_tensor(out=dst[:], in0=ta[:], in1=tb[:], op=ALU.add), reads=[tak, tbk_], writes=[dstk])

            with contextlib.ExitStack() as st:
                P = Phase(ctx, "A1")
                W = SB(st, "W_a1", [128, NCH, 2048], BF16)
                xt = ring(st, "xt", 3, [128, D], F32)
                tabt = ring(st, "tabt", 6, [128, 276], F32)
                xn = ring(st, "xn", 2, [128, D], BF16)
                hT = ring(st, "hT", 2, [128, NCH, 128], BF16)
                pqs = ring(st, "pqs", 2, [128, 1024], BF16)
                kvs = ring(st, "kvs", 2, [128, 1024], BF16)
                tmpr = [ring(st, "tmp%d_" % i, 2, [128, 4, 128], F32) for i in range(2)]
                kr = ring(st, "kr", 2, [128, 4, 128], F32)
                kh = ring(st, "kh", 2, [128, 4, 128], BF16)
                snb = ring(st, "snb", 2, [128, 4, 128], BF16)
                B = [PS(st, "B%d" % i) for i in range(8)]

                P.dma(POOL, lambda e: e.dma_start(out=W[:, :, 0:1024], in_=wab_d.rearrange("(k p) n -> p k n", p=128)),
                      reads=["wab_d"], writes=["W"], slot="W0")
                P.dma(POOL, lambda e: e.dma_start(out=W[:, :, 1024:2048], in_=w_in[:, 1024:2048].rearrange("(k p) n -> p k n", p=128)),
                      writes=["W"], slot="W1")
                P.dma(SP, lambda e: e.dma_start(out=maskT[:], in_=mask_d[:, :, :]), writes=["maskT"], slot="maskT")
                stages = []
                cnt = 0
                for s in ("p", "s"):
                    sq = seqs[s]
                    r = sq["row"]
                    own_pos = {v: m for m, v in enumerate(sq["own"])}
                    for v in range(sq["Tv"] - 1, -1, -1):
                        i = cnt
                        cnt += 1
                        first = (v == sq["Tv"] - 1)

                        def sl(i=i, s=s, v=v):
                            x_t, xk = xt[i % 3]
                            tb_t, tbk = tabt[i % 6]
                            P.dma(SP, lambda e: e.dma_start(out=x_t[:], in_=X[s][v * 128:(v + 1) * 128, :]), writes=[xk], slot=xk)
                            P.dma(SP, lambda e: e.dma_start(out=tb_t[:], in_=TAB[s][v * 128:(v + 1) * 128, :]), writes=[tbk], slot=tbk)

                        def s0(i=i, s=s, v=v):
                            x_t, xk = xt[i % 3]
                            norm_a(P, x_t[:], xk, xn[i % 2][0][:], xn[i % 2][1], i % 8)

                        def s1(i=i, r=r):
                            h_t, hk = hT[i % 2]
                            trans_b(P, xn[i % 2][0], xn[i % 2][1], [(B[0], "B0", ACT), (B[1], "B1", DVE)], r, 0, lambda c: h_t[:, c, :],
                                    lambda c: "%s_%d" % (hk, c))

                        def s2(i=i, s=s, v=v):
                            h_t, hk = hT[i % 2]
                            tb_t, tbk = tabt[i % 6]
                            pq_t, pqk = pqs[i % 2]
                            for half in range(2):
                                for k in range(NCH):
                                    P.op(PE, lambda e, half=half, k=k: e.matmul(B[2 + half][:, :], lhsT=h_t[:, k, :],
                                                                               rhs=W[:, k, half * 512:(half + 1) * 512],
                                                                               start=(k == 0), stop=(k == NCH - 1)),
                                         reads=["%s_%d" % (hk, c_) for c_ in range(NCH)] + ["W"], writes=["B%d" % (2 + half)])
                            P.op(ACT, lambda e: e.activation(out=pq_t[:, 0:512], in_=B[2][:, :], func=AF.Identity, scale=tb_t[:, 272:273]),
                                 reads=["B2", tbk], writes=[pqk + "_a"])
                            P.op(DVE, lambda e: e.tensor_scalar(out=pq_t[:, 512:1024], in0=B[3][:, :], scalar1=tb_t[:, 272:273],
                                                                scalar2=None, op0=ALU.mult), reads=["B3", tbk], writes=[pqk + "_b"])
                            P.dma(SP, lambda e: e.dma_start(out=PQ[s][v * 128:(v + 1) * 128, :], in_=pq_t[:]),
                                  reads=[pqk + "_a", pqk + "_b"], writes=["PQ_" + s], slot="st_" + pqk)
                            bk = 4 if i % 2 == 0 else 7
                            for (bi, c0) in ((bk, 1024), (5, 1536)):
                                for k in range(NCH):
                                    P.op(PE, lambda e, bi=bi, c0=c0, k=k: e.matmul(B[bi][:, :], lhsT=h_t[:, k, :], rhs=W[:, k, c0:c0 + 512],
                                                                                  start=(k == 0), stop=(k == NCH - 1)),
                                         reads=["%s_%d" % (hk, c_) for c_ in range(NCH)] + ["W"], writes=["B%d" % bi])
                            v_t, vk = kvs[i % 2]
                            P.op(ACT, lambda e: e.activation(out=v_t[:, 512:1024], in_=B[5][:, :], func=AF.Copy), reads=["B5"], writes=[vk + "_v"])
                            kr_t, krk = kr[i % 2]
                            rotary(P, B[bk], "B%d" % bk, tb_t, tbk, kr_t, krk, tmpr[i % 2])
                            kh_t, khk = kh[i % 2]
                            P.op(POOL, lambda e: e.tensor_tensor(out=kh_t[:], in0=kr_t[:],
                                                                 in1=tb_t[:, 268:272].unsqueeze(2).broadcast_to([128, 4, 128]), op=ALU.mult),
                                 reads=[krk, tbk], writes=[khk])
                            P.op(POOL, lambda e: e.tensor_tensor(out=v_t[:, 0:512].rearrange("p (h d) -> p h d", h=4), in0=kr_t[:],
                                                                 in1=tb_t[:, 264:268].unsqueeze(2).broadcast_to([128, 4, 128]), op=ALU.mult),
                                 reads=[krk, tbk], writes=[vk + "_k"])
                            P.dma(SP, lambda e: e.dma_start(out=KV[s][v * 128:(v + 1) * 128, :], in_=v_t[:]),
                                  reads=[vk + "_k", vk + "_v"], writes=["KV_" + s], slot="st_" + vk)

                        def s3(i=i, s=s, v=v, first=first, own_pos=own_pos):
                            kh_t, khk = kh[i % 2]
                            v_t, vk = kvs[i % 2]
                            if first:
                                P.op(POOL, lambda e: e.memset(S8[:, 4:8, :], 0.0), writes=["S8b_%d" % h_ for h_ in range(4)])
                            for h in range(4):
                                P.op(PE, lambda e, h=h: e.matmul(B[6][:, h * 128:(h + 1) * 128], lhsT=kh_t[:, h, :],
                                                                 rhs=v_t[:, 512 + h * 128:512 + (h + 1) * 128], start=True, stop=True),
                                     reads=[khk, vk + "_v"], writes=["B6"])
                            if v in own_pos:
                                m = own_pos[v]
                                sn_t, snk = snb[m % 2]
                                P.op(ACT, lambda e: e.activation(out=sn_t[:], in_=S8[:, 4:8, :], func=AF.Copy), reads=["S8b_%d" % h_ for h_ in range(4)], writes=[snk])
                                P.dma(SP, lambda e: e.dma_start(out=SNAP[s][m * 128:(m + 1) * 128, :], in_=sn_t[:].rearrange("p h d -> p (h d)")),
                                      reads=[snk], writes=["SNAP_" + s], slot="st_" + snk)
                            state_update(P, S8, 1, B[6], "B6", "S8b")

                        stages.append([sl, s0, s1, s2, s3])
                run_pipeline(stages, 5, order=[0, 1, 2, 3, 4])
                print("A1", P.emit())

            for s in ("p", "s"):
                sq = seqs[s]
                T, Tv, NO = sq["T"], sq["Tv"], sq["NO"]
                Ka = T
                Kb = Tv - T
                NB = min(64, 1024 // (2 * T))
                KB2 = max(1, min(T, 512 // NO))
                with contextlib.ExitStack() as st:
                    P = Phase(ctx, "F" + s)
                    Za_r = ring(st, "Za", 2, [Ka, 128, 256], BF16)
                    zbp = ring(st, "zbp", 2, [max(Kb, 1), 4, 256], BF16)
                    f1a = SB(st, "f1a", [Ka, 4 * T], BF16)
                    c2s = SB(st, "c2s", [128, T, 2, NO], BF16)
                    Yp = SB(st, "Yp", [128, 2, 64, T], BF16)
                    f2t = SB(st, "f2t", [64, NO, T], BF16)
                    YB = [st.enter_context(nc.psum_tensor("YB%d_%s" % (i, s), [128, 1024], F32)) for i in range(2)]
                    OB = [PS(st, "OB%d" % i) for i in range(4)]
                    P.dma(POOL, lambda e: e.dma_start(out=f1a[:], in_=F1[s][0:Ka, :]), writes=["f1a"], slot="f1a")
                    cstep = max(1, 1024 // (2 * NO))
                    for t0_ in range(0, T, cstep):
                        t1_ = min(T, t0_ + cstep)
                        P.dma(POOL, lambda e, t0_=t0_, t1_=t1_: e.dma_start(out=c2s[:, t0_:t1_, :, :], in_=C2[s][:, t0_:t1_, :, :]),
                              writes=["c2s"], slot="c2s")

                    def load_group(g):
                        Za, Zak = Za_r[g % 2]
                        for z0 in range(0, Ka, 16):
                            z1 = min(Ka, z0 + 16)
                            P.dma(SP, lambda e, z0=z0, z1=z1: e.dma_start(
                                out=Za[z0:z1, :, :], in_=PQ[s][z0 * 128:z1 * 128, g * 256:(g + 1) * 256].rearrange("(t r) c -> t r c", r=128)),
                                reads=["PQ_" + s], writes=["%s_%d" % (Zak, z0 // 16)], slot=Zak)
                    def fold_group(g):
                        Za, Zak = Za_r[g % 2]
                        if Kb:
                            for pc in range(32):
                                zt, zk = zbp[pc % 2]
                                P.dma(SP, lambda e, pc=pc, zt=zt: e.dma_start(
                                    out=zt[:], in_=PQ[s][Ka * 128:Tv * 128, g * 256:(g + 1) * 256].rearrange(
                                        "(t r) c -> t r c", r=128)[:, pc * 4:(pc + 1) * 4, :]),
                                    reads=["PQ_" + s], writes=[zk], slot=zk)
                                P.op(DVE, lambda e, pc=pc, zt=zt: e.tensor_tensor(out=Za[0:Kb, pc * 4:(pc + 1) * 4, :],
                                                                                 in0=Za[0:Kb, pc * 4:(pc + 1) * 4, :], in1=zt[:], op=ALU.add),
                                     reads=["%s_%d" % (Zak, z_) for z_ in range((Ka + 15) // 16)] + [zk], writes=[Zak + "_0"])

                    ybc = 0
                    obc = 0
                    load_group(0)
                    fold_group(0)
                    for g in range(4):
                        if g + 1 < 4:
                            load_group(g + 1)
                        Za, Zak = Za_r[g % 2]
                        for dh in range(2):
                            for b0 in range(0, 64, NB):
                                yb = YB[ybc % 2]
                                ybk = "YB%d" % (ybc % 2)
                                for i in range(NB):
                                    dp = dh * 64 + b0 + i
                                    for pl in range(2):
                                        P.op(PE, lambda e, pl=pl, dp=dp, i=i, yb=yb, Za=Za: e.matmul(
                                            yb[:, i * 2 * T:(i + 1) * 2 * T], lhsT=Za[:, :, pl * 128 + dp],
                                            rhs=f1a[:, pl * 2 * T:(pl + 1) * 2 * T], start=(pl == 0), stop=(pl == 1)),
                                             reads=["%s_%d" % (Zak, z_) for z_ in range((Ka + 15) // 16)] + ["f1a"], writes=[ybk])
                                yv = yb[:, 0:NB * 2 * T].rearrange("p (n c t) -> p n c t", n=NB, c=2)
                                for c in range(2):
                                    dst = Yp[:, c, b0:b0 + NB, :]
                                    src = yv[:, :, c, :]
                                    if ybc % 2 == 0:
                                        P.op(ACT, lambda e, src=src, dst=dst: e.activation(out=dst, in_=src, func=AF.Copy),
                                             reads=[ybk], writes=["Yp_%d_%d" % (b0 // NB, c)])
                                    else:
                                        P.op(DVE, lambda e, src=src, dst=dst: e.tensor_copy(out=dst, in_=src),
                                             reads=[ybk], writes=["Yp_%d_%d" % (b0 // NB, c)])
                                ybc += 1
                            ypk = ["Yp_%d_%d" % (b, c) for b in range(64 // NB) for c in range(2)]
                            for k0 in range(0, T, KB2):
                                ob = OB[obc % 4]
                                obk = "OB%d" % (obc % 4)
                                obc += 1
                                for kk in range(KB2):
                                    k2 = k0 + kk
                                    for c in range(2):
                                        P.op(PE, lambda e, c=c, k2=k2, kk=kk, ob=ob: e.matmul(
                                            ob[0:64, kk * NO:(kk + 1) * NO], lhsT=Yp[:, c, :, k2], rhs=c2s[:, k2, c, :],
                                            start=(c == 0), stop=(c == 1)), reads=ypk + ["c2s"], writes=[obk])
                                src = ob[0:64, 0:KB2 * NO].rearrange("p (k m) -> p m k", k=KB2)
                                if obc % 2:
                                    P.op(ACT, lambda e, k0=k0, src=src: e.activation(out=f2t[:, :, k0:k0 + KB2], in_=src, func=AF.Copy),
                                         reads=[obk], writes=["f2_%d" % (k0 // KB2)])
                                else:
                                    P.op(DVE, lambda e, k0=k0, src=src: e.tensor_copy(out=f2t[:, :, k0:k0 + KB2], in_=src),
                                         reads=[obk], writes=["f2_%d" % (k0 // KB2)])
                            P.dma(SP, lambda e, g=g, dh=dh: e.dma_start(
                                out=MIX[s][g * 128 + dh * 64:g * 128 + dh * 64 + 64, :], in_=f2t[:].rearrange("p m k -> p (m k)")),
                                reads=["f2_%d" % b for b in range((T + KB2 - 1) // KB2)], writes=["MIX_" + s], slot="st_f2")
                        if g + 1 < 4:
                            fold_group(g + 1)
                    print("F" + s, P.emit())

            with contextlib.ExitStack() as st:
                P = Phase(ctx, "A2")
                W = SB(st, "W_a2", [128, NCH, 1536], BF16)
                Wo = SB(st, "Wo", [128, NCH, D], BF16)
                gt1 = SB(st, "gt1", [128, 2, D], F32)
                kvi = ring(st, "kvi", 8, [128, 1024], BF16)
                xt = ring(st, "xt", 3, [128, D], F32)
                tabt = ring(st, "tabt", 6, [128, 276], F32)
                xn = ring(st, "xn", 2, [128, D], BF16)
                hT = ring(st, "hT", 2, [128, NCH, 128], BF16)
                tmpr = [ring(st, "tmp%d_" % i, 2, [128, 4, 128], F32) for i in range(2)]
                tmpq = [ring(st, "tmq%d_" % i, 2, [128, 4, 128], F32) for i in range(2)]
                kr = ring(st, "kr", 2, [128, 4, 128], F32)
                qr = ring(st, "qr", 2, [128, 4, 128], F32)
                tm = ring(st, "tm", 2, [128, 16, 128], BF16)
                TQ = ring(st, "TQ", 3, [128, 16, 128], BF16)
                sT = ring(st, "sT", 2, [128, 4, 128], BF16)
                Sfb = ring(st, "Sfb", 3, [128, 4, 128], BF16)
                snp = ring(st, "snp", 3, [128, 4, 128], BF16)
                sg = ring(st, "sg", 4, [128, 512], F32)
                t1 = SB(st, "t1", [128, 512], F32)
                rb = ring(st, "rb", 2, [128, 512], BF16)
                rT = ring(st, "rT", 2, [128, 4, 128], BF16)
                fT = ring(st, "fT", 2, [128, 4, 128], BF16)
                ty = SB(st, "ty", [128, D], F32)
                xr = ring(st, "xr", 2, [128, D], F32)
                B = [PS(st, "B%d" % i) for i in range(8)]

                P.dma(POOL, lambda e: e.dma_start(out=W[:, :, 0:1024], in_=w_in[:, 512:1536].rearrange("(k p) n -> p k n", p=128)),
                      writes=["W"], slot="W0")
                P.dma(POOL, lambda e: e.dma_start(out=W[:, :, 1024:1536], in_=w_in[:, 2048:2560].rearrange("(k p) n -> p k n", p=128)),
                      writes=["W"], slot="W1")
                P.dma(POOL, lambda e: e.dma_start(out=Wo[:], in_=w_out.rearrange("(k p) n -> p k n", p=128)), writes=["Wo"], slot="Wo")
                for r in range(2):
                    P.dma(SP, lambda e, r=r: e.dma_start(out=gt1[:, r, :], in_=mod_d[r, 2 * D:3 * D].partition_broadcast(128)),
                          reads=["mod_d"], writes=["gt1"], slot="gt1")
                kvn = ring(st, "kvn", 8, [128, 1024], BF16)
                stages = []
                ocn = 0
                ncn = 0
                for s in ("p", "s"):
                    sq = seqs[s]
                    r = sq["row"]
                    own_pos = {v: m for m, v in enumerate(sq["own"])}
                    items = []
                    pend = []
                    for v in range(sq["Tv"]):
                        if v in own_pos:
                            items.append((pend, v))
                            pend = []
                        else:
                            pend.append(v)
                    if pend:
                        items.append((pend, None))
                    for (others, v) in items:
                        own = v is not None
                        m = own_pos.get(v, -1)
                        oc = ocn
                        if own:
                            ocn += 1
                        i = oc
                        first = (0 in others) or (v == 0)
                        nslots = []
                        for _ in others:
                            nslots.append(ncn % 8)
                            ncn += 1

                        def sl(s=s, v=v, oc=oc):
                            x_t, xk = xt[oc % 3]
                            tb_t, tbk = tabt[oc % 6]
                            P.dma(SP, lambda e: e.dma_start(out=x_t[:], in_=X[s][v * 128:(v + 1) * 128, :]), writes=[xk], slot=xk)
                            P.dma(SP, lambda e: e.dma_start(out=tb_t[:], in_=TAB[s][v * 128:(v + 1) * 128, :]), writes=[tbk], slot=tbk)

                        def s0(oc=oc):
                            x_t, xk = xt[oc % 3]
                            norm_a(P, x_t[:], xk, xn[oc % 2][0][:], xn[oc % 2][1], oc % 4)

                        def s1(oc=oc, r=r):
                            h_t, hk = hT[oc % 2]
                            trans_b(P, xn[oc % 2][0], xn[oc % 2][1], [(B[0], "B0", ACT)], r, 0, lambda c: h_t[:, c, :], lambda c: "%s_%d" % (hk, c))

                        def s2(s=s, v=v, own=own, oc=oc, others=others, nslots=nslots):
                            for ov, ns in zip(others, nslots):
                                kv_t, kvk = kvn[ns]
                                P.dma(SP, lambda e, ov=ov, kv_t=kv_t: e.dma_start(out=kv_t[:], in_=KV[s][ov * 128:(ov + 1) * 128, :]),
                                      reads=["KV_" + s], writes=[kvk], slot=kvk)
                            if not own:
                                return
                            kv_t, kvk = kvi[oc % 8]
                            P.dma(SP, lambda e: e.dma_start(out=kv_t[:], in_=KV[s][v * 128:(v + 1) * 128, :]),
                                  reads=["KV_" + s], writes=[kvk], slot=kvk)
                            h_t, hk = hT[oc % 2]
                            tb_t, tbk = tabt[oc % 6]
                            for (bi, c0) in ((4, 512), (6, 0), (7, 1024)):
                                for k in range(NCH):
                                    P.op(PE, lambda e, bi=bi, c0=c0, k=k: e.matmul(B[bi][:, :], lhsT=h_t[:, k, :], rhs=W[:, k, c0:c0 + 512],
                                                                                  start=(k == 0), stop=(k == NCH - 1)),
                                         reads=["%s_%d" % (hk, c_) for c_ in range(NCH)] + ["W"], writes=["B%d" % bi])
                            kr_t, krk = kr[oc % 2]
                            rotary(P, B[4], "B4", tb_t, tbk, kr_t, krk, tmpr[oc % 2])
                            qr_t, qrk = qr[oc % 2]
                            sg_t, sgk = sg[oc % 4]
                            tm_t, tmk = tm[oc % 2]
                            rotary(P, B[6], "B6", tb_t, tbk, qr_t, qrk, tmpq[oc % 2])
                            P.op(ACT, lambda e: e.activation(out=sg_t[:], in_=B[7][:, :], func=AF.Silu), reads=["B7"], writes=[sgk])
                            P.op(ACT, lambda e: e.activation(out=tm_t[:, 0:4, :], in_=qr_t[:], func=AF.Copy), reads=[qrk], writes=[tmk + "_0"])
                            P.op(POOL, lambda e: e.tensor_tensor(out=tm_t[:, 4:8, :], in0=qr_t[:],
                                                                 in1=tb_t[:, 256:260].unsqueeze(2).broadcast_to([128, 4, 128]), op=ALU.mult),
                                 reads=[qrk, tbk], writes=[tmk + "_1"])
                            P.op(POOL, lambda e: e.tensor_tensor(out=tm_t[:, 8:12, :], in0=qr_t[:],
                                                                 in1=tb_t[:, 260:264].unsqueeze(2).broadcast_to([128, 4, 128]), op=ALU.mult),
                                 reads=[qrk, tbk], writes=[tmk + "_2"])
                            P.op(ACT, lambda e: e.activation(out=tm_t[:, 12:16, :], in_=kr_t[:], func=AF.Copy), reads=[krk], writes=[tmk + "_3"])

                        def s3(s=s, own=own, oc=oc, m=m, first=first, others=others, nslots=nslots):
                            if first:
                                P.op(POOL, lambda e: e.memset(S8[:, 0:4, :], 0.0), writes=["S8f_%d" % h_ for h_ in range(4)])
                            seq_t = [kvn[ns] for ns in nslots] + ([kvi[oc % 8]] if own else [])
                            for ti_, (kv_t, kvk) in enumerate(seq_t):
                                is_own = own and ti_ == len(seq_t) - 1
                                if is_own:
                                    sf_t, sfk = Sfb[oc % 3]
                                    P.op(ACT, lambda e, sf_t=sf_t: e.activation(out=sf_t[:], in_=S8[:, 0:4, :], func=AF.Copy), reads=["S8f_%d" % h_ for h_ in range(4)], writes=[sfk])
                                for h in range(4):
                                    P.op(PE, lambda e, h=h, kv_t=kv_t: e.matmul(B[5][:, h * 128:(h + 1) * 128], lhsT=kv_t[:, h * 128:(h + 1) * 128],
                                                                             rhs=kv_t[:, 512 + h * 128:512 + (h + 1) * 128], start=True, stop=True),
                                         reads=[kvk, "B5"], writes=["B5"])
                                state_update(P, S8, 0, B[5], "B5", "S8f")
                            if own:
                                tm_t, tmk = tm[oc % 2]
                                tq_t, tqk = TQ[oc % 3]
                                sp_t, spk = snp[oc % 3]
                                P.dma(SP, lambda e: e.dma_start(out=sp_t[:].rearrange("p h d -> p (h d)"), in_=SNAP[s][m * 128:(m + 1) * 128, :]),
                                      reads=["SNAP_" + s], writes=[spk], slot=spk)
                                for half in range(2):
                                    bv = B[2 + half][:, :].bitcast(BF16)
                                    for j in range(8):
                                        P.op(PE, lambda e, half=half, j=j, bv=bv: e.transpose(out=bv[:, j * 128:(j + 1) * 128],
                                                                                            in_=tm_t[:, half * 8 + j, :], identity=ident[:]),
                                             reads=["%s_%d" % (tmk, (half * 8 + j) // 4), "ident"], writes=["B%d" % (2 + half)])
                                P.op(ACT, lambda e: e.activation(out=tq_t[:, 0:8, :], in_=B[2][:, :].bitcast(BF16).rearrange("p (a b) -> p a b", a=8),
                                                                 func=AF.Copy), reads=["B2"], writes=[tqk + "_a"])
                                P.op(DVE, lambda e: e.tensor_copy(out=tq_t[:, 8:16, :], in_=B[3][:, :].bitcast(BF16).rearrange("p (a b) -> p a b", a=8)),
                                     reads=["B3"], writes=[tqk + "_b"])

                        def s4(oc=oc):
                            tq_t, tqk = TQ[oc % 3]
                            st_t, stk = sT[oc % 2]
                            for h in range(4):
                                P.op(PE, lambda e, h=h: e.matmul(B[1][:, h * 128:(h + 1) * 128], lhsT=tq_t[:, 12 + h, :], rhs=tq_t[:, h, :],
                                                                 start=True, stop=True), reads=[tqk + "_a", tqk + "_b"], writes=["B1"])
                            P.op(DVE, lambda e: e.tensor_tensor(out=st_t[:], in0=B[1][:, :].rearrange("p (h i) -> p h i", h=4), in1=maskT[:],
                                                                op=ALU.mult), reads=["B1", "maskT"], writes=[stk])

                        def s5(i=i, oc=oc):
                            tq_t, tqk = TQ[oc % 3]
                            st_t, stk = sT[oc % 2]
                            sf_t, sfk = Sfb[oc % 3]
                            sp_t, spk = snp[oc % 3]
                            kv_t, kvk = kvi[i % 8]
                            sg_t, sgk = sg[oc % 4]
                            rb_t, rbk = rb[oc % 2]
                            for h in range(4):
                                o = B[6][:, h * 128:(h + 1) * 128]
                                P.op(PE, lambda e, h=h, o=o: e.matmul(o, lhsT=st_t[:, h, :], rhs=kv_t[:, 512 + h * 128:512 + (h + 1) * 128],
                                                                      start=True, stop=False), reads=[stk, kvk], writes=["B6"])
                                P.op(PE, lambda e, h=h, o=o: e.matmul(o, lhsT=tq_t[:, 4 + h, :], rhs=sf_t[:, h, :], start=False, stop=False),
                                     reads=[tqk + "_a", sfk], writes=["B6"])
                                P.op(PE, lambda e, h=h, o=o: e.matmul(o, lhsT=tq_t[:, 8 + h, :], rhs=sp_t[:, h, :], start=False, stop=True),
                                     reads=[tqk + "_b", spk], writes=["B6"])
                            for h in range(4):
                                P.op(ACT, lambda e, h=h: e.activation(out=junk[:, 0:128], in_=B[6][:, h * 128:(h + 1) * 128], func=AF.Square,
                                                                      accum_out=stat[:, 0, 4 + h:5 + h]), reads=["B6"], writes=["ssg%d" % h])
                            P.op(DVE, lambda e: e.tensor_scalar(out=stat[:, 1, 4:8], in0=stat[:, 0, 4:8], scalar1=1.0 / 128, scalar2=EPS,
                                                                op0=ALU.mult, op1=ALU.add), reads=["ssg0", "ssg1", "ssg2", "ssg3"], writes=["msg"])
                            P.op(POOL, lambda e: e.tensor_tensor(out=stat[:, 2, 4:8], in0=stat[:, 1, 4:8], in1=mh[:, 0:1].broadcast_to([128, 4]),
                                                                 op=ALU.pow), reads=["msg", "mh"], writes=["rsg"])
                            P.op(DVE, lambda e: e.tensor_tensor(out=t1[:].rearrange("p (h d) -> p h d", h=4),
                                                                in0=B[6][:, :].rearrange("p (h d) -> p h d", h=4),
                                                                in1=stat[:, 2, 4:8].unsqueeze(2).broadcast_to([128, 4, 128]), op=ALU.mult),
                                 reads=["B6", "rsg"], writes=["t1"])
                            P.op(DVE, lambda e: e.tensor_tensor(out=rb_t[:], in0=t1[:], in1=sg_t[:], op=ALU.mult), reads=["t1", sgk], writes=[rbk])

                        def s6(s=s, v=v, oc=oc, m=m):
                            rb_t, rbk = rb[oc % 2]
                            rT_t, rTk = rT[oc % 2]
                            f_t, fk = fT[oc % 2]
                            xr_t, xrk = xr[oc % 2]
                            b1v = B[1][:, :].bitcast(BF16)
                            for h in range(4):
                                P.op(PE, lambda e, h=h: e.transpose(out=b1v[:, h * 128:(h + 1) * 128], in_=rb_t[:, h * 128:(h + 1) * 128],
                                                                    identity=ident[:]), reads=[rbk, "ident"], writes=["B1"])
                            P.op(ACT, lambda e: e.activation(out=rT_t[:], in_=b1v[:, 0:512].rearrange("p (h d) -> p h d", h=4), func=AF.Copy),
                                 reads=["B1"], writes=[rTk])
                            P.dma(SP, lambda e: e.dma_start(out=f_t[:], in_=MIX[s][:, m * 128:(m + 1) * 128].rearrange("(g d) t -> d g t", d=128)),
                                  reads=["MIX_" + s], writes=[fk], slot=fk)
                            P.dma(SP, lambda e: e.dma_start(out=xr_t[:], in_=X[s][v * 128:(v + 1) * 128, :]), writes=[xrk], slot=xrk)

                        def s7(s=s, r=r, oc=oc, m=m):
                            rT_t, rTk = rT[oc % 2]
                            f_t, fk = fT[oc % 2]
                            xr_t, xrk = xr[oc % 2]
                            for half in range(2):
                                for kc in range(NCH):
                                    lhs = f_t[:, kc, :] if kc < 4 else rT_t[:, kc - 4, :]
                                    P.op(PE, lambda e, half=half, kc=kc, lhs=lhs: e.matmul(B[2 + half][:, :], lhsT=lhs,
                                                                                         rhs=Wo[:, kc, half * 512:(half + 1) * 512],
                                                                                         start=(kc == 0), stop=(kc == NCH - 1)),
                                         reads=[fk, rTk, "Wo"], writes=["B%d" % (2 + half)])
                            for half in range(2):
                                P.op(DVE, lambda e, half=half: e.tensor_tensor(out=ty[:, half * 512:(half + 1) * 512], in0=B[2 + half][:, :],
                                                                              in1=gt1[:, r, half * 512:(half + 1) * 512], op=ALU.mult),
                                     reads=["B%d" % (2 + half), "gt1"], writes=["ty_%d" % half])
                            P.op(DVE, lambda e: e.tensor_tensor(out=xr_t[:], in0=ty[:], in1=xr_t[:], op=ALU.add), reads=["ty_0", "ty_1", xrk], writes=[xrk])
                            P.dma(SP, lambda e: e.dma_start(out=X1[s][m * 128:(m + 1) * 128, :], in_=xr_t[:]),
                                  reads=[xrk], writes=["X1_" + s], slot="st_" + xrk)

                        stages.append([sl, s0, s1, s2, s3, s4, s5, s6, s7] if own else [None, None, None, s2, s3])
                run_pipeline(stages, 9, order=[0, 1, 2, 3, 4, 5, 6, 7, 8])
                print("A2", P.emit())

        with contextlib.ExitStack() as st:
            P = Phase(ctx, "B")
            W1 = SB(st, "W1", [128, NCH, DFF], BF16)
            W2 = SB(st, "W2", [128, 32, D], BF16)
            modfm_b = modfm
            stat_b = SB(st, "stat", [128, 3, 8], F32)
            junk_b = SB(st, "junk", [128, D], BF16)
            gt2 = SB(st, "gt2", [128, D], F32)
            nfb = SB(st, "nfb", [128, D], F32)
            x1a = ring(st, "x1a", 2, [128, D], F32)
            x1c = ring(st, "x1c", 2, [128, D], F32)
            xnb = ring(st, "xnb", 4, [128, D], BF16)
            h2T = ring(st, "h2T", 2, [128, NCH, 256], BF16)
            aT = SB(st, "aT", [128, 32, 256], BF16)
            rl = ring(st, "rl", 2, [128, 256], F32)
            ty = SB(st, "ty", [128, D], F32)
            yo = ring(st, "yo", 2, [128, D], F32)
            B = [PS(st, "B%d" % i) for i in range(8)]

            for q4 in range(4):
                P.dma(POOL, lambda e, q4=q4: e.dma_start(out=W1[:, :, q4 * 1024:(q4 + 1) * 1024],
                                                        in_=w_mlp_in[:, q4 * 1024:(q4 + 1) * 1024].rearrange("(k p) n -> p k n", p=128)),
                      writes=["W1_%d" % q4], slot="W1_%d" % q4)
            for q4 in range(4):
                P.dma(POOL, lambda e, q4=q4: e.dma_start(out=W2[:, q4 * 8:(q4 + 1) * 8, :],
                                                        in_=w_mlp_out[q4 * 1024:(q4 + 1) * 1024, :].rearrange("(k p) n -> p k n", p=128)),
                      writes=["W2_%d" % q4], slot="W2_%d" % q4)
            P.dma(SP, lambda e: e.dma_start(out=nfb[:], in_=nf_bc[:, :]), writes=["nfb"], slot="nfb")
            bcn = 0
            ycn = 0
            for s in ("p", "s"):
                sq = seqs[s]
                r = sq["row"]
                ntile = len(sq["own"])
                P.dma(SP, lambda e, r=r: e.dma_start(out=gt2[:], in_=mod_d[r, 5 * D:6 * D].partition_broadcast(128)),
                      reads=["mod_d"], writes=["gt2"], slot="gt2")
                stages = []
                for blk in range(0, ntile, 2):
                    tiles = list(range(blk, min(blk + 2, ntile)))
                    bi_ = bcn
                    bcn += 1

                    def s0(tiles=tiles, s=s, bi_=bi_):
                        for ti, m in enumerate(tiles):
                            x_t, xk = x1a[ti]
                            xb_ = xnb[2 * (bi_ % 2) + ti]
                            P.dma(SP, lambda e, x_t=x_t, m=m: e.dma_start(out=x_t[:], in_=X1[s][m * 128:(m + 1) * 128, :]),
                                  reads=["X1_" + s], writes=[xk], slot=xk)
                            norm_a(P, x_t[:], xk, xb_[0][:], xb_[1], ti, stat=stat_b, junk=junk_b)

                    def s1(tiles=tiles, bi_=bi_, r=r):
                        h_t, hk = h2T[bi_ % 2]
                        for ti, m in enumerate(tiles):
                            xb_ = xnb[2 * (bi_ % 2) + ti]
                            trans_b(P, xb_[0], xb_[1], [(B[ti], "B%d" % ti, ACT if ti == 0 else DVE)], r, 2,
                                    lambda c, ti=ti: h_t[:, c, ti * 128:(ti + 1) * 128], lambda c, ti=ti: "%s_%d_%d" % (hk, c, ti), modfm=modfm_b)

                    def s2(tiles=tiles, bi_=bi_):
                        h_t, hk = h2T[bi_ % 2]
                        NW = len(tiles) * 128
                        for f in range(32):
                            bi = 2 + (f % 4)
                            bk = "B%d" % bi
                            rl_t, rlk = rl[f % 2]
                            for k in range(NCH):
                                P.op(PE, lambda e, f=f, k=k, bi=bi: e.matmul(B[bi][:, 0:NW], lhsT=W1[:, k, f * 128:(f + 1) * 128],
                                                                            rhs=h_t[:, k, 0:NW], start=(k == 0), stop=(k == NCH - 1)),
                                     reads=["W1_%d" % (f // 8)] + ["%s_%d_%d" % (hk, c_, ti) for ti in range(len(tiles)) for c_ in range(NCH)], writes=[bk])
                            P.op(ACT, lambda e, bi=bi, rl_t=rl_t: e.activation(out=rl_t[:, 0:NW], in_=B[bi][:, 0:NW], func=AF.Relu),
                                 reads=[bk], writes=[rlk])
                            P.op(DVE, lambda e, f=f, bi=bi, rl_t=rl_t: e.tensor_tensor(out=aT[:, f, 0:NW], in0=B[bi][:, 0:NW], in1=rl_t[:, 0:NW],
                                                                                     op=ALU.mult), reads=[bk, rlk], writes=["aT_%d" % f])

                    def s3(tiles=tiles, s=s):
                        nonlocal ycn
                        for ti, m in enumerate(tiles):
                            x_t, xk = x1c[ti]
                            P.dma(SP, lambda e, x_t=x_t, m=m: e.dma_start(out=x_t[:], in_=X1[s][m * 128:(m + 1) * 128, :]),
                                  reads=["X1_" + s], writes=[xk], slot=xk)
                            for half in range(2):
                                bi = 6 + half
                                for f in range(32):
                                    P.op(PE, lambda e, f=f, ti=ti, half=half, bi=bi: e.matmul(B[bi][:, :], lhsT=aT[:, f, ti * 128:(ti + 1) * 128],
                                                                                             rhs=W2[:, f, half * 512:(half + 1) * 512],
                                                                                             start=(f == 0), stop=(f == 31)),
                                         reads=["aT_%d" % f, "W2_%d" % (f // 8)], writes=["B%d" % bi])
                                P.op(DVE, lambda e, half=half, bi=bi: e.tensor_tensor(out=ty[:, half * 512:(half + 1) * 512], in0=B[bi][:, :],
                                                                                     in1=gt2[:, half * 512:(half + 1) * 512], op=ALU.mult),
                                     reads=["B%d" % bi, "gt2"], writes=["ty_%d" % half])
                            P.op(DVE, lambda e, x_t=x_t: e.tensor_tensor(out=ty[:], in0=ty[:], in1=x_t[:], op=ALU.add),
                                 reads=["ty_0", "ty_1", xk], writes=["ty_0", "ty_1"])
                            y_t, yk = yo[ycn % 2]
                            ycn += 1
                            col = 2 + ti
                            ssc, msc, rsc = stat_b[:, 0, col:col + 1], stat_b[:, 1, col:col + 1], stat_b[:, 2, col:col + 1]
                            P.op(ACT, lambda e, ssc=ssc: e.activation(out=junk_b[:], in_=ty[:], func=AF.Square, accum_out=ssc),
                                 reads=["ty_0", "ty_1"], writes=["ss%d" % col])
                            P.op(DVE, lambda e, ssc=ssc, msc=msc: e.tensor_scalar(out=msc, in0=ssc, scalar1=1.0 / D, scalar2=EPS,
                                                                                 op0=ALU.mult, op1=ALU.add),
                                 reads=["ss%d" % col], writes=["ms%d" % col])
                            P.op(POOL, lambda e, msc=msc, rsc=rsc: e.tensor_tensor(out=rsc, in0=msc, in1=mh[:, 0:1], op=ALU.pow),
                                 reads=["ms%d" % col, "mh"], writes=["rs%d" % col])
                            P.op(DVE, lambda e, y_t=y_t, rsc=rsc: e.scalar_tensor_tensor(out=y_t[:], in0=ty[:], scalar=rsc, in1=nfb[:],
                                                                                        op0=ALU.mult, op1=ALU.mult),
                                 reads=["ty_0", "ty_1", "rs%d" % col, "nfb"], writes=[yk])
                            P.dma(SP, lambda e, y_t=y_t, m=m: e.dma_start(out=Y[s][m * 128:(m + 1) * 128, :], in_=y_t[:]),
                                  reads=[yk], writes=["Y_" + s], slot="st_" + yk)

                    stages.append([s0, s1, s2, s3])
                run_pipeline(stages, 4, order=[0, 1, 3, 2])
            print("B", P.emit())
    return nc


def make_in_maps(inputs, Tp, Ts):
    f = lambda a: np.ascontiguousarray(np.asarray(a, dtype=np.float32))
    xp, xs = f(inputs["x_prompt"]), f(inputs["x_sample"])
    cp, cs = f(inputs["c_prompt"]), f(inputs["c_sample"])
    NOWN = Tp // 4
    Tvp = Tp + 3
    maskT, g8 = _mask_tables()
    idx = np.arange(128)
    cs128 = np.concatenate([np.cos(2 * np.pi * np.outer(idx, idx) / 128), np.sin(2 * np.pi * np.outer(idx, idx) / 128)], axis=1)
    cs128 = (cs128 / np.sqrt(128.0)).astype(np.float32)
    nm_fm = np.stack([f(inputs["norm_mix"])[0].reshape(NCH, 128).T, f(inputs["norm_mlp"])[0].reshape(NCH, 128).T], axis=1)
    nf_bc = np.ascontiguousarray(np.broadcast_to(f(inputs["norm_final"])[None, :], (128, D)))
    ada_b2 = np.ascontiguousarray(np.broadcast_to(f(inputs["ada_b"])[0][None, :], (2, 6 * D)))
    tab_s = _tile_tables([t * 128 for t in range(Ts)], [1.0] * Ts)
    f1_s, c2_s = _fft_tables(Ts, list(range(Ts)), list(range(128)))
    shared = dict(
        ada_w=f(inputs["ada_w"])[0], ada_b2=ada_b2, nm_fm=np.ascontiguousarray(nm_fm), nf_bc=nf_bc,
        w_in=f(inputs["w_in"])[0], w_fnet=f(inputs["w_fnet"])[0], w_out=f(inputs["w_out"])[0],
        w_mlp_in=f(inputs["w_mlp_in"])[0], w_mlp_out=f(inputs["w_mlp_out"])[0],
        ident=np.eye(128, dtype=np.float32), cs128=cs128, maskT=maskT, g8=g8,
        tab_s=tab_s, f1_s=f1_s, c2_s=c2_s,
    )
    per_j = {}
    for j in range(4):
        pad = 3 - j
        vmap = [-1] * pad + list(range(Tp)) + [-1] * j
        tab_p = _tile_tables([max(t, 0) * 128 for t in vmap], [1.0 if t >= 0 else 0.0 for t in vmap])
        kpt = 128 // Tp
        f1_p, c2_p = _fft_tables(Tp, vmap, [(4 * m + j) * kpt + i for m in range(NOWN) for i in range(kpt)])
        per_j[j] = dict(tab_p=tab_p, f1_p=f1_p, c2_p=c2_p)
    xv_cache = {}
    in_maps = []
    for core in range(8):
        b, j = core // 4, core % 4
        if (b, j) not in xv_cache:
            xv = np.zeros((Tvp * 128, D), np.float32)
            xv[(3 - j) * 128:(3 - j) * 128 + Tp * 128] = xp[b]
            xv_cache[(b, j)] = xv
        c2T = np.stack([cp[b].reshape(NCH, 128).T, cs[core].reshape(NCH, 128).T], axis=2)
        m = dict(shared)
        m.update(per_j[j])
        m.update(xv=xv_cache[(b, j)], xs=xs[core], c2T=np.ascontiguousarray(c2T))
        in_maps.append(m)
    return in_maps


_NC_CACHE = {}


def run(inputs, Tp, Ts):
    key = (Tp, Ts)
    if key not in _NC_CACHE:
        _NC_CACHE[key] = build(Tp, Ts)
    nc = _NC_CACHE[key]
    in_maps = make_in_maps(inputs, Tp, Ts)
    res = run_bass_kernel_spmd(nc, in_maps, core_ids=list(range(8)))
    NOWN = Tp // 4
    B = inputs["x_prompt"].shape[0]
    yp = np.zeros((B, Tp * 128, D), np.float32)
    ys = np.zeros((8, Ts * 128, D), np.float32)
    for core in range(8):
        b, j = core // 4, core % 4
        o = np.asarray(res.results[core]["yp"]).reshape(NOWN, 128, D)
        yp[b].reshape(Tp, 128, D)[j::4] = o
        ys[core] = np.asarray(res.results[core]["ys"])
    return yp, ys


def kernel(x_prompt, x_sample, c_prompt, c_sample, ada_w, ada_b, norm_mix, w_in, w_fnet, w_out,
           norm_mlp, w_mlp_in, w_mlp_out, norm_final):
    inputs = dict(x_prompt=x_prompt, x_sample=x_sample, c_prompt=c_prompt, c_sample=c_sample, ada_w=ada_w,
                  ada_b=ada_b, norm_mix=norm_mix, w_in=w_in, w_fnet=w_fnet, w_out=w_out, norm_mlp=norm_mlp,
                  w_mlp_in=w_mlp_in, w_mlp_out=w_mlp_out, norm_final=norm_final)
    Tp = np.asarray(x_prompt).shape[1] // 128
    Ts = np.asarray(x_sample).shape[1] // 128
    yp, ys = run(inputs, Tp, Ts)
    return (yp, ys)
```
